# Optimizing a Trainium2 kernel written in Bass

```python
import math
import jax, jax.numpy as jnp
from jax import lax
import numpy as np

D_MODEL = 1024
BATCH = 8
SEQ = 4096
DEPTH = 4

GRID_W = 64
Q_BLOCK = 128
N_MIXERS = 3
A_HEADS = 16
A_KV_HEADS = 4
A_HEAD_DIM = D_MODEL // A_HEADS
AXIAL_THETA = 10000.0
FNET_GROUPS = 4
C_HEADS = 8
C_HEAD_DIM = D_MODEL // C_HEADS // 2
PARTIAL_ROPE_DIM = C_HEAD_DIM // 4
ROPE_THETA = 500000.0
D_FF = 2816
CONV_WIDTH = 3
DEEPNORM_ALPHA = (2.0 * DEPTH) ** 0.25
DEEPNORM_BETA = (8.0 * DEPTH) ** -0.25
LN_EPS = 1e-5
RMS_EPS = 1e-6

kernel_name = 'hybrid_interleaved_gqa_fnet_diffattn_encoder'


def _layernorm(x, g, b):
    x32 = x.astype(jnp.float32)
    mu = jnp.mean(x32, axis=-1, keepdims=True)
    var = jnp.mean(jnp.square(x32 - mu), axis=-1, keepdims=True)
    y = (x32 - mu) * lax.rsqrt(var + LN_EPS) * g.astype(jnp.float32) + b.astype(jnp.float32)
    return y.astype(x.dtype)


def _rmsnorm(x, g):
    x32 = x.astype(jnp.float32)
    y = x32 * lax.rsqrt(jnp.mean(jnp.square(x32), axis=-1, keepdims=True) + RMS_EPS) * g.astype(jnp.float32)
    return y.astype(x.dtype)


def _rope_cos_sin(pos, dim, theta):
    inv_freq = theta ** (-jnp.arange(0, dim, 2, dtype=jnp.float32) / dim)
    ang = pos.astype(jnp.float32)[:, None] * inv_freq[None, :]
    return jnp.cos(ang), jnp.sin(ang)


def _rotate_half(x, cos, sin):
    half = x.shape[-1] // 2
    shape = (cos.shape[0],) + (1,) * (x.ndim - 3) + (cos.shape[-1],)
    c = cos.reshape(shape).astype(x.dtype)
    s = sin.reshape(shape).astype(x.dtype)
    x1, x2 = x[..., :half], x[..., half:]
    return jnp.concatenate([x1 * c - x2 * s, x2 * c + x1 * s], axis=-1)


def _axial_rope(x, cos_r, sin_r, cos_c, sin_c):
    half = x.shape[-1] // 2
    return jnp.concatenate([_rotate_half(x[..., :half], cos_r, sin_r),
                            _rotate_half(x[..., half:], cos_c, sin_c)], axis=-1)


def _partial_rope(x, cos, sin):
    return jnp.concatenate([_rotate_half(x[..., :PARTIAL_ROPE_DIM], cos, sin),
                            x[..., PARTIAL_ROPE_DIM:]], axis=-1)


def _to_blocks(q):
    b, s = q.shape[:2]
    return jnp.moveaxis(q.reshape((b, s // Q_BLOCK, Q_BLOCK) + q.shape[2:]), 1, 0)


def _from_blocks(o):
    o = jnp.moveaxis(o, 0, 1)
    return o.reshape((o.shape[0], o.shape[1] * o.shape[2]) + o.shape[3:])


def _lambda_init(layer_idx):
    return 0.8 - 0.6 * math.exp(-0.3 * layer_idx)


def _mixer_gqa_axial(x, w_qkv, q_norm, k_norm, w_o, axial):
    b, s, _ = x.shape
    nq = A_HEADS * A_HEAD_DIM
    nkv = A_KV_HEADS * A_HEAD_DIM
    qkv = x @ w_qkv
    q = qkv[..., :nq].reshape(b, s, A_KV_HEADS, A_HEADS // A_KV_HEADS, A_HEAD_DIM)
    k = qkv[..., nq:nq + nkv].reshape(b, s, A_KV_HEADS, A_HEAD_DIM)
    v = qkv[..., nq + nkv:].reshape(b, s, A_KV_HEADS, A_HEAD_DIM)
    q = _axial_rope(_rmsnorm(q, q_norm), *axial)
    k = _axial_rope(_rmsnorm(k, k_norm), *axial)
    scale = A_HEAD_DIM ** -0.5

    def attend(qb):
        sc = jnp.einsum('bqkgd,bskd->bkgqs', qb, k).astype(jnp.float32) * scale
        p = jax.nn.softmax(sc, axis=-1).astype(v.dtype)
        return jnp.einsum('bkgqs,bskd->bqkgd', p, v)

    o = _from_blocks(lax.map(attend, _to_blocks(q)))
    return o.reshape(b, s, nq) @ w_o


def _mixer_fourier(x, w_o, b_o):
    b, s, d = x.shape
    u = x.astype(jnp.float32).reshape(b, s, FNET_GROUPS, d // FNET_GROUPS)
    f = jnp.fft.fft2(u, axes=(1, 3), norm='ortho').real
    return f.reshape(b, s, d).astype(x.dtype) @ w_o + b_o


def _mixer_diff(x, w_qkv, lq1, lk1, lq2, lk2, subln, w_o, rope, lambda_init):
    b, s, d = x.shape
    q, k, v = jnp.split(x @ w_qkv, 3, axis=-1)
    q = _partial_rope(q.reshape(b, s, C_HEADS, 2, C_HEAD_DIM), *rope)
    k = _partial_rope(k.reshape(b, s, C_HEADS, 2, C_HEAD_DIM), *rope)
    v = v.reshape(b, s, C_HEADS, 2 * C_HEAD_DIM)
    f32 = jnp.float32
    lam = (jnp.exp(jnp.sum(lq1.astype(f32) * lk1.astype(f32)))
           - jnp.exp(jnp.sum(lq2.astype(f32) * lk2.astype(f32))) + lambda_init)
    scale = C_HEAD_DIM ** -0.5

    def attend(qb):
        sc = jnp.einsum('bqhcd,bshcd->bhcqs', qb, k).astype(f32) * scale
        p = jax.nn.softmax(sc, axis=-1)
        a = (p[:, :, 0] - lam * p[:, :, 1]).astype(v.dtype)
        return jnp.einsum('bhqs,bshe->bqhe', a, v)

    o = _from_blocks(lax.map(attend, _to_blocks(q)))
    o = _rmsnorm(o, subln) * (1.0 - lambda_init)
    return o.reshape(b, s, d) @ w_o


def _conv_ffn(x, w_up, conv_w, conv_b, w_down):
    s = x.shape[1]
    h = x @ w_up
    pad = CONV_WIDTH // 2
    hp = jnp.pad(h, ((0, 0), (pad, pad), (0, 0)))
    h = conv_b + sum(hp[:, j:j + s] * conv_w[j] for j in range(CONV_WIDTH))
    gate, val = jnp.split(h, 2, axis=-1)
    return (jax.nn.silu(gate) * val) @ w_down


def _post_norm(x, sub, g, b):
    return _layernorm(DEEPNORM_ALPHA * x + sub, g, b)


def _dense(key, fan_in, fan_out, scale=1.0):
    return jax.random.normal(key, (fan_in, fan_out), jnp.float32) * (scale * fan_in ** -0.5)


def _gain(key, n):
    return 1.0 + 0.01 * jax.random.normal(key, (n,), jnp.float32)


def _bias(key, n):
    return 0.01 * jax.random.normal(key, (n,), jnp.float32)


def setup_inputs(seed: int = 0) -> dict:
    key = jax.random.key(seed)
    key, kx = jax.random.split(key)
    inputs = {'x': jax.random.normal(kx, (BATCH, SEQ, D_MODEL), jnp.float32)}
    qkv_a = (A_HEADS + 2 * A_KV_HEADS) * A_HEAD_DIM
    for i in range(DEPTH):
        ks = jax.random.split(jax.random.fold_in(key, i), 20)
        p = 'l%d_' % i
        kind = i % N_MIXERS
        if kind == 0:
            inputs[p + 'a_wqkv'] = _dense(ks[0], D_MODEL, qkv_a)
            inputs[p + 'a_qnorm'] = _gain(ks[1], A_HEAD_DIM)
            inputs[p + 'a_knorm'] = _gain(ks[2], A_HEAD_DIM)
            inputs[p + 'a_wo'] = _dense(ks[3], A_HEADS * A_HEAD_DIM, D_MODEL, DEEPNORM_BETA)
        elif kind == 1:
            inputs[p + 'f_wo'] = _dense(ks[0], D_MODEL, D_MODEL, DEEPNORM_BETA)
            inputs[p + 'f_bo'] = _bias(ks[1], D_MODEL)
        else:
            inputs[p + 'c_wqkv'] = _dense(ks[0], D_MODEL, 3 * D_MODEL)
            inputs[p + 'c_lq1'] = 0.1 * jax.random.normal(ks[1], (C_HEAD_DIM,), jnp.float32)
            inputs[p + 'c_lk1'] = 0.1 * jax.random.normal(ks[2], (C_HEAD_DIM,), jnp.float32)
            inputs[p + 'c_lq2'] = 0.1 * jax.random.normal(ks[3], (C_HEAD_DIM,), jnp.float32)
            inputs[p + 'c_lk2'] = 0.1 * jax.random.normal(ks[4], (C_HEAD_DIM,), jnp.float32)
            inputs[p + 'c_subln'] = _gain(ks[5], 2 * C_HEAD_DIM)
            inputs[p + 'c_wo'] = _dense(ks[6], D_MODEL, D_MODEL, DEEPNORM_BETA)
        inputs[p + 'ln1_g'] = _gain(ks[10], D_MODEL)
        inputs[p + 'ln1_b'] = _bias(ks[11], D_MODEL)
        inputs[p + 'ffn_wup'] = _dense(ks[12], D_MODEL, 2 * D_FF)
        inputs[p + 'ffn_conv_w'] = jax.random.normal(ks[13], (CONV_WIDTH, 2 * D_FF), jnp.float32) * CONV_WIDTH ** -0.5
        inputs[p + 'ffn_conv_b'] = _bias(ks[14], 2 * D_FF)
        inputs[p + 'ffn_wdown'] = _dense(ks[15], D_FF, D_MODEL, DEEPNORM_BETA)
        inputs[p + 'ln2_g'] = _gain(ks[16], D_MODEL)
        inputs[p + 'ln2_b'] = _bias(ks[17], D_MODEL)
    return inputs


def reference(x,
              l0_a_wqkv, l0_a_qnorm, l0_a_knorm, l0_a_wo,
              l0_ln1_g, l0_ln1_b, l0_ffn_wup, l0_ffn_conv_w, l0_ffn_conv_b, l0_ffn_wdown, l0_ln2_g, l0_ln2_b,
              l1_f_wo, l1_f_bo,
              l1_ln1_g, l1_ln1_b, l1_ffn_wup, l1_ffn_conv_w, l1_ffn_conv_b, l1_ffn_wdown, l1_ln2_g, l1_ln2_b,
              l2_c_wqkv, l2_c_lq1, l2_c_lk1, l2_c_lq2, l2_c_lk2, l2_c_subln, l2_c_wo,
              l2_ln1_g, l2_ln1_b, l2_ffn_wup, l2_ffn_conv_w, l2_ffn_conv_b, l2_ffn_wdown, l2_ln2_g, l2_ln2_b,
              l3_a_wqkv, l3_a_qnorm, l3_a_knorm, l3_a_wo,
              l3_ln1_g, l3_ln1_b, l3_ffn_wup, l3_ffn_conv_w, l3_ffn_conv_b, l3_ffn_wdown, l3_ln2_g, l3_ln2_b):
    s = x.shape[1]
    rows = s // GRID_W
    t_row = jnp.repeat(jnp.arange(rows, dtype=jnp.int32), GRID_W)
    t_col = jnp.tile(jnp.arange(GRID_W, dtype=jnp.int32), rows)
    axial = (_rope_cos_sin(t_row, A_HEAD_DIM // 2, AXIAL_THETA)
             + _rope_cos_sin(t_col, A_HEAD_DIM // 2, AXIAL_THETA))
    rope_c = _rope_cos_sin(jnp.arange(s, dtype=jnp.int32), PARTIAL_ROPE_DIM, ROPE_THETA)

    mixers = ((l0_a_wqkv, l0_a_qnorm, l0_a_knorm, l0_a_wo),
              (l1_f_wo, l1_f_bo),
              (l2_c_wqkv, l2_c_lq1, l2_c_lk1, l2_c_lq2, l2_c_lk2, l2_c_subln, l2_c_wo),
              (l3_a_wqkv, l3_a_qnorm, l3_a_knorm, l3_a_wo))
    norms1 = ((l0_ln1_g, l0_ln1_b), (l1_ln1_g, l1_ln1_b), (l2_ln1_g, l2_ln1_b), (l3_ln1_g, l3_ln1_b))
    ffns = ((l0_ffn_wup, l0_ffn_conv_w, l0_ffn_conv_b, l0_ffn_wdown),
            (l1_ffn_wup, l1_ffn_conv_w, l1_ffn_conv_b, l1_ffn_wdown),
            (l2_ffn_wup, l2_ffn_conv_w, l2_ffn_conv_b, l2_ffn_wdown),
            (l3_ffn_wup, l3_ffn_conv_w, l3_ffn_conv_b, l3_ffn_wdown))
    norms2 = ((l0_ln2_g, l0_ln2_b), (l1_ln2_g, l1_ln2_b), (l2_ln2_g, l2_ln2_b), (l3_ln2_g, l3_ln2_b))

    for i in range(DEPTH):
        kind = i % N_MIXERS
        if kind == 0:
            sub = _mixer_gqa_axial(x, *mixers[i], axial)
        elif kind == 1:
            sub = _mixer_fourier(x, *mixers[i])
        else:
            sub = _mixer_diff(x, *mixers[i], rope_c, _lambda_init(i))
        x = _post_norm(x, sub, *norms1[i])
        x = _post_norm(x, _conv_ffn(x, *ffns[i]), *norms2[i])
    return x
```

```python
import math
from contextlib import ExitStack
import numpy as np
import ml_dtypes
import concourse.bass as bass
import concourse.mybir as mybir
from concourse.bass_utils import run_bass_kernel_spmd

F32 = mybir.dt.float32
BF16 = mybir.dt.bfloat16
AF = mybir.ActivationFunctionType
ALU = mybir.AluOpType
AX = mybir.AxisListType

D = 1024
DFF = 2816
NJ = DFF // 128
DEPTH = 4
ALPHA = (2.0 * DEPTH) ** 0.25
LN_EPS = 1e-5
RMS_EPS = 1e-6
ENGINES = ("pe", "act", "dve", "pool", "sp")


class LT:
    __slots__ = ("name", "last_w", "readers", "dsem")

    def __init__(self, name=""):
        self.name = name
        self.last_w = None
        self.readers = []
        self.dsem = None


def lts(n, name=""):
    return [LT(name + str(i)) for i in range(n)]


class DmaSem:
    __slots__ = ("sem", "count")

    def __init__(self, sem):
        self.sem = sem
        self.count = 0


class Op:
    __slots__ = ("eng", "fn", "pos", "needs_inc", "tick", "is_dma", "dsem", "dtarget",
                 "waits_eng", "waits_dma", "clock")

    def __init__(self, eng, fn):
        self.eng = eng
        self.fn = fn
        self.pos = -1
        self.needs_inc = False
        self.tick = 0
        self.is_dma = False
        self.dsem = None
        self.dtarget = 0
        self.waits_eng = {}
        self.waits_dma = {}
        self.clock = None


class Prog:
    def __init__(self, nc):
        self.nc = nc
        self.ops = {e: [] for e in ENGINES}
        self.known = {e: {x: -1 for x in ENGINES} for e in ENGINES}
        self.known_dma = {e: {} for e in ENGINES}
        self.free_dsems = []
        self.all_dsems = []
        self.esem = {}
        self.phase_tiles = []
        self.swq = []

    def setup_sems(self, stack, n_dma):
        for e in ("pe", "act", "dve", "pool"):
            self.esem[e] = stack.enter_context(self.nc.semaphore("es_" + e))
        self.free_dsems = {"pool": [], "sp": []}
        for i in range(n_dma):
            d = DmaSem(stack.enter_context(self.nc.semaphore("ds_%d" % i)))
            self.free_dsems["pool" if i < 30 else "sp"].append(d)
            self.all_dsems.append(d)
        self.dsem_owner = {}

    def _get_dsem(self, tile, eng):
        if tile.dsem is None:
            tile.dsem = self.free_dsems[eng].pop()
            self.dsem_owner[tile.dsem] = eng
            self.phase_tiles.append(tile)
        assert self.dsem_owner[tile.dsem] == eng
        return tile.dsem

    def _add_dep(self, op, dep):
        if dep is None or dep is op:
            return
        e = op.eng
        if dep.is_dma:
            if self.known_dma[e].get(dep.dsem, 0) >= dep.dtarget:
                return
            if op.waits_dma.get(dep.dsem, 0) < dep.dtarget:
                op.waits_dma[dep.dsem] = dep.dtarget
            return
        x = dep.eng
        if x == "pe" and e == "pe":
            return
        if self.known[e][x] >= dep.pos:
            return
        cur = op.waits_eng.get(x)
        if cur is None or cur.pos < dep.pos:
            op.waits_eng[x] = dep

    def add(self, eng, fn, reads=(), writes=(), dma_tile=None):
        op = Op(eng, fn)
        for t in reads:
            self._add_dep(op, t.last_w)
        for t in writes:
            self._add_dep(op, t.last_w)
            for r in t.readers:
                self._add_dep(op, r)
        kn = self.known[eng]
        for x, dep in op.waits_eng.items():
            dep.needs_inc = True
            if kn[x] < dep.pos:
                kn[x] = dep.pos
            if dep.clock is not None:
                for y, p in dep.clock.items():
                    if kn[y] < p:
                        kn[y] = p
        for ds, tgt in op.waits_dma.items():
            self.known_dma[eng][ds] = tgt
        op.pos = len(self.ops[eng])
        self.ops[eng].append(op)
        if dma_tile is not None:
            op.is_dma = True
            op.dsem = self._get_dsem(dma_tile, eng)
            op.dsem.count += 16
            op.dtarget = op.dsem.count
        else:
            op.clock = dict(kn)
            op.clock[eng] = op.pos
        for t in reads:
            t.readers.append(op)
        for t in writes:
            t.last_w = op
            t.readers = []
        return op

    def barrier(self):
        lasts = {}
        for e in ("pe", "act", "dve", "pool"):
            for op in reversed(self.ops[e]):
                if op.fn is not None and not op.is_dma:
                    lasts[e] = op
                    break
        dtargets = {d: d.count for d in self.all_dsems if d.count > 0}
        for e in ENGINES:
            op = Op(e, None)
            for x, dep in lasts.items():
                if x == e:
                    continue
                if self.known[e][x] < dep.pos:
                    op.waits_eng[x] = dep
                    dep.needs_inc = True
                    self.known[e][x] = dep.pos
            for d, tgt in dtargets.items():
                if self.known_dma[e].get(d, 0) < tgt:
                    op.waits_dma[d] = tgt
                    self.known_dma[e][d] = tgt
            op.pos = len(self.ops[e])
            self.ops[e].append(op)
            op.clock = dict(self.known[e])
        for t in self.phase_tiles:
            if t.dsem is not None:
                self.free_dsems[self.dsem_owner[t.dsem]].append(t.dsem)
                t.dsem = None
        self.phase_tiles = []

    def emit(self, block):
        for e in ("pe", "act", "dve", "pool"):
            t = 0
            for op in self.ops[e]:
                if op.needs_inc:
                    t += 1
                    op.tick = t
        esem = self.esem

        def run(e, eng):
            for op in self.ops[e]:
                for x, dep in op.waits_eng.items():
                    eng.wait_ge(esem[x], dep.tick)
                for ds, tgt in op.waits_dma.items():
                    eng.wait_ge(ds.sem, tgt)
                if op.fn is None:
                    continue
                ins = op.fn(eng)
                if op.is_dma:
                    ins.then_inc(op.dsem.sem, 16)
                elif op.needs_inc:
                    ins.then_inc(esem[e], 1)

        @block.tensor
        def _(eng):
            run("pe", eng)

        @block.scalar
        def _(eng):
            run("act", eng)

        @block.vector
        def _(eng):
            run("dve", eng)

        @block.gpsimd
        def _(eng):
            run("pool", eng)

        @block.sync
        def _(eng):
            run("sp", eng)


class K:
    pass


_uid = [0]


def _nm(s):
    _uid[0] += 1
    return "%s_%d" % (s, _uid[0])


def sb(k, st, name, shape, dt):
    return st.enter_context(k.nc.sbuf_tensor(_nm(name), list(shape), dt))


def pst(k, st, name, shape, dt):
    return st.enter_context(k.nc.psum_tensor(_nm(name), list(shape), dt))


def load_cast(k, dst_ap, src_ap, lt):
    P = k.P
    if len(P.swq) >= 8:
        old = P.swq.pop(0)
        w = Op("pool", None)
        if P.known_dma["pool"].get(old.dsem, 0) < old.dtarget:
            w.waits_dma[old.dsem] = old.dtarget
            P.known_dma["pool"][old.dsem] = old.dtarget
        w.pos = len(P.ops["pool"])
        P.ops["pool"].append(w)
    op = P.add("pool", lambda e: e.dma_start(out=dst_ap, in_=src_ap), writes=[lt], dma_tile=lt)
    P.swq.append(op)


def dma(k, dst_ap, src_ap, reads=(), writes=(), tile=None):
    k.P.add("sp", lambda e: e.dma_start(out=dst_ap, in_=src_ap), reads=reads, writes=writes, dma_tile=tile)


class PostNorm:
    NB = 3

    def __init__(self, k, st, g_ap, b_ap, x_src, x_dst, nt):
        self.k = k
        self.nt = nt
        self.x_src = x_src
        self.x_dst = x_dst
        NB = self.NB
        self.G = sb(k, st, "lnG", [128, D], F32)
        self.B = sb(k, st, "lnB", [128, D], F32)
        self.G_lt, self.B_lt = LT(), LT()
        dma(k, self.G[:], g_ap.partition_broadcast(128), writes=[self.G_lt], tile=self.G_lt)
        dma(k, self.B[:], b_ap.partition_broadcast(128), writes=[self.B_lt], tile=self.B_lt)
        self.xin = [sb(k, st, "pn_x", [128, D], F32) for _ in range(NB)]
        self.u = [sb(k, st, "pn_u", [128, D], F32) for _ in range(NB)]
        self.y = [sb(k, st, "pn_y", [128, D], F32) for _ in range(NB)]
        self.ybf = [sb(k, st, "pn_yb", [128, D], BF16) for _ in range(2)]
        self.stt = [sb(k, st, "pn_st", [128, 2, 6], F32) for _ in range(NB)]
        self.mv = [sb(k, st, "pn_mv", [128, 2], F32) for _ in range(NB)]
        self.rstd = [sb(k, st, "pn_rs", [128, 1], F32) for _ in range(NB)]
        self.nmr = [sb(k, st, "pn_nm", [128, 1], F32) for _ in range(NB)]
        self.tp = [pst(k, st, "pn_tp", [128, 8, 128], BF16) for _ in range(2)]
        self.l_xin, self.l_u, self.l_y = lts(NB), lts(NB), lts(NB)
        self.l_st, self.l_mv, self.l_rstd, self.l_nmr = lts(NB), lts(NB), lts(NB), lts(NB)
        self.l_ybf, self.l_tp = lts(2), lts(2)
        self.subs = {}
        self.prefetch(0)
        if nt > 1:
            self.prefetch(1)

    def prefetch(self, t):
        k = self.k
        b = t % self.NB
        dma(k, self.xin[b][:], self.x_src[t * 128:(t + 1) * 128, :], writes=[self.l_xin[b]], tile=self.l_xin[b])

    def pre(self, t):
        if t - 2 >= 0:
            self._s3a(t - 2)

    def step(self, t, sub_ps, sub_lt):
        if t - 2 >= 0:
            self._s3b(t - 2)
        self._s1(t, sub_ps, sub_lt)
        if t - 1 >= 0:
            self._s2(t - 1)
        if t + 2 < self.nt:
            self.prefetch(t + 2)

    def drain(self):
        nt = self.nt
        if nt - 2 >= 0:
            self._s3a(nt - 2)
            self._s3b(nt - 2)
        self._s2(nt - 1)
        self._s3a(nt - 1)
        self._s3b(nt - 1)

    def _s1(self, t, sub_ps, sub_lt):
        k = self.k
        P = k.P
        b = t % self.NB
        xin, u, stt, mv, rstd, nmr = self.xin[b], self.u[b], self.stt[b], self.mv[b], self.rstd[b], self.nmr[b]
        l_xin, l_u, l_st, l_mv, l_rstd, l_nmr = (self.l_xin[b], self.l_u[b], self.l_st[b], self.l_mv[b],
                                                  self.l_rstd[b], self.l_nmr[b])
        P.add("dve", lambda e: e.scalar_tensor_tensor(out=u[:], in0=xin[:], scalar=ALPHA, in1=sub_ps,
                                                      op0=ALU.mult, op1=ALU.add),
              reads=[l_xin, sub_lt], writes=[l_u])
        for c in range(2):
            P.add("dve", lambda e, c=c: e.bn_stats(out=stt[:, c, :], in_=u[:, c * 512:(c + 1) * 512]),
                  reads=[l_u], writes=[l_st])
        P.add("dve", lambda e: e.bn_aggr(out=mv[:], in_=stt[:]), reads=[l_st], writes=[l_mv])
        P.add("pool", lambda e: e.tensor_scalar(out=rstd[:], in0=mv[:, 1:2], scalar1=LN_EPS, scalar2=1.0,
                                                op0=ALU.add, op1=ALU.mult), reads=[l_mv], writes=[l_rstd])
        P.add("pool", lambda e: e.tensor_tensor(out=rstd[:], in0=rstd[:], in1=k.mhalf[:], op=ALU.pow),
              reads=[l_rstd, k.eps_lt], writes=[l_rstd])
        P.add("pool", lambda e: e.tensor_scalar(out=nmr[:], in0=mv[:, 0:1], scalar1=-1.0, scalar2=1.0,
                                                op0=ALU.mult, op1=ALU.mult), reads=[l_mv], writes=[l_nmr])
        P.add("pool", lambda e: e.tensor_tensor(out=nmr[:], in0=nmr[:], in1=rstd[:], op=ALU.mult),
              reads=[l_nmr, l_rstd], writes=[l_nmr])

    def _s2(self, t):
        k = self.k
        P = k.P
        b = t % self.NB
        u, y, rstd, nmr = self.u[b], self.y[b], self.rstd[b], self.nmr[b]
        l_u, l_y, l_rstd, l_nmr = self.l_u[b], self.l_y[b], self.l_rstd[b], self.l_nmr[b]
        P.add("act", lambda e: e.activation(out=u[:], in_=u[:], func=AF.Identity, bias=nmr[:], scale=rstd[:]),
              reads=[l_u, l_nmr, l_rstd], writes=[l_u])
        P.add("dve", lambda e: e.tensor_tensor(out=u[:], in0=u[:], in1=self.G[:], op=ALU.mult),
              reads=[l_u, self.G_lt], writes=[l_u])
        P.add("pool", lambda e: e.tensor_tensor(out=y[:], in0=u[:], in1=self.B[:], op=ALU.add),
              reads=[l_u, self.B_lt], writes=[l_y])
        dma(k, self.x_dst[t * 128:(t + 1) * 128, :], y[:], reads=[l_y], tile=l_y)

    def _s3a(self, t):
        k = self.k
        P = k.P
        if not k.need_xt:
            return
        b = t % self.NB
        y, l_y = self.y[b], self.l_y[b]
        ybf, l_ybf = self.ybf[t % 2], self.l_ybf[t % 2]
        P.add("act", lambda e: e.copy(out=ybf[:], in_=y[:]), reads=[l_y], writes=[l_ybf])

    def _s3b(self, t):
        k = self.k
        P = k.P
        if not k.need_xt:
            return
        ybf, l_ybf = self.ybf[t % 2], self.l_ybf[t % 2]
        tp, l_tp = self.tp[t % 2], self.l_tp[t % 2]
        for c in range(8):
            P.add("pe", lambda e, c=c: e.transpose(out=tp[:, c, :], in_=ybf[:, c * 128:(c + 1) * 128],
                                                   identity=k.ident[:]),
                  reads=[l_ybf, k.ident_lt], writes=[l_tp])
        XT = k.XT
        P.add("dve", lambda e: e.tensor_copy(out=XT[:, :, t * 128:(t + 1) * 128], in_=tp[:]),
              reads=[l_tp], writes=[k.XT_lt[t]])


def phase_prep(k, x_in):
    P = k.P
    S = k.S
    with ExitStack() as st:
        xin = [sb(k, st, "pp_x", [128, D], F32) for _ in range(2)]
        xbf = [sb(k, st, "pp_xb", [128, D], BF16) for _ in range(2)]
        tp = [pst(k, st, "pp_tp", [128, 8, 128], BF16) for _ in range(2)]
        l_x, l_xb, l_tp = lts(2), lts(2), lts(2)
        for t in range(S // 128):
            b = t % 2
            dma(k, xin[b][:], x_in[t * 128:(t + 1) * 128, :], writes=[l_x[b]], tile=l_x[b])
            P.add("act", lambda e, b=b: e.copy(out=xbf[b][:], in_=xin[b][:]), reads=[l_x[b]], writes=[l_xb[b]])
            for c in range(8):
                P.add("pe", lambda e, b=b, c=c: e.transpose(out=tp[b][:, c, :], in_=xbf[b][:, c * 128:(c + 1) * 128],
                                                            identity=k.ident[:]),
                      reads=[l_xb[b], k.ident_lt], writes=[l_tp[b]])
            XT = k.XT
            P.add("dve", lambda e, b=b, t=t: e.tensor_copy(out=XT[:, :, t * 128:(t + 1) * 128], in_=tp[b][:]),
                  reads=[l_tp[b]], writes=[k.XT_lt[t]])
        P.barrier()


def load_wo(k, st, w_ap, bias_ap):
    wo = sb(k, st, "wo", [128, 8, D], BF16)
    wo_lt = lts(8)
    wv = w_ap.rearrange("(c p) n -> p c n", p=128)
    for c in range(8):
        load_cast(k, wo[:, c, :], wv[:, c, :], wo_lt[c])
    bo, bo_lt = None, None
    if bias_ap is not None:
        bo = sb(k, st, "bo", [1, D], BF16)
        bo_lt = LT()
        load_cast(k, bo[:], bias_ap.unsqueeze(0), bo_lt)
    return wo, wo_lt, bo, bo_lt


def phase_proj_postnorm(k, w_ap, bias_ap, g_ap, b_ap, x_src, x_dst, pre=None):
    P = k.P
    S = k.S
    with ExitStack() as st:
        if pre is None:
            pre = load_wo(k, st, w_ap, bias_ap)
        wo, wo_lt, bo, bo_lt = pre
        ob = [sb(k, st, "ob", [128, 8, 512], BF16) for _ in range(2)]
        ob_lt = lts(2)
        sub = [pst(k, st, "sub", [128, D], F32) for _ in range(2)]
        sub_lt = lts(2)
        nt = S // 128
        pn = PostNorm(k, st, g_ap, b_ap, x_src, x_dst, nt)
        otv = k.OT.rearrange("c p s -> p c s")
        def load_ob(blk):
            bb = blk % 2
            dma(k, ob[bb][:], otv[:, :, blk * 512:(blk + 1) * 512], writes=[ob_lt[bb]], tile=ob_lt[bb])

        load_ob(0)
        for t in range(nt):
            blk = t // 4
            bb = blk % 2
            if t % 4 == 0 and (blk + 1) * 512 < S:
                load_ob(blk + 1)
            s_ = t % 2
            pn.pre(t)
            for c in range(8):
                for hf in range(2):
                    P.add("pe", lambda e, c=c, hf=hf, bb=bb, s_=s_, t=t: e.matmul(
                        sub[s_][:, hf * 512:(hf + 1) * 512], ob[bb][:, c, (t % 4) * 128:(t % 4 + 1) * 128],
                        wo[:, c, hf * 512:(hf + 1) * 512], start=(c == 0), stop=(c == 7 and bias_ap is None)),
                        reads=[ob_lt[bb], wo_lt[c]], writes=[sub_lt[s_]])
            if bias_ap is not None:
                for hf in range(2):
                    P.add("pe", lambda e, hf=hf, s_=s_: e.matmul(
                        sub[s_][:, hf * 512:(hf + 1) * 512], k.ones[0:1, :], bo[0:1, hf * 512:(hf + 1) * 512],
                        start=False, stop=True), reads=[bo_lt, k.ones_lt], writes=[sub_lt[s_]])
            pn.step(t, sub[s_][:], sub_lt[s_])
        pn.drain()
        P.barrier()


def phase_ffn_up(k, w_up, cp_ap):
    P = k.P
    S = k.S
    nblk = S // 512
    with ExitStack() as st:
        cp = sb(k, st, "cp", [128, NJ, 8], F32)
        cp_lt = LT()
        dma(k, cp[:], cp_ap, writes=[cp_lt], tile=cp_lt)
        wu = [sb(k, st, "wu", [128, 8, 256], BF16) for _ in range(3)]
        wu_lt = lts(3)
        H = [[sb(k, st, "H", [128, S + 2], F32) for _ in range(2)] for _ in range(2)]
        H_lt = [[lts(nblk) for _ in range(2)] for _ in range(2)]
        pad_lt = LT()
        for hf in range(2):
            for jb in range(2):
                P.add("pool", lambda e, hf=hf, jb=jb: e.memset(H[hf][jb][:, 0:1], 0.0), writes=[pad_lt])
                P.add("pool", lambda e, hf=hf, jb=jb: e.memset(H[hf][jb][:, S + 1:S + 2], 0.0), writes=[pad_lt])
        NTB = 3
        Tg = [sb(k, st, "Tg", [128, 512], F32) for _ in range(NTB)]
        Tv = [sb(k, st, "Tv", [128, 512], F32) for _ in range(NTB)]
        Tg_lt, Tv_lt = lts(NTB), lts(NTB)
        TT = [Tg, Tv]
        TT_lt = [Tg_lt, Tv_lt]
        A = [sb(k, st, "A", [128, S], BF16) for _ in range(2)]
        A_lt = lts(2)
        hp = [[pst(k, st, "hp", [128, 512], F32) for _ in range(2)] for _ in range(2)]
        hp_lt = [lts(2), lts(2)]
        wv = w_up.rearrange("(kc p) f -> p kc f", p=128)
        XT = k.XT

        def load_w(j):
            wb = j % 3
            load_cast(k, wu[wb][:, :, 0:128], wv[:, :, j * 128:(j + 1) * 128], wu_lt[wb])
            load_cast(k, wu[wb][:, :, 128:256], wv[:, :, DFF + j * 128:DFF + (j + 1) * 128], wu_lt[wb])

        load_w(0)
        if NJ > 1:
            load_w(1)
        cnt = 0
        for j in range(NJ):
            if j + 2 < NJ:
                load_w(j + 2)
            wb = j % 3
            jb = j % 2

            def conv(blk, j=j, jb=jb):
                o = blk * 512
                tb = blk % NTB
                rl = [H_lt[0][jb][b2] for b2 in (blk - 1, blk, blk + 1) if 0 <= b2 < nblk] + [pad_lt, cp_lt]
                rv = [H_lt[1][jb][b2] for b2 in (blk - 1, blk, blk + 1) if 0 <= b2 < nblk] + [pad_lt, cp_lt]
                Hg, Hv = H[0][jb], H[1][jb]
                tg, tv = Tg[tb], Tv[tb]
                P.add("dve", lambda e: e.scalar_tensor_tensor(out=tg[:], in0=Hg[:, o:o + 512], scalar=cp[:, j, 0:1],
                                                              in1=tg[:], op0=ALU.mult, op1=ALU.add),
                      reads=rl + [Tg_lt[tb]], writes=[Tg_lt[tb]])
                P.add("dve", lambda e: e.scalar_tensor_tensor(out=tg[:], in0=Hg[:, 2 + o:2 + o + 512],
                                                              scalar=cp[:, j, 2:3], in1=tg[:], op0=ALU.mult,
                                                              op1=ALU.add),
                      reads=rl + [Tg_lt[tb]], writes=[Tg_lt[tb]])
                P.add("dve", lambda e: e.scalar_tensor_tensor(out=tv[:], in0=Hv[:, o:o + 512], scalar=cp[:, j, 4:5],
                                                               in1=tv[:], op0=ALU.mult, op1=ALU.add),
                      reads=rv + [Tv_lt[tb]], writes=[Tv_lt[tb]])
                P.add("dve", lambda e: e.scalar_tensor_tensor(out=tv[:], in0=Hv[:, 2 + o:2 + o + 512],
                                                               scalar=cp[:, j, 6:7], in1=tv[:], op0=ALU.mult,
                                                               op1=ALU.add),
                      reads=rv + [Tv_lt[tb]], writes=[Tv_lt[tb]])
                P.add("act", lambda e: e.activation(out=tg[:], in_=tg[:], func=AF.Silu),
                      reads=[Tg_lt[tb]], writes=[Tg_lt[tb]])
                P.add("pool", lambda e: e.tensor_tensor(out=A[jb][:, o:o + 512], in0=tg[:], in1=tv[:], op=ALU.mult),
                      reads=[Tg_lt[tb], Tv_lt[tb]], writes=[A_lt[jb]])

            for blk in range(nblk):
                for hf in range(2):
                    pb = cnt % 2
                    for kc in range(8):
                        P.add("pe", lambda e, hf=hf, pb=pb, kc=kc, blk=blk, wb=wb: e.matmul(
                            hp[hf][pb][:], wu[wb][:, kc, hf * 128:(hf + 1) * 128],
                            XT[:, kc, blk * 512:(blk + 1) * 512], start=(kc == 0), stop=(kc == 7)),
                            reads=[wu_lt[wb]] + k.XT_lt[blk * 4:blk * 4 + 4], writes=[hp_lt[hf][pb]])
                    P.add("act", lambda e, hf=hf, pb=pb, blk=blk, jb=jb: e.copy(
                        out=H[hf][jb][:, 1 + blk * 512:1 + (blk + 1) * 512], in_=hp[hf][pb][:]),
                        reads=[hp_lt[hf][pb]], writes=[H_lt[hf][jb][blk]])
                    P.add("act", lambda e, hf=hf, pb=pb, blk=blk, j=j: e.activation(
                        out=TT[hf][blk % NTB][:], in_=hp[hf][pb][:], func=AF.Identity,
                        bias=cp[:, j, hf * 4 + 3:hf * 4 + 4], scale=cp[:, j, hf * 4 + 1:hf * 4 + 2]),
                        reads=[hp_lt[hf][pb], cp_lt], writes=[TT_lt[hf][blk % NTB]])
                cnt += 1
                if blk >= 1:
                    conv(blk - 1)
            conv(nblk - 1)
            dma(k, k.AT[j], A[jb][:], reads=[A_lt[jb]], tile=A_lt[jb])
        P.barrier()


def phase_ffn_down(k, w_down, g_ap, b_ap, x_src, x_dst):
    P = k.P
    S = k.S
    with ExitStack() as st:
        wd = sb(k, st, "wd", [128, NJ, D], BF16)
        wd_lt = lts(NJ)
        wv = w_down.rearrange("(c p) n -> p c n", p=128)
        for c in range(NJ):
            load_cast(k, wd[:, c, :], wv[:, c, :], wd_lt[c])
        ab = [sb(k, st, "ab", [128, NJ, 256], BF16) for _ in range(2)]
        ab_lt = lts(2)
        sub = [pst(k, st, "sub", [128, D], F32) for _ in range(2)]
        sub_lt = lts(2)
        nt = S // 128
        pn = PostNorm(k, st, g_ap, b_ap, x_src, x_dst, nt)
        atv = k.AT.rearrange("c p s -> p c s")

        def load_ab(blk):
            bb = blk % 2
            dma(k, ab[bb][:], atv[:, :, blk * 256:(blk + 1) * 256], writes=[ab_lt[bb]], tile=ab_lt[bb])

        load_ab(0)
        for t in range(nt):
            blk = t // 2
            bb = blk % 2
            if t % 2 == 0 and (blk + 1) * 256 < S:
                load_ab(blk + 1)
            s_ = t % 2
            pn.pre(t)
            for c in range(NJ):
                for hf in range(2):
                    P.add("pe", lambda e, c=c, hf=hf, bb=bb, s_=s_, t=t: e.matmul(
                        sub[s_][:, hf * 512:(hf + 1) * 512], ab[bb][:, c, (t % 2) * 128:(t % 2 + 1) * 128],
                        wd[:, c, hf * 512:(hf + 1) * 512], start=(c == 0), stop=(c == NJ - 1)),
                        reads=[ab_lt[bb], wd_lt[c]], writes=[sub_lt[s_]])
            pn.step(t, sub[s_][:], sub_lt[s_])
        pn.drain()
        P.barrier()


def phase_qkv_gqa(k, w_qkv, gain_ap, rope_ap):
    P = k.P
    S = k.S
    with ExitStack() as st:
        wq = sb(k, st, "wq", [128, 8, 1536], BF16)
        wq_lt = lts(8)
        wv = w_qkv.rearrange("(c p) n -> p c n", p=128)
        for c in range(8):
            load_cast(k, wq[:, c, :], wv[:, c, :], wq_lt[c])
        gain = sb(k, st, "gain", [128, 1280], F32)
        gain_lt = LT()
        dma(k, gain[:], gain_ap.partition_broadcast(128), writes=[gain_lt], tile=gain_lt)
        rp = [sb(k, st, "rp", [128, 2, 64], F32) for _ in range(2)]
        rp_lt = lts(2)
        qkv = [pst(k, st, "qkv", [128, 1536], F32) for _ in range(2)]
        qkv_lt = lts(2)
        tq = pst(k, st, "tq", [128, 8, 128], BF16)
        tk = pst(k, st, "tk", [128, 4, 128], BF16)
        tq_lt, tk_lt = LT(), LT()
        sq = [sb(k, st, "sq", [128, 1280], F32) for _ in range(2)]
        qn = [sb(k, st, "qn", [128, 1280], F32) for _ in range(2)]
        t1 = [sb(k, st, "t1", [128, 1280], F32) for _ in range(2)]
        t2 = [sb(k, st, "t2", [128, 1280], F32) for _ in range(2)]
        ss = [sb(k, st, "ss", [128, 20], F32) for _ in range(2)]
        qb = [sb(k, st, "qb", [128, 1024], BF16) for _ in range(2)]
        kd = [sb(k, st, "kd", [128, 4, 128], BF16) for _ in range(2)]
        va = [sb(k, st, "va", [128, 4, 2, 128], BF16) for _ in range(2)]
        l_sq, l_qn, l_t1, l_t2, l_ss, l_qb, l_kd, l_va = (lts(2) for _ in range(8))
        for b in range(2):
            P.add("pool", lambda e, b=b: e.memset(va[b][:], 1.0), writes=[l_va[b]])
        qs = [sb(k, st, "qs", [128, 8, 512], BF16) for _ in range(2)]
        ks = [sb(k, st, "ks", [128, 4, 512], BF16) for _ in range(2)]
        l_qs, l_ks = lts(2), lts(2)
        XT = k.XT
        qtv = k.QT.rearrange("c p s -> p c s")
        ktv = k.KT.rearrange("c p s -> p c s")
        pending = []
        for t in range(S // 128):
            b = t % 2
            blk = t // 4
            sbb = blk % 2
            dma(k, rp[b][:], rope_ap[t * 128:(t + 1) * 128], writes=[rp_lt[b]], tile=rp_lt[b])
            for kc in range(8):
                for n in range(3):
                    P.add("pe", lambda e, kc=kc, n=n, b=b, t=t: e.matmul(
                        qkv[b][:, n * 512:(n + 1) * 512], XT[:, kc, t * 128:(t + 1) * 128],
                        wq[:, kc, n * 512:(n + 1) * 512], start=(kc == 0), stop=(kc == 7)),
                        reads=[k.XT_lt[t], wq_lt[kc]], writes=[qkv_lt[b]])
            if pending:
                pending.pop(0)()
            qk_ps = qkv[b][:, 0:1280]
            P.add("act", lambda e, b=b, qk_ps=qk_ps: e.activation(out=sq[b][:], in_=qk_ps, func=AF.Square),
                  reads=[qkv_lt[b]], writes=[l_sq[b]])
            P.add("dve", lambda e, b=b: e.tensor_reduce(out=ss[b][:], in_=sq[b][:].rearrange("p (h d) -> p h d", d=64),
                                                        axis=AX.X, op=ALU.add),
                  reads=[l_sq[b]], writes=[l_ss[b]])
            P.add("act", lambda e, b=b: e.activation(out=ss[b][:], in_=ss[b][:], func=AF.Sqrt, bias=k.eps_rms[:],
                                                     scale=1.0 / 64), reads=[l_ss[b], k.eps_lt], writes=[l_ss[b]])
            P.add("dve", lambda e, b=b: e.reciprocal(out=ss[b][:], in_=ss[b][:]), reads=[l_ss[b]], writes=[l_ss[b]])
            P.add("dve", lambda e, b=b, qk_ps=qk_ps: e.tensor_tensor(
                out=qn[b][:].rearrange("p (h d) -> p h d", d=64), in0=qk_ps.rearrange("p (h d) -> p h d", d=64),
                in1=ss[b][:].unsqueeze(2).broadcast_to([128, 20, 64]), op=ALU.mult),
                reads=[qkv_lt[b], l_ss[b]], writes=[l_qn[b]])
            P.add("pool", lambda e, b=b: e.tensor_tensor(out=qn[b][:], in0=qn[b][:], in1=gain[:], op=ALU.mult),
                  reads=[l_qn[b], gain_lt], writes=[l_qn[b]])
            P.add("pool", lambda e, b=b: e.tensor_tensor(
                out=t1[b][:].rearrange("p (h d) -> p h d", d=64), in0=qn[b][:].rearrange("p (h d) -> p h d", d=64),
                in1=rp[b][:, 0, :].unsqueeze(1).broadcast_to([128, 20, 64]), op=ALU.mult),
                reads=[l_qn[b], rp_lt[b]], writes=[l_t1[b]])
            for a in range(2):
                P.add("dve", lambda e, b=b, a=a: e.tensor_tensor(
                    out=t2[b][:].rearrange("p (h g a d) -> p h g a d", g=2, a=2, d=16)[:, :, :, a, :],
                    in0=qn[b][:].rearrange("p (h g a d) -> p h g a d", g=2, a=2, d=16)[:, :, :, 1 - a, :],
                    in1=rp[b][:, 1, :].rearrange("p (g a d) -> p g a d", g=2, a=2, d=16)[:, :, a, :].unsqueeze(1)
                    .broadcast_to([128, 20, 2, 16]),
                    op=ALU.mult), reads=[l_qn[b], rp_lt[b]], writes=[l_t2[b]])
            P.add("dve", lambda e, b=b: e.tensor_tensor(out=qb[b][:], in0=t1[b][:, 0:1024], in1=t2[b][:, 0:1024],
                                                        op=ALU.add), reads=[l_t1[b], l_t2[b]], writes=[l_qb[b]])
            for hh in range(2):
                P.add("pool", lambda e, b=b, hh=hh: e.tensor_tensor(
                    out=kd[b][:, :, hh * 64:(hh + 1) * 64], in0=t1[b][:, 1024:1280].rearrange("p (g d) -> p g d", d=64),
                    in1=t2[b][:, 1024:1280].rearrange("p (g d) -> p g d", d=64), op=ALU.add),
                    reads=[l_t1[b], l_t2[b]], writes=[l_kd[b]])
            v_ps = qkv[b][:, 1280:1536].rearrange("p (g d) -> p g d", d=64)
            P.add("act", lambda e, b=b, v_ps=v_ps: e.copy(out=va[b][:, :, 0, 0:64], in_=v_ps),
                  reads=[qkv_lt[b]], writes=[l_va[b]])
            P.add("act", lambda e, b=b, v_ps=v_ps: e.copy(out=va[b][:, :, 1, 64:128], in_=v_ps),
                  reads=[qkv_lt[b]], writes=[l_va[b]])
            dma(k, k.VA[t], va[b][:].rearrange("p g v d -> p (g v d)"), reads=[l_va[b]], tile=l_va[b])
            def tail(t=t, b=b, blk=blk, sbb=sbb):
                for c in range(8):
                    P.add("pe", lambda e, b=b, c=c: e.transpose(out=tq[:, c, :], in_=qb[b][:, c * 128:(c + 1) * 128],
                                                                identity=k.ident[:]),
                          reads=[l_qb[b], k.ident_lt], writes=[tq_lt])
                for g in range(4):
                    P.add("pe", lambda e, b=b, g=g: e.transpose(out=tk[:, g, :], in_=kd[b][:, g, :], identity=k.ident[:]),
                          reads=[l_kd[b], k.ident_lt], writes=[tk_lt])
                o = (t % 4) * 128
                P.add("act", lambda e, sbb=sbb, o=o: e.copy(out=qs[sbb][:, :, o:o + 128], in_=tq[:]),
                      reads=[tq_lt], writes=[l_qs[sbb]])
                P.add("dve", lambda e, sbb=sbb, o=o: e.tensor_copy(out=ks[sbb][:, :, o:o + 128], in_=tk[:]),
                      reads=[tk_lt], writes=[l_ks[sbb]])
                if t % 4 == 3 or t == S // 128 - 1:
                    dma(k, qtv[:, :, blk * 512:(blk + 1) * 512], qs[sbb][:], reads=[l_qs[sbb]], tile=l_qs[sbb])
                    dma(k, ktv[:, 0:4, blk * 512:(blk + 1) * 512], ks[sbb][:], reads=[l_ks[sbb]], tile=l_ks[sbb])
            pending.append(tail)
        pending.pop(0)()
        P.barrier()


def phase_attn_gqa(k):
    P = k.P
    S = k.S
    nkt = S // 128
    nqb = S // 512
    with ExitStack() as st:
        ktd = [sb(k, st, "ktd", [128, S], BF16) for _ in range(2)]
        vag = [sb(k, st, "vag", [128, nkt, 2, 128], BF16) for _ in range(2)]
        qtc = [sb(k, st, "qtc", [128, 2, S], BF16) for _ in range(2)]
        l_ktd, l_vag, l_qtc = lts(2), lts(2), lts(2)
        sps = [[pst(k, st, "sps", [128, 512], F32) for _ in range(2)] for _ in range(2)]
        l_sps = [lts(2), lts(2)]
        acc = [[pst(k, st, "acc", [128, 512], F32) for _ in range(2)] for _ in range(2)]
        l_acc = [lts(2), lts(2)]
        NPB = 3
        pT = [[sb(k, st, "pT", [128, 512], BF16) for _ in range(NPB)] for _ in range(2)]
        l_pT = [lts(NPB), lts(NPB)]
        rc = [sb(k, st, "rc", [128, 512], F32) for _ in range(2)]
        l_rc = lts(2)
        ost = [sb(k, st, "ost", [128, 512], BF16) for _ in range(2)]
        l_ost = lts(2)
        vav = k.VA.rearrange("t p (g x) -> p t g x", g=4)

        def load_g(g):
            gb = g % 2
            dma(k, ktd[gb][:], k.KT[g], writes=[l_ktd[gb]], tile=l_ktd[gb])
            dma(k, vag[gb][:].rearrange("p t v d -> p t (v d)"), vav[:, :, g, :], writes=[l_vag[gb]], tile=l_vag[gb])
            dma(k, qtc[gb][:], k.QT[2 * g:2 * g + 2].rearrange("c p s -> p c s"), writes=[l_qtc[gb]], tile=l_qtc[gb])

        load_g(0)
        it = 0
        for g in range(4):
            gb = g % 2
            if g + 1 < 4:
                load_g(g + 1)
            for pr in range(2):
                c = 2 * g + pr
                for qb_ in range(nqb):
                    ab = it % 2
                    it += 1

                    def qk(kt, gb=gb, pr=pr, qb_=qb_):
                        s_ = kt % 2
                        for h in range(2):
                            P.add("pe", lambda e, h=h, s_=s_, kt=kt: e.matmul(
                                sps[h][s_][:], ktd[gb][h * 64:(h + 1) * 64, kt * 128:(kt + 1) * 128],
                                qtc[gb][h * 64:(h + 1) * 64, pr, qb_ * 512:(qb_ + 1) * 512], start=True, stop=True),
                                reads=[l_ktd[gb], l_qtc[gb]], writes=[l_sps[h][s_]])
                        for h in range(2):
                            pb = kt % NPB
                            P.add("act", lambda e, h=h, s_=s_, pb=pb: e.activation(
                                out=pT[h][pb][:], in_=sps[h][s_][:], func=AF.Exp, scale=0.125),
                                reads=[l_sps[h][s_]], writes=[l_pT[h][pb]])

                    def pv(kt, gb=gb, ab=ab):
                        pb = kt % NPB
                        for h in range(2):
                            P.add("pe", lambda e, h=h, pb=pb, kt=kt: e.matmul(
                                acc[h][ab][:], vag[gb][:, kt, h, :], pT[h][pb][:], start=(kt == 0),
                                stop=(kt == nkt - 1)),
                                reads=[l_vag[gb], l_pT[h][pb]], writes=[l_acc[h][ab]])

                    for kt in range(nkt + 1):
                        if kt < nkt:
                            qk(kt)
                        if kt >= 1:
                            pv(kt - 1)
                    ob_ = ab
                    P.add("dve", lambda e, ab=ab: e.reciprocal(out=rc[ab][0:64, :], in_=acc[0][ab][64:128, :]),
                          reads=[l_acc[0][ab]], writes=[l_rc[ab]])
                    P.add("dve", lambda e, ab=ab: e.reciprocal(out=rc[ab][64:128, :], in_=acc[1][ab][0:64, :]),
                          reads=[l_acc[1][ab]], writes=[l_rc[ab]])
                    P.add("dve", lambda e, ab=ab: e.tensor_tensor(out=ost[ab][0:64, :], in0=acc[0][ab][0:64, :],
                                                                  in1=rc[ab][0:64, :], op=ALU.mult),
                          reads=[l_acc[0][ab], l_rc[ab]], writes=[l_ost[ab]])
                    P.add("dve", lambda e, ab=ab: e.tensor_tensor(out=ost[ab][64:128, :], in0=acc[1][ab][64:128, :],
                                                                  in1=rc[ab][64:128, :], op=ALU.mult),
                          reads=[l_acc[1][ab], l_rc[ab]], writes=[l_ost[ab]])
                    dma(k, k.OT[c][:, qb_ * 512:(qb_ + 1) * 512], ost[ab][:], reads=[l_ost[ab]], tile=l_ost[ab])
        P.barrier()


def phase_qkv_diff(k, w_qkv, rope_ap):
    P = k.P
    S = k.S
    with ExitStack() as st:
        wq = sb(k, st, "wq", [128, 8, 3072], BF16)
        wq_lt = lts(8)
        wv = w_qkv.rearrange("(c p) n -> p c n", p=128)
        for c in range(8):
            for q3 in range(2):
                load_cast(k, wq[:, c, q3 * 1536:(q3 + 1) * 1536], wv[:, c, q3 * 1536:(q3 + 1) * 1536], wq_lt[c])
        rp = [sb(k, st, "rp", [128, 2, 128], F32) for _ in range(2)]
        rp_lt = lts(2)
        ps_ = [pst(k, st, "qps", [128, 1024], F32) for _ in range(2)]
        ps_lt = lts(2)
        tq = [pst(k, st, "tq", [128, 8, 128], BF16) for _ in range(2)]
        tq_lt = lts(2)
        xb = [sb(k, st, "xb", [128, 1024], BF16) for _ in range(2)]
        l_xb = lts(2)
        xf = [sb(k, st, "xf", [128, 1024], F32) for _ in range(2)]
        l_xf = lts(2)
        tmp = [[sb(k, st, "tmp", [128, 16, 8], F32) for _ in range(4)] for _ in range(2)]
        l_tmp = [lts(4), lts(4)]
        stg = [[sb(k, st, "stg", [128, 8, 512], BF16) for _ in range(2)] for _ in range(2)]
        l_stg = [lts(2), lts(2)]
        XT = k.XT
        dstv = [k.QT.rearrange("c p s -> p c s"), k.KT.rearrange("c p s -> p c s")]
        cnt = 0
        pending = []
        for t in range(S // 128):
            rb = t % 2
            blk = t // 4
            sbb = blk % 2
            dma(k, rp[rb][:], rope_ap[t * 128:(t + 1) * 128], writes=[rp_lt[rb]], tile=rp_lt[rb])
            for part in range(3):
                b = cnt % 2
                cnt += 1
                for kc in range(8):
                    for n in range(2):
                        P.add("pe", lambda e, kc=kc, n=n, b=b, t=t, part=part: e.matmul(
                            ps_[b][:, n * 512:(n + 1) * 512], XT[:, kc, t * 128:(t + 1) * 128],
                            wq[:, kc, part * 1024 + n * 512:part * 1024 + (n + 1) * 512], start=(kc == 0),
                            stop=(kc == 7)), reads=[k.XT_lt[t], wq_lt[kc]], writes=[ps_lt[b]])
                if pending:
                    pending.pop(0)()
                if part == 2:
                    P.add("act", lambda e, b=b: e.copy(out=xb[b][:], in_=ps_[b][:]), reads=[ps_lt[b]], writes=[l_xb[b]])
                    dma(k, k.VA[t], xb[b][:], reads=[l_xb[b]], tile=l_xb[b])
                    continue
                P.add("act", lambda e, b=b: e.copy(out=xf[b][:], in_=ps_[b][:]), reads=[ps_lt[b]], writes=[l_xf[b]])
                xfv = xf[b][:].rearrange("p (m d) -> p m d", d=64)
                x1, x2 = xfv[:, :, 0:8], xfv[:, :, 8:16]
                cb = rp[rb][:, 0, :].rearrange("p (m d) -> p m d", d=8)
                sn = rp[rb][:, 1, :].rearrange("p (m d) -> p m d", d=8)
                tm = tmp[b]
                lt_ = l_tmp[b]
                for i_, (xx, tb_) in enumerate(((x1, cb), (x2, sn), (x2, cb), (x1, sn))):
                    P.add("dve", lambda e, xx=xx, tb_=tb_, i_=i_, tm=tm: e.tensor_tensor(
                        out=tm[i_][:], in0=xx, in1=tb_, op=ALU.mult),
                        reads=[l_xf[b], rp_lt[rb]], writes=[lt_[i_]])
                P.add("dve", lambda e, tm=tm, x1=x1: e.tensor_tensor(out=x1, in0=tm[0][:], in1=tm[1][:], op=ALU.subtract),
                      reads=[lt_[0], lt_[1], l_xf[b]], writes=[l_xf[b]])
                P.add("dve", lambda e, tm=tm, x2=x2: e.tensor_tensor(out=x2, in0=tm[2][:], in1=tm[3][:], op=ALU.add),
                      reads=[lt_[2], lt_[3], l_xf[b]], writes=[l_xf[b]])
                P.add("act", lambda e, b=b: e.copy(out=xb[b][:], in_=xf[b][:]), reads=[l_xf[b]], writes=[l_xb[b]])
                def tail(t=t, b=b, part=part, blk=blk, sbb=sbb):
                    for c in range(8):
                        P.add("pe", lambda e, b=b, c=c, part=part: e.transpose(
                            out=tq[part][:, c, :], in_=xb[b][:, c * 128:(c + 1) * 128], identity=k.ident[:]),
                            reads=[l_xb[b], k.ident_lt], writes=[tq_lt[part]])
                    o = (t % 4) * 128
                    eng = "act" if part == 0 else "dve"
                    if eng == "act":
                        P.add("act", lambda e, part=part, sbb=sbb, o=o: e.copy(out=stg[part][sbb][:, :, o:o + 128],
                                                                              in_=tq[part][:]),
                              reads=[tq_lt[part]], writes=[l_stg[part][sbb]])
                    else:
                        P.add("dve", lambda e, part=part, sbb=sbb, o=o: e.tensor_copy(out=stg[part][sbb][:, :, o:o + 128],
                                                                                     in_=tq[part][:]),
                              reads=[tq_lt[part]], writes=[l_stg[part][sbb]])
                    if t % 4 == 3 or t == S // 128 - 1:
                        dma(k, dstv[part][:, :, blk * 512:(blk + 1) * 512], stg[part][sbb][:],
                            reads=[l_stg[part][sbb]], tile=l_stg[part][sbb])
                pending.append(tail)
        while pending:
            pending.pop(0)()
        P.barrier()


def phase_attn_diff(k, lam_ap, subln_ap, lambda_init):
    P = k.P
    S = k.S
    nkt = S // 128
    nqb = S // 512
    with ExitStack() as st:
        lv = sb(k, st, "lv", [128, 4, 64], F32)
        lv_lt = LT()
        dma(k, lv[:].rearrange("p a d -> p (a d)"), lam_ap.partition_broadcast(128), writes=[lv_lt], tile=lv_lt)
        pr_ = sb(k, st, "lpr", [128, 2, 64], F32)
        ls = sb(k, st, "ls", [128, 2], F32)
        neglam = sb(k, st, "neglam", [128, 1], F32)
        l_pr, l_ls, l_nl = LT(), LT(), LT()
        P.add("dve", lambda e: e.tensor_tensor(out=pr_[:], in0=lv[:, 0:4:2, :], in1=lv[:, 1:4:2, :], op=ALU.mult),
              reads=[lv_lt], writes=[l_pr])
        P.add("dve", lambda e: e.tensor_reduce(out=ls[:], in_=pr_[:], axis=AX.X, op=ALU.add), reads=[l_pr],
              writes=[l_ls])
        P.add("act", lambda e: e.activation(out=ls[:], in_=ls[:], func=AF.Exp), reads=[l_ls], writes=[l_ls])
        P.add("dve", lambda e: e.scalar_tensor_tensor(out=neglam[:], in0=ls[:, 1:2], scalar=-lambda_init,
                                                      in1=ls[:, 0:1], op0=ALU.add, op1=ALU.subtract),
              reads=[l_ls], writes=[l_nl])
        sl = sb(k, st, "subln", [128, 1], F32)
        sl_lt = LT()
        dma(k, sl[:], subln_ap.unsqueeze(1), writes=[sl_lt], tile=sl_lt)
        P.add("dve", lambda e: e.tensor_single_scalar(out=sl[:], in_=sl[:], scalar=1.0 - lambda_init, op=ALU.mult),
              reads=[sl_lt], writes=[sl_lt])
        kth = [sb(k, st, "kth", [128, S], BF16) for _ in range(2)]
        vh = [sb(k, st, "vh", [128, nkt, 128], BF16) for _ in range(2)]
        qth = [sb(k, st, "qth", [128, S], BF16) for _ in range(2)]
        l_kth, l_vh, l_qth = lts(2), lts(2), lts(2)
        spsb = [pst(k, st, "sps", [128, 512], F32) for _ in range(3)]
        l_spsb = lts(3)
        k.ssp = pst(k, st, "ssp", [128, 512], F32)
        k.ssp_lt = LT()
        scnt = [0]
        acc = [pst(k, st, "acc", [128, 512], F32) for _ in range(2)]
        rs_ = [pst(k, st, "rsum", [128, 512], F32) for _ in range(2)]
        l_acc, l_rs = lts(2), lts(2)
        NPB = 8
        pT = [[sb(k, st, "pT", [128, 512], BF16) for _ in range(NPB)] for _ in range(2)]
        l_pT = [lts(NPB), lts(NPB)]
        pp = [[sb(k, st, "pp", [128, 512], BF16) for _ in range(2)] for _ in range(2)]
        l_pp = [lts(2), lts(2)]
        qq = [[sb(k, st, "qq", [128, 512], BF16) for _ in range(2)] for _ in range(2)]
        l_qq = [lts(2), lts(2)]
        accS = [[sb(k, st, "accS", [128, 512], F32) for _ in range(2)] for _ in range(2)]
        rsS = [[sb(k, st, "rsS", [128, 512], F32) for _ in range(2)] for _ in range(2)]
        l_accS, l_rsS = [lts(2), lts(2)], [lts(2), lts(2)]
        o_ = [sb(k, st, "o", [128, 512], F32) for _ in range(2)]
        sqb = [sb(k, st, "sqb", [128, 512], BF16) for _ in range(2)]
        ssS = [sb(k, st, "ssS", [128, 512], F32) for _ in range(2)]
        l_o, l_sqb, l_ssS = lts(2), lts(2), lts(2)
        ost = [sb(k, st, "ost", [128, 512], BF16) for _ in range(2)]
        l_ost = lts(2)
        vav = k.VA.rearrange("t p (h d) -> p t h d", h=8)

        def load_h(h):
            hb = h % 2
            dma(k, kth[hb][:], k.KT[h], writes=[l_kth[hb]], tile=l_kth[hb])
            dma(k, vh[hb][:], vav[:, :, h, :], writes=[l_vh[hb]], tile=l_vh[hb])
            dma(k, qth[hb][:], k.QT[h], writes=[l_qth[hb]], tile=l_qth[hb])

        load_h(0)
        it = 0
        deferred = []

        def run_deferred(kt):
            while deferred and deferred[0][0] <= kt:
                deferred.pop(0)[1]()

        for h in range(8):
            hb = h % 2
            if h + 1 < 8:
                load_h(h + 1)
            for qb_ in range(nqb):
                par = it % 2
                it += 1

                def qk(kt, hb=hb, qb_=qb_):
                    sb_ = [(scnt[0] + m) % 3 for m in range(2)]
                    scnt[0] += 2
                    for m in range(2):
                        P.add("pe", lambda e, m=m, kt=kt, bk=sb_[m]: e.matmul(
                            spsb[bk][:], kth[hb][m * 64:(m + 1) * 64, kt * 128:(kt + 1) * 128],
                            qth[hb][m * 64:(m + 1) * 64, qb_ * 512:(qb_ + 1) * 512], start=True, stop=True),
                            reads=[l_kth[hb], l_qth[hb]], writes=[l_spsb[sb_[m]]])
                    for m in range(2):
                        pb = kt % NPB
                        P.add("act", lambda e, m=m, pb=pb, bk=sb_[m]: e.activation(
                            out=pT[m][pb][:], in_=spsb[bk][:], func=AF.Exp, scale=0.125),
                            reads=[l_spsb[sb_[m]]], writes=[l_pT[m][pb]])
                    if kt % 2 == 1:
                        pq = (kt // 2) % 2
                        for m in range(2):
                            P.add("dve", lambda e, m=m, pq=pq, kt=kt: e.tensor_tensor(
                                out=pp[m][pq][:], in0=pT[m][(kt - 1) % NPB][:], in1=pT[m][kt % NPB][:], op=ALU.add),
                                reads=[l_pT[m][(kt - 1) % NPB], l_pT[m][kt % NPB]], writes=[l_pp[m][pq]])
                        if kt % 4 == 3:
                            qi = (kt // 4) % 2
                            for m in range(2):
                                P.add("dve", lambda e, m=m, qi=qi: e.tensor_tensor(
                                    out=qq[m][qi][:], in0=pp[m][0][:], in1=pp[m][1][:], op=ALU.add),
                                    reads=[l_pp[m][0], l_pp[m][1]], writes=[l_qq[m][qi]])

                def pv(kt, hb=hb):
                    pb = kt % NPB
                    for m in range(2):
                        P.add("pe", lambda e, m=m, pb=pb, kt=kt: e.matmul(
                            acc[m][:], vh[hb][:, kt, :], pT[m][pb][:], start=(kt == 0), stop=(kt == nkt - 1)),
                            reads=[l_vh[hb], l_pT[m][pb]], writes=[l_acc[m]])
                    if kt % 4 == 3:
                        qi = (kt // 4) % 2
                        for m in range(2):
                            P.add("pe", lambda e, m=m, qi=qi, kt=kt: e.matmul(
                                rs_[m][:], k.ones[:], qq[m][qi][:], start=(kt == 3), stop=(kt == nkt - 1)),
                                reads=[k.ones_lt, l_qq[m][qi]], writes=[l_rs[m]])

                for kt in range(nkt + 1):
                    if kt < nkt:
                        qk(kt)
                    if kt >= 1:
                        pv(kt - 1)
                    run_deferred(kt)
                run_deferred(10 ** 9)
                for m in range(2):
                    P.add("dve", lambda e, m=m, par=par: e.tensor_copy(out=accS[par][m][:], in_=acc[m][:]),
                          reads=[l_acc[m]], writes=[l_accS[par][m]])
                    P.add("dve", lambda e, m=m, par=par: e.tensor_copy(out=rsS[par][m][:], in_=rs_[m][:]),
                          reads=[l_rs[m]], writes=[l_rsS[par][m]])
                def d_r0(par=par):
                    P.add("dve", lambda e: e.reciprocal(out=rsS[par][0][:], in_=rsS[par][0][:]),
                          reads=[l_rsS[par][0]], writes=[l_rsS[par][0]])

                def d_r1(par=par):
                    P.add("dve", lambda e: e.reciprocal(out=rsS[par][1][:], in_=rsS[par][1][:]),
                          reads=[l_rsS[par][1]], writes=[l_rsS[par][1]])
                    P.add("pool", lambda e: e.tensor_tensor(out=accS[par][0][:], in0=accS[par][0][:],
                                                            in1=rsS[par][0][:], op=ALU.mult),
                          reads=[l_accS[par][0], l_rsS[par][0]], writes=[l_accS[par][0]])

                def d_m1(par=par):
                    P.add("pool", lambda e: e.tensor_tensor(out=accS[par][1][:], in0=accS[par][1][:],
                                                            in1=rsS[par][1][:], op=ALU.mult),
                          reads=[l_accS[par][1], l_rsS[par][1]], writes=[l_accS[par][1]])

                def d_o(par=par):
                    P.add("dve", lambda e: e.scalar_tensor_tensor(out=o_[par][:], in0=accS[par][1][:],
                                                                  scalar=neglam[:], in1=accS[par][0][:],
                                                                  op0=ALU.mult, op1=ALU.add),
                          reads=[l_accS[par][0], l_accS[par][1], l_nl], writes=[l_o[par]])
                    P.add("pool", lambda e: e.tensor_tensor(out=sqb[par][:], in0=o_[par][:], in1=o_[par][:],
                                                            op=ALU.mult), reads=[l_o[par]], writes=[l_sqb[par]])

                def d_ss(par=par):
                    P.add("pe", lambda e: e.matmul(k.ssp[:], k.ones[:], sqb[par][:], start=True, stop=True),
                          reads=[k.ones_lt, l_sqb[par]], writes=[k.ssp_lt])

                def d_sq(par=par):
                    P.add("act", lambda e: e.activation(out=ssS[par][:], in_=k.ssp[:], func=AF.Sqrt, bias=k.eps_rms[:],
                                                        scale=1.0 / 128), reads=[k.ssp_lt, k.eps_lt],
                          writes=[l_ssS[par]])

                def d_rr(par=par):
                    P.add("dve", lambda e: e.reciprocal(out=ssS[par][:], in_=ssS[par][:]),
                          reads=[l_ssS[par]], writes=[l_ssS[par]])

                def d_on(par=par):
                    P.add("pool", lambda e: e.tensor_tensor(out=o_[par][:], in0=o_[par][:], in1=ssS[par][:],
                                                            op=ALU.mult), reads=[l_o[par], l_ssS[par]],
                          writes=[l_o[par]])

                def d_out(par=par, h=h, qb_=qb_):
                    P.add("act", lambda e: e.activation(out=ost[par][:], in_=o_[par][:], func=AF.Identity, scale=sl[:]),
                          reads=[l_o[par], sl_lt], writes=[l_ost[par]])
                    dma(k, k.OT[h][:, qb_ * 512:(qb_ + 1) * 512], ost[par][:], reads=[l_ost[par]], tile=l_ost[par])

                for trig, fn in ((1, d_r0), (4, d_r1), (7, d_m1), (9, d_o), (13, d_ss), (15, d_sq), (17, d_rr),
                                 (21, d_on), (24, d_out)):
                    deferred.append((min(trig, nkt - 1), fn))
        run_deferred(10 ** 9)
        P.barrier()


def phase_fnet(k, cc_ap, dft_ap):
    P = k.P
    S = k.S
    nt = S // 128
    nkb = S // 512
    with ExitStack() as st:
        ucs = sb(k, st, "ucs", [128, nt, 4, 512], BF16)
        ucs_lt = lts(nt)
        with ExitStack() as st1:
            ccs = sb(k, st1, "ccs", [128, 2, 512], BF16)
            ccs_lt = LT()
            load_cast(k, ccs[:], cc_ap.rearrange("(c p) n -> p c n", p=128), ccs_lt)
            p1 = [pst(k, st1, "p1", [128, 512], F32) for _ in range(4)]
            p1_lt = lts(4)
            XT = k.XT
            cnt = 0
            for t in range(nt):
                for gr in range(4):
                    b = cnt % 4
                    cnt += 1
                    for kc in range(2):
                        P.add("pe", lambda e, b=b, kc=kc, gr=gr, t=t: e.matmul(
                            p1[b][:], XT[:, gr * 2 + kc, t * 128:(t + 1) * 128], ccs[:, kc, :], start=(kc == 0),
                            stop=(kc == 1)), reads=[k.XT_lt[t], ccs_lt], writes=[p1_lt[b]])
                    if gr % 2 == 0:
                        P.add("act", lambda e, b=b, gr=gr, t=t: e.copy(out=ucs[:, t, gr, :], in_=p1[b][:]),
                              reads=[p1_lt[b]], writes=[ucs_lt[t]])
                    else:
                        P.add("dve", lambda e, b=b, gr=gr, t=t: e.tensor_copy(out=ucs[:, t, gr, :], in_=p1[b][:]),
                              reads=[p1_lt[b]], writes=[ucs_lt[t]])
            P.barrier()
        nq = 4 if nt >= 4 else 1
        jq = nt // nq
        dq = [k.XT_flat[:, q * (jq * 1024):(q + 1) * (jq * 1024)].rearrange("p (j c n) -> p j c n", c=2, n=512)
              for q in range(nq)]
        dq_lt = lts(nq)
        p2 = [pst(k, st, "p2", [128, 512], F32) for _ in range(2)]
        p2_lt = lts(2)
        fst = [sb(k, st, "fst", [128, 512], BF16) for _ in range(2)]
        fst_lt = lts(2)

        def load_d(kb, q):
            dma(k, dq[q], dft_ap[kb, :, q * jq:(q + 1) * jq], writes=[dq_lt[q]], tile=dq_lt[q])

        for q in range(nq):
            load_d(0, q)
        cnt = 0
        for kb in range(nkb):
            for fc in range(8):
                gr, hh = fc // 2, fc % 2
                b = cnt % 2
                cnt += 1
                for jt in range(nt):
                    q, jj = jt // jq, jt % jq
                    for cs in range(2):
                        P.add("pe", lambda e, b=b, jt=jt, q=q, jj=jj, cs=cs, gr=gr, hh=hh: e.matmul(
                            p2[b][:], ucs[:, jt, gr, cs * 256 + hh * 128:cs * 256 + (hh + 1) * 128], dq[q][:, jj, cs, :],
                            start=(jt == 0 and cs == 0), stop=(jt == nt - 1 and cs == 1)),
                            reads=[ucs_lt[jt], dq_lt[q]], writes=[p2_lt[b]])
                    if fc == 7 and jj == jq - 1 and kb + 1 < nkb:
                        load_d(kb + 1, q)
                P.add("act", lambda e, b=b: e.activation(out=fst[b][:], in_=p2[b][:], func=AF.Identity, scale=1.0 / math.sqrt(S * 256.0)),
                      reads=[p2_lt[b]], writes=[fst_lt[b]])
                dma(k, k.OT[fc][:, kb * 512:(kb + 1) * 512], fst[b][:], reads=[fst_lt[b]], tile=fst_lt[b])
        P.barrier()


def lambda_init_of(layer_idx):
    return 0.8 - 0.6 * math.exp(-0.3 * layer_idx)


def build(S, layers=(0, 1, 2, 3)):
    nc = bass.Bass("TRN2", target_bir_lowering=False)
    k = K()
    k.nc = nc
    k.S = S
    nt = S // 128

    def din(name, shape, dt=F32):
        return nc.dram_tensor(name, list(shape), dt, kind="ExternalInput").ap()

    x_in = din("x", [S, D])
    ident_in = din("ident", [128, 128])
    W = {}
    for l in layers:
        p = "l%d_" % l
        kind = l % 3
        if kind == 0:
            W[p + "wqkv"] = din(p + "wqkv", [D, 1536])
            W[p + "gain"] = din(p + "gain", [1280])
            W[p + "wo"] = din(p + "wo", [D, D])
        elif kind == 1:
            W[p + "wo"] = din(p + "wo", [D, D])
            W[p + "bo"] = din(p + "bo", [D])
        else:
            W[p + "wqkv"] = din(p + "wqkv", [D, 3072])
            W[p + "lam"] = din(p + "lam", [256])
            W[p + "subln"] = din(p + "subln", [128])
            W[p + "wo"] = din(p + "wo", [D, D])
        for nm, shp in (("ln1_g", [D]), ("ln1_b", [D]), ("wup", [D, 2 * DFF]), ("cp", [128, NJ, 8]),
                        ("wdown", [DFF, D]), ("ln2_g", [D]), ("ln2_b", [D])):
            W[p + nm] = din(p + nm, shp)
    kinds = set(l % 3 for l in layers)
    if 0 in kinds:
        rope_a = din("rope_a", [S, 2, 64])
    if 2 in kinds:
        rope_c = din("rope_c", [S, 2, 128])
    if 1 in kinds:
        cc_in = din("cc", [256, 512])
        dft_in = din("dft", [S // 512, 128, nt, 2, 512], BF16)
    y_out = nc.dram_tensor("y", [S, D], F32, kind="ExternalOutput").ap()

    def scr(name, shape, dt):
        return nc.dram_tensor(name, list(shape), dt, kind="Internal").ap()

    k.X32 = scr("x32s", [S, D], F32)
    k.AT = scr("at", [NJ, 128, S], BF16)
    k.QT = scr("qt", [8, 128, S], BF16)
    k.KT = scr("kt", [8, 128, S], BF16)
    k.VA = scr("va", [nt, 128, 1024], BF16)
    k.OT = scr("ot", [8, 128, S], BF16)

    with ExitStack() as gst:
        P = Prog(nc)
        k.P = P
        P.setup_sems(gst, 80)
        k.XT = gst.enter_context(nc.sbuf_tensor("XT_sb", [128, 8, S], BF16))
        k.XT_flat = k.XT[:].rearrange("p c s -> p (c s)")
        k.XT_lt = lts(nt, "XT")
        k.ident = gst.enter_context(nc.sbuf_tensor("ident_sb", [128, 128], BF16))
        k.ident_lt = LT()
        load_cast(k, k.ident[:], ident_in, k.ident_lt)
        k.ones = gst.enter_context(nc.sbuf_tensor("ones_sb", [128, 128], BF16))
        k.ones_lt = LT()
        P.add("pool", lambda e: e.memset(k.ones[:], 1.0), writes=[k.ones_lt])
        k.eps_ln = gst.enter_context(nc.sbuf_tensor("eps_ln", [128, 1], F32))
        k.eps_rms = gst.enter_context(nc.sbuf_tensor("eps_rms", [128, 1], F32))
        k.eps_lt = LT()
        P.add("pool", lambda e: e.memset(k.eps_ln[:], LN_EPS), writes=[k.eps_lt])
        P.add("pool", lambda e: e.memset(k.eps_rms[:], RMS_EPS), writes=[k.eps_lt])
        k.mhalf = gst.enter_context(nc.sbuf_tensor("mhalf", [128, 1], F32))
        P.add("pool", lambda e: e.memset(k.mhalf[:], -0.5), writes=[k.eps_lt])
        k.need_xt = True
        phase_prep(k, x_in)
        x_cur = x_in
        for li, l in enumerate(layers):
            p = "l%d_" % l
            kind = l % 3
            last = (li == len(layers) - 1)
            with ExitStack() as wst:
                pre = None
                if kind == 0:
                    phase_qkv_gqa(k, W[p + "wqkv"], W[p + "gain"], rope_a)
                    pre = load_wo(k, wst, W[p + "wo"], None)
                    phase_attn_gqa(k)
                    bias = None
                elif kind == 1:
                    phase_fnet(k, cc_in, dft_in)
                    bias = W[p + "bo"]
                else:
                    phase_qkv_diff(k, W[p + "wqkv"], rope_c)
                    pre = load_wo(k, wst, W[p + "wo"], None)
                    phase_attn_diff(k, W[p + "lam"], W[p + "subln"], lambda_init_of(l))
                    bias = None
                phase_proj_postnorm(k, W[p + "wo"], bias, W[p + "ln1_g"], W[p + "ln1_b"], x_cur, k.X32, pre=pre)
            x_cur = k.X32
            phase_ffn_up(k, W[p + "wup"], W[p + "cp"])
            k.need_xt = not last
            phase_ffn_down(k, W[p + "wdown"], W[p + "ln2_g"], W[p + "ln2_b"], x_cur, y_out if last else k.X32)
        with nc.Block() as block:
            P.emit(block)
    return nc


def _rope_tables(S):
    def cs(pos, dim, theta):
        inv = theta ** (-np.arange(0, dim, 2, dtype=np.float32) / np.float32(dim))
        ang = pos.astype(np.float32)[:, None] * inv[None, :].astype(np.float32)
        return np.cos(ang).astype(np.float32), np.sin(ang).astype(np.float32)
    rows = S // 64
    t_row = np.repeat(np.arange(rows), 64)
    t_col = np.tile(np.arange(64), rows)
    cr, sr = cs(t_row, 32, 10000.0)
    cc, sc = cs(t_col, 32, 10000.0)
    C = np.concatenate([cr, cr, cc, cc], axis=1)
    Sg = np.concatenate([-sr, sr, -sc, sc], axis=1)
    rope_a = np.stack([C, Sg], axis=1).astype(np.float32)
    c8, s8 = cs(np.arange(S), 16, 500000.0)
    rope_c = np.stack([np.tile(c8, (1, 16)), np.tile(s8, (1, 16))], axis=1).astype(np.float32)
    return rope_a, rope_c


def _dft_tables(S):
    n = np.arange(256)
    ang = 2.0 * np.pi * ((n[:, None] * n[None, :]) % 256) / 256.0
    cc = np.concatenate([np.cos(ang), -np.sin(ang)], axis=1).astype(np.float32)
    j = np.arange(S, dtype=np.int64)
    m = (j[:, None] * j[None, :]) % S
    ang = (2.0 * np.pi / S) * m
    Cs = np.cos(ang).astype(ml_dtypes.bfloat16)
    Ss = np.sin(ang).astype(ml_dtypes.bfloat16)
    nt = S // 128
    d = np.stack([Cs, Ss], axis=0).reshape(2, nt, 128, S // 512, 512)
    d = np.ascontiguousarray(d.transpose(3, 2, 1, 0, 4))
    return cc, d


def make_in_maps(inputs, S, layers=(0, 1, 2, 3), n_cores=8):
    f = lambda a: np.ascontiguousarray(np.asarray(a, dtype=np.float32))
    shared = {"ident": np.eye(128, dtype=np.float32)}
    kinds = set(l % 3 for l in layers)
    rope_a, rope_c = _rope_tables(S)
    if 0 in kinds:
        shared["rope_a"] = rope_a
    if 2 in kinds:
        shared["rope_c"] = rope_c
    if 1 in kinds:
        cc, d = _dft_tables(S)
        shared["cc"] = cc
        shared["dft"] = d
    for l in layers:
        p = "l%d_" % l
        kind = l % 3
        if kind == 0:
            shared[p + "wqkv"] = f(inputs[p + "a_wqkv"])
            qn, kn = f(inputs[p + "a_qnorm"]), f(inputs[p + "a_knorm"])
            shared[p + "gain"] = np.concatenate([np.tile(qn, 16), np.tile(kn, 4)]).astype(np.float32)
            shared[p + "wo"] = f(inputs[p + "a_wo"])
        elif kind == 1:
            shared[p + "wo"] = f(inputs[p + "f_wo"])
            shared[p + "bo"] = f(inputs[p + "f_bo"])
        else:
            shared[p + "wqkv"] = f(inputs[p + "c_wqkv"])
            shared[p + "lam"] = np.concatenate([f(inputs[p + "c_lq1"]), f(inputs[p + "c_lk1"]),
                                                f(inputs[p + "c_lq2"]), f(inputs[p + "c_lk2"])]).astype(np.float32)
            shared[p + "subln"] = f(inputs[p + "c_subln"])
            shared[p + "wo"] = f(inputs[p + "c_wo"])
        cw, cb = f(inputs[p + "ffn_conv_w"]), f(inputs[p + "ffn_conv_b"])
        cp = np.zeros((128, NJ, 8), np.float32)
        for hf in range(2):
            sl = slice(hf * DFF, (hf + 1) * DFF)
            cp[:, :, hf * 4 + 0] = cw[0, sl].reshape(NJ, 128).T
            cp[:, :, hf * 4 + 1] = cw[1, sl].reshape(NJ, 128).T
            cp[:, :, hf * 4 + 2] = cw[2, sl].reshape(NJ, 128).T
            cp[:, :, hf * 4 + 3] = cb[sl].reshape(NJ, 128).T
        shared[p + "cp"] = cp
        shared[p + "wup"] = f(inputs[p + "ffn_wup"])
        shared[p + "wdown"] = f(inputs[p + "ffn_wdown"])
        for nm in ("ln1_g", "ln1_b", "ln2_g", "ln2_b"):
            shared[p + nm] = f(inputs[p + nm])
    x = f(inputs["x"])
    maps = []
    for c in range(n_cores):
        m = dict(shared)
        m["x"] = np.ascontiguousarray(x[c])
        maps.append(m)
    return maps


_CACHE = {}


def kernel(**inputs):
    x = np.asarray(inputs["x"])
    B, S, _ = x.shape
    key = (S,)
    if key not in _CACHE:
        _CACHE[key] = build(S)
    nc = _CACHE[key]
    maps = make_in_maps(inputs, S, n_cores=B)
    res = run_bass_kernel_spmd(nc, maps, core_ids=list(range(B)))
    out = np.stack([np.asarray(r["y"], dtype=np.float32) for r in res.results], axis=0)
    return out
```

```python
import math
from contextlib import ExitStack
import numpy as np
import ml_dtypes
import concourse.bass as bass
import concourse.mybir as mybir
from concourse.bass_utils import run_bass_kernel_spmd

F32 = mybir.dt.float32
BF16 = mybir.dt.bfloat16
AF = mybir.ActivationFunctionType
ALU = mybir.AluOpType
AX = mybir.AxisListType

D = 1024
DFF = 2816
NJ = DFF // 128
DEPTH = 4
ALPHA = (2.0 * DEPTH) ** 0.25
LN_EPS = 1e-5
RMS_EPS = 1e-6
ENGINES = ("pe", "act", "dve", "pool", "sp")


class LT:
    __slots__ = ("name", "last_w", "readers", "dsem")

    def __init__(self, name=""):
        self.name = name
        self.last_w = None
        self.readers = []
        self.dsem = None


def lts(n, name=""):
    return [LT(name + str(i)) for i in range(n)]


class DmaSem:
    __slots__ = ("sem", "count")

    def __init__(self, sem):
        self.sem = sem
        self.count = 0


class Op:
    __slots__ = ("eng", "fn", "pos", "needs_inc", "tick", "is_dma", "dsem", "dtarget",
                 "waits_eng", "waits_dma", "clock")

    def __init__(self, eng, fn):
        self.eng = eng
        self.fn = fn
        self.pos = -1
        self.needs_inc = False
        self.tick = 0
        self.is_dma = False
        self.dsem = None
        self.dtarget = 0
        self.waits_eng = {}
        self.waits_dma = {}
        self.clock = None


class Prog:
    def __init__(self, nc):
        self.nc = nc
        self.ops = {e: [] for e in ENGINES}
        self.known = {e: {x: -1 for x in ENGINES} for e in ENGINES}
        self.known_dma = {e: {} for e in ENGINES}
        self.free_dsems = []
        self.all_dsems = []
        self.esem = {}
        self.phase_tiles = []
        self.swq = []

    def setup_sems(self, stack, n_dma):
        for e in ("pe", "act", "dve", "pool"):
            self.esem[e] = stack.enter_context(self.nc.semaphore("es_" + e))
        self.free_dsems = {"pool": [], "sp": []}
        for i in range(n_dma):
            d = DmaSem(stack.enter_context(self.nc.semaphore("ds_%d" % i)))
            self.free_dsems["pool" if i < 30 else "sp"].append(d)
            self.all_dsems.append(d)
        self.dsem_owner = {}

    def _get_dsem(self, tile, eng):
        if tile.dsem is None:
            tile.dsem = self.free_dsems[eng].pop()
            self.dsem_owner[tile.dsem] = eng
            self.phase_tiles.append(tile)
        assert self.dsem_owner[tile.dsem] == eng
        return tile.dsem

    def _add_dep(self, op, dep):
        if dep is None or dep is op:
            return
        e = op.eng
        if dep.is_dma:
            if self.known_dma[e].get(dep.dsem, 0) >= dep.dtarget:
                return
            if op.waits_dma.get(dep.dsem, 0) < dep.dtarget:
                op.waits_dma[dep.dsem] = dep.dtarget
            return
        x = dep.eng
        if x == "pe" and e == "pe":
            return
        if self.known[e][x] >= dep.pos:
            return
        cur = op.waits_eng.get(x)
        if cur is None or cur.pos < dep.pos:
            op.waits_eng[x] = dep

    def add(self, eng, fn, reads=(), writes=(), dma_tile=None):
        op = Op(eng, fn)
        for t in reads:
            self._add_dep(op, t.last_w)
        for t in writes:
            self._add_dep(op, t.last_w)
            for r in t.readers:
                self._add_dep(op, r)
        kn = self.known[eng]
        for x, dep in op.waits_eng.items():
            dep.needs_inc = True
            if kn[x] < dep.pos:
                kn[x] = dep.pos
            if dep.clock is not None:
                for y, p in dep.clock.items():
                    if kn[y] < p:
                        kn[y] = p
        for ds, tgt in op.waits_dma.items():
            self.known_dma[eng][ds] = tgt
        op.pos = len(self.ops[eng])
        self.ops[eng].append(op)
        if dma_tile is not None:
            op.is_dma = True
            op.dsem = self._get_dsem(dma_tile, eng)
            op.dsem.count += 16
            op.dtarget = op.dsem.count
        else:
            op.clock = dict(kn)
            op.clock[eng] = op.pos
        for t in reads:
            t.readers.append(op)
        for t in writes:
            t.last_w = op
            t.readers = []
        return op

    def barrier(self):
        lasts = {}
        for e in ("pe", "act", "dve", "pool"):
            for op in reversed(self.ops[e]):
                if op.fn is not None and not op.is_dma:
                    lasts[e] = op
                    break
        dtargets = {d: d.count for d in self.all_dsems if d.count > 0}
        for e in ENGINES:
            op = Op(e, None)
            for x, dep in lasts.items():
                if x == e:
                    continue
                if self.known[e][x] < dep.pos:
                    op.waits_eng[x] = dep
                    dep.needs_inc = True
                    self.known[e][x] = dep.pos
            for d, tgt in dtargets.items():
                if self.known_dma[e].get(d, 0) < tgt:
                    op.waits_dma[d] = tgt
                    self.known_dma[e][d] = tgt
            op.pos = len(self.ops[e])
            self.ops[e].append(op)
            op.clock = dict(self.known[e])
        for t in self.phase_tiles:
            if t.dsem is not None:
                self.free_dsems[self.dsem_owner[t.dsem]].append(t.dsem)
                t.dsem = None
        self.phase_tiles = []

    def emit(self, block):
        for e in ("pe", "act", "dve", "pool"):
            t = 0
            for op in self.ops[e]:
                if op.needs_inc:
                    t += 1
                    op.tick = t
        esem = self.esem

        def run(e, eng):
            for op in self.ops[e]:
                for x, dep in op.waits_eng.items():
                    eng.wait_ge(esem[x], dep.tick)
                for ds, tgt in op.waits_dma.items():
                    eng.wait_ge(ds.sem, tgt)
                if op.fn is None:
                    continue
                ins = op.fn(eng)
                if op.is_dma:
                    ins.then_inc(op.dsem.sem, 16)
                elif op.needs_inc:
                    ins.then_inc(esem[e], 1)

        @block.tensor
        def _(eng):
            run("pe", eng)

        @block.scalar
        def _(eng):
            run("act", eng)

        @block.vector
        def _(eng):
            run("dve", eng)

        @block.gpsimd
        def _(eng):
            run("pool", eng)

        @block.sync
        def _(eng):
            run("sp", eng)


class K:
    pass


_uid = [0]


def _nm(s):
    _uid[0] += 1
    return "%s_%d" % (s, _uid[0])


def sb(k, st, name, shape, dt):
    return st.enter_context(k.nc.sbuf_tensor(_nm(name), list(shape), dt))


def pst(k, st, name, shape, dt):
    return st.enter_context(k.nc.psum_tensor(_nm(name), list(shape), dt))


def load_cast(k, dst_ap, src_ap, lt):
    P = k.P
    if len(P.swq) >= 8:
        old = P.swq.pop(0)
        w = Op("pool", None)
        if P.known_dma["pool"].get(old.dsem, 0) < old.dtarget:
            w.waits_dma[old.dsem] = old.dtarget
            P.known_dma["pool"][old.dsem] = old.dtarget
        w.pos = len(P.ops["pool"])
        P.ops["pool"].append(w)
    op = P.add("pool", lambda e: e.dma_start(out=dst_ap, in_=src_ap), writes=[lt], dma_tile=lt)
    P.swq.append(op)


def dma(k, dst_ap, src_ap, reads=(), writes=(), tile=None):
    k.P.add("sp", lambda e: e.dma_start(out=dst_ap, in_=src_ap), reads=reads, writes=writes, dma_tile=tile)


class PostNorm:
    NB = 3

    def __init__(self, k, st, g_ap, b_ap, x_src, x_dst, nt):
        self.k = k
        self.nt = nt
        self.x_src = x_src
        self.x_dst = x_dst
        NB = self.NB
        self.G = sb(k, st, "lnG", [128, D], F32)
        self.B = sb(k, st, "lnB", [128, D], F32)
        self.G_lt, self.B_lt = LT(), LT()
        dma(k, self.G[:], g_ap.partition_broadcast(128), writes=[self.G_lt], tile=self.G_lt)
        dma(k, self.B[:], b_ap.partition_broadcast(128), writes=[self.B_lt], tile=self.B_lt)
        self.xin = [sb(k, st, "pn_x", [128, D], F32) for _ in range(NB)]
        self.u = [sb(k, st, "pn_u", [128, D], F32) for _ in range(NB)]
        self.y = [sb(k, st, "pn_y", [128, D], F32) for _ in range(NB)]
        self.ybf = [sb(k, st, "pn_yb", [128, D], BF16) for _ in range(2)]
        self.stt = [sb(k, st, "pn_st", [128, 2, 6], F32) for _ in range(NB)]
        self.mv = [sb(k, st, "pn_mv", [128, 2], F32) for _ in range(NB)]
        self.rstd = [sb(k, st, "pn_rs", [128, 1], F32) for _ in range(NB)]
        self.nmr = [sb(k, st, "pn_nm", [128, 1], F32) for _ in range(NB)]
        self.tp = [pst(k, st, "pn_tp", [128, 8, 128], BF16) for _ in range(2)]
        self.l_xin, self.l_u, self.l_y = lts(NB), lts(NB), lts(NB)
        self.l_st, self.l_mv, self.l_rstd, self.l_nmr = lts(NB), lts(NB), lts(NB), lts(NB)
        self.l_ybf, self.l_tp = lts(2), lts(2)
        self.subs = {}
        self.prefetch(0)
        if nt > 1:
            self.prefetch(1)

    def prefetch(self, t):
        k = self.k
        b = t % self.NB
        dma(k, self.xin[b][:], self.x_src[t * 128:(t + 1) * 128, :], writes=[self.l_xin[b]], tile=self.l_xin[b])

    def pre(self, t):
        if t - 2 >= 0:
            self._s3a(t - 2)

    def step(self, t, sub_ps, sub_lt):
        if t - 2 >= 0:
            self._s3b(t - 2)
        self._s1(t, sub_ps, sub_lt)
        if t - 1 >= 0:
            self._s2(t - 1)
        if t + 2 < self.nt:
            self.prefetch(t + 2)

    def drain(self):
        nt = self.nt
        if nt - 2 >= 0:
            self._s3a(nt - 2)
            self._s3b(nt - 2)
        self._s2(nt - 1)
        self._s3a(nt - 1)
        self._s3b(nt - 1)

    def _s1(self, t, sub_ps, sub_lt):
        k = self.k
        P = k.P
        b = t % self.NB
        xin, u, stt, mv, rstd, nmr = self.xin[b], self.u[b], self.stt[b], self.mv[b], self.rstd[b], self.nmr[b]
        l_xin, l_u, l_st, l_mv, l_rstd, l_nmr = (self.l_xin[b], self.l_u[b], self.l_st[b], self.l_mv[b],
                                                  self.l_rstd[b], self.l_nmr[b])
        P.add("dve", lambda e: e.scalar_tensor_tensor(out=u[:], in0=xin[:], scalar=ALPHA, in1=sub_ps,
                                                      op0=ALU.mult, op1=ALU.add),
              reads=[l_xin, sub_lt], writes=[l_u])
        for c in range(2):
            P.add("dve", lambda e, c=c: e.bn_stats(out=stt[:, c, :], in_=u[:, c * 512:(c + 1) * 512]),
                  reads=[l_u], writes=[l_st])
        P.add("dve", lambda e: e.bn_aggr(out=mv[:], in_=stt[:]), reads=[l_st], writes=[l_mv])
        P.add("pool", lambda e: e.tensor_scalar(out=rstd[:], in0=mv[:, 1:2], scalar1=LN_EPS, scalar2=1.0,
                                                op0=ALU.add, op1=ALU.mult), reads=[l_mv], writes=[l_rstd])
        P.add("pool", lambda e: e.tensor_tensor(out=rstd[:], in0=rstd[:], in1=k.mhalf[:], op=ALU.pow),
              reads=[l_rstd, k.eps_lt], writes=[l_rstd])
        P.add("pool", lambda e: e.tensor_scalar(out=nmr[:], in0=mv[:, 0:1], scalar1=-1.0, scalar2=1.0,
                                                op0=ALU.mult, op1=ALU.mult), reads=[l_mv], writes=[l_nmr])
        P.add("pool", lambda e: e.tensor_tensor(out=nmr[:], in0=nmr[:], in1=rstd[:], op=ALU.mult),
              reads=[l_nmr, l_rstd], writes=[l_nmr])

    def _s2(self, t):
        k = self.k
        P = k.P
        b = t % self.NB
        u, y, rstd, nmr = self.u[b], self.y[b], self.rstd[b], self.nmr[b]
        l_u, l_y, l_rstd, l_nmr = self.l_u[b], self.l_y[b], self.l_rstd[b], self.l_nmr[b]
        P.add("act", lambda e: e.activation(out=u[:], in_=u[:], func=AF.Identity, bias=nmr[:], scale=rstd[:]),
              reads=[l_u, l_nmr, l_rstd], writes=[l_u])
        P.add("dve", lambda e: e.tensor_tensor(out=u[:], in0=u[:], in1=self.G[:], op=ALU.mult),
              reads=[l_u, self.G_lt], writes=[l_u])
        P.add("pool", lambda e: e.tensor_tensor(out=y[:], in0=u[:], in1=self.B[:], op=ALU.add),
              reads=[l_u, self.B_lt], writes=[l_y])
        dma(k, self.x_dst[t * 128:(t + 1) * 128, :], y[:], reads=[l_y], tile=l_y)

    def _s3a(self, t):
        k = self.k
        P = k.P
        if not k.need_xt:
            return
        b = t % self.NB
        y, l_y = self.y[b], self.l_y[b]
        ybf, l_ybf = self.ybf[t % 2], self.l_ybf[t % 2]
        P.add("act", lambda e: e.copy(out=ybf[:], in_=y[:]), reads=[l_y], writes=[l_ybf])

    def _s3b(self, t):
        k = self.k
        P = k.P
        if not k.need_xt:
            return
        ybf, l_ybf = self.ybf[t % 2], self.l_ybf[t % 2]
        tp, l_tp = self.tp[t % 2], self.l_tp[t % 2]
        for c in range(8):
            P.add("pe", lambda e, c=c: e.transpose(out=tp[:, c, :], in_=ybf[:, c * 128:(c + 1) * 128],
                                                   identity=k.ident[:]),
                  reads=[l_ybf, k.ident_lt], writes=[l_tp])
        XT = k.XT
        P.add("dve", lambda e: e.tensor_copy(out=XT[:, :, t * 128:(t + 1) * 128], in_=tp[:]),
              reads=[l_tp], writes=[k.XT_lt[t]])


def phase_prep(k, x_in):
    P = k.P
    S = k.S
    with ExitStack() as st:
        xin = [sb(k, st, "pp_x", [128, D], F32) for _ in range(2)]
        xbf = [sb(k, st, "pp_xb", [128, D], BF16) for _ in range(2)]
        tp = [pst(k, st, "pp_tp", [128, 8, 128], BF16) for _ in range(2)]
        l_x, l_xb, l_tp = lts(2), lts(2), lts(2)
        for t in range(S // 128):
            b = t % 2
            dma(k, xin[b][:], x_in[t * 128:(t + 1) * 128, :], writes=[l_x[b]], tile=l_x[b])
            P.add("act", lambda e, b=b: e.copy(out=xbf[b][:], in_=xin[b][:]), reads=[l_x[b]], writes=[l_xb[b]])
            for c in range(8):
                P.add("pe", lambda e, b=b, c=c: e.transpose(out=tp[b][:, c, :], in_=xbf[b][:, c * 128:(c + 1) * 128],
                                                            identity=k.ident[:]),
                      reads=[l_xb[b], k.ident_lt], writes=[l_tp[b]])
            XT = k.XT
            P.add("dve", lambda e, b=b, t=t: e.tensor_copy(out=XT[:, :, t * 128:(t + 1) * 128], in_=tp[b][:]),
                  reads=[l_tp[b]], writes=[k.XT_lt[t]])
        P.barrier()


def load_wo(k, st, w_ap, bias_ap):
    wo = sb(k, st, "wo", [128, 8, D], BF16)
    wo_lt = lts(8)
    wv = w_ap.rearrange("(c p) n -> p c n", p=128)
    for c in range(8):
        load_cast(k, wo[:, c, :], wv[:, c, :], wo_lt[c])
    bo, bo_lt = None, None
    if bias_ap is not None:
        bo = sb(k, st, "bo", [1, D], BF16)
        bo_lt = LT()
        load_cast(k, bo[:], bias_ap.unsqueeze(0), bo_lt)
    return wo, wo_lt, bo, bo_lt


def phase_proj_postnorm(k, w_ap, bias_ap, g_ap, b_ap, x_src, x_dst, pre=None):
    P = k.P
    S = k.S
    with ExitStack() as st:
        if pre is None:
            pre = load_wo(k, st, w_ap, bias_ap)
        wo, wo_lt, bo, bo_lt = pre
        ob = [sb(k, st, "ob", [128, 8, 512], BF16) for _ in range(2)]
        ob_lt = lts(2)
        sub = [pst(k, st, "sub", [128, D], F32) for _ in range(2)]
        sub_lt = lts(2)
        nt = S // 128
        pn = PostNorm(k, st, g_ap, b_ap, x_src, x_dst, nt)
        otv = k.OT.rearrange("c p s -> p c s")
        def load_ob(blk):
            bb = blk % 2
            dma(k, ob[bb][:], otv[:, :, blk * 512:(blk + 1) * 512], writes=[ob_lt[bb]], tile=ob_lt[bb])

        load_ob(0)
        for t in range(nt):
            blk = t // 4
            bb = blk % 2
            if t % 4 == 0 and (blk + 1) * 512 < S:
                load_ob(blk + 1)
            s_ = t % 2
            pn.pre(t)
            for c in range(8):
                for hf in range(2):
                    P.add("pe", lambda e, c=c, hf=hf, bb=bb, s_=s_, t=t: e.matmul(
                        sub[s_][:, hf * 512:(hf + 1) * 512], ob[bb][:, c, (t % 4) * 128:(t % 4 + 1) * 128],
                        wo[:, c, hf * 512:(hf + 1) * 512], start=(c == 0), stop=(c == 7 and bias_ap is None)),
                        reads=[ob_lt[bb], wo_lt[c]], writes=[sub_lt[s_]])
            if bias_ap is not None:
                for hf in range(2):
                    P.add("pe", lambda e, hf=hf, s_=s_: e.matmul(
                        sub[s_][:, hf * 512:(hf + 1) * 512], k.ones[0:1, :], bo[0:1, hf * 512:(hf + 1) * 512],
                        start=False, stop=True), reads=[bo_lt, k.ones_lt], writes=[sub_lt[s_]])
            pn.step(t, sub[s_][:], sub_lt[s_])
        pn.drain()
        P.barrier()


def phase_ffn_up(k, w_up, cp_ap):
    P = k.P
    S = k.S
    nblk = S // 512
    with ExitStack() as st:
        cp = sb(k, st, "cp", [128, NJ, 8], F32)
        cp_lt = LT()
        dma(k, cp[:], cp_ap, writes=[cp_lt], tile=cp_lt)
        wu = [sb(k, st, "wu", [128, 8, 256], BF16) for _ in range(3)]
        wu_lt = lts(3)
        H = [[sb(k, st, "H", [128, S + 2], F32) for _ in range(2)] for _ in range(2)]
        H_lt = [[lts(nblk) for _ in range(2)] for _ in range(2)]
        pad_lt = LT()
        for hf in range(2):
            for jb in range(2):
                P.add("pool", lambda e, hf=hf, jb=jb: e.memset(H[hf][jb][:, 0:1], 0.0), writes=[pad_lt])
                P.add("pool", lambda e, hf=hf, jb=jb: e.memset(H[hf][jb][:, S + 1:S + 2], 0.0), writes=[pad_lt])
        NTB = 3
        Tg = [sb(k, st, "Tg", [128, 512], F32) for _ in range(NTB)]
        Tv = [sb(k, st, "Tv", [128, 512], F32) for _ in range(NTB)]
        Tg_lt, Tv_lt = lts(NTB), lts(NTB)
        TT = [Tg, Tv]
        TT_lt = [Tg_lt, Tv_lt]
        A = [sb(k, st, "A", [128, S], BF16) for _ in range(2)]
        A_lt = lts(2)
        hp = [[pst(k, st, "hp", [128, 512], F32) for _ in range(2)] for _ in range(2)]
        hp_lt = [lts(2), lts(2)]
        wv = w_up.rearrange("(kc p) f -> p kc f", p=128)
        XT = k.XT

        def load_w(j):
            wb = j % 3
            load_cast(k, wu[wb][:, :, 0:128], wv[:, :, j * 128:(j + 1) * 128], wu_lt[wb])
            load_cast(k, wu[wb][:, :, 128:256], wv[:, :, DFF + j * 128:DFF + (j + 1) * 128], wu_lt[wb])

        load_w(0)
        if NJ > 1:
            load_w(1)
        cnt = 0
        for j in range(NJ):
            if j + 2 < NJ:
                load_w(j + 2)
            wb = j % 3
            jb = j % 2

            def conv(blk, j=j, jb=jb):
                o = blk * 512
                tb = blk % NTB
                rl = [H_lt[0][jb][b2] for b2 in (blk - 1, blk, blk + 1) if 0 <= b2 < nblk] + [pad_lt, cp_lt]
                rv = [H_lt[1][jb][b2] for b2 in (blk - 1, blk, blk + 1) if 0 <= b2 < nblk] + [pad_lt, cp_lt]
                Hg, Hv = H[0][jb], H[1][jb]
                tg, tv = Tg[tb], Tv[tb]
                P.add("dve", lambda e: e.scalar_tensor_tensor(out=tg[:], in0=Hg[:, o:o + 512], scalar=cp[:, j, 0:1],
                                                              in1=tg[:], op0=ALU.mult, op1=ALU.add),
                      reads=rl + [Tg_lt[tb]], writes=[Tg_lt[tb]])
                P.add("dve", lambda e: e.scalar_tensor_tensor(out=tg[:], in0=Hg[:, 2 + o:2 + o + 512],
                                                              scalar=cp[:, j, 2:3], in1=tg[:], op0=ALU.mult,
                                                              op1=ALU.add),
                      reads=rl + [Tg_lt[tb]], writes=[Tg_lt[tb]])
                P.add("dve", lambda e: e.scalar_tensor_tensor(out=tv[:], in0=Hv[:, o:o + 512], scalar=cp[:, j, 4:5],
                                                               in1=tv[:], op0=ALU.mult, op1=ALU.add),
                      reads=rv + [Tv_lt[tb]], writes=[Tv_lt[tb]])
                P.add("dve", lambda e: e.scalar_tensor_tensor(out=tv[:], in0=Hv[:, 2 + o:2 + o + 512],
                                                               scalar=cp[:, j, 6:7], in1=tv[:], op0=ALU.mult,
                                                               op1=ALU.add),
                      reads=rv + [Tv_lt[tb]], writes=[Tv_lt[tb]])
                P.add("act", lambda e: e.activation(out=tg[:], in_=tg[:], func=AF.Silu),
                      reads=[Tg_lt[tb]], writes=[Tg_lt[tb]])
                P.add("pool", lambda e: e.tensor_tensor(out=A[jb][:, o:o + 512], in0=tg[:], in1=tv[:], op=ALU.mult),
                      reads=[Tg_lt[tb], Tv_lt[tb]], writes=[A_lt[jb]])

            for blk in range(nblk):
                for hf in range(2):
                    pb = cnt % 2
                    for kc in range(8):
                        P.add("pe", lambda e, hf=hf, pb=pb, kc=kc, blk=blk, wb=wb: e.matmul(
                            hp[hf][pb][:], wu[wb][:, kc, hf * 128:(hf + 1) * 128],
                            XT[:, kc, blk * 512:(blk + 1) * 512], start=(kc == 0), stop=(kc == 7)),
                            reads=[wu_lt[wb]] + k.XT_lt[blk * 4:blk * 4 + 4], writes=[hp_lt[hf][pb]])
                    P.add("act", lambda e, hf=hf, pb=pb, blk=blk, jb=jb: e.copy(
                        out=H[hf][jb][:, 1 + blk * 512:1 + (blk + 1) * 512], in_=hp[hf][pb][:]),
                        reads=[hp_lt[hf][pb]], writes=[H_lt[hf][jb][blk]])
                    P.add("act", lambda e, hf=hf, pb=pb, blk=blk, j=j: e.activation(
                        out=TT[hf][blk % NTB][:], in_=hp[hf][pb][:], func=AF.Identity,
                        bias=cp[:, j, hf * 4 + 3:hf * 4 + 4], scale=cp[:, j, hf * 4 + 1:hf * 4 + 2]),
                        reads=[hp_lt[hf][pb], cp_lt], writes=[TT_lt[hf][blk % NTB]])
                cnt += 1
                if blk >= 1:
                    conv(blk - 1)
            conv(nblk - 1)
            dma(k, k.AT[j], A[jb][:], reads=[A_lt[jb]], tile=A_lt[jb])
        P.barrier()


def phase_ffn_down(k, w_down, g_ap, b_ap, x_src, x_dst):
    P = k.P
    S = k.S
    with ExitStack() as st:
        wd = sb(k, st, "wd", [128, NJ, D], BF16)
        wd_lt = lts(NJ)
        wv = w_down.rearrange("(c p) n -> p c n", p=128)
        for c in range(NJ):
            load_cast(k, wd[:, c, :], wv[:, c, :], wd_lt[c])
        ab = [sb(k, st, "ab", [128, NJ, 256], BF16) for _ in range(2)]
        ab_lt = lts(2)
        sub = [pst(k, st, "sub", [128, D], F32) for _ in range(2)]
        sub_lt = lts(2)
        nt = S // 128
        pn = PostNorm(k, st, g_ap, b_ap, x_src, x_dst, nt)
        atv = k.AT.rearrange("c p s -> p c s")

        def load_ab(blk):
            bb = blk % 2
            dma(k, ab[bb][:], atv[:, :, blk * 256:(blk + 1) * 256], writes=[ab_lt[bb]], tile=ab_lt[bb])

        load_ab(0)
        for t in range(nt):
            blk = t // 2
            bb = blk % 2
            if t % 2 == 0 and (blk + 1) * 256 < S:
                load_ab(blk + 1)
            s_ = t % 2
            pn.pre(t)
            for c in range(NJ):
                for hf in range(2):
                    P.add("pe", lambda e, c=c, hf=hf, bb=bb, s_=s_, t=t: e.matmul(
                        sub[s_][:, hf * 512:(hf + 1) * 512], ab[bb][:, c, (t % 2) * 128:(t % 2 + 1) * 128],
                        wd[:, c, hf * 512:(hf + 1) * 512], start=(c == 0), stop=(c == NJ - 1)),
                        reads=[ab_lt[bb], wd_lt[c]], writes=[sub_lt[s_]])
            pn.step(t, sub[s_][:], sub_lt[s_])
        pn.drain()
        P.barrier()


def phase_qkv_gqa(k, w_qkv, gain_ap, rope_ap):
    P = k.P
    S = k.S
    with ExitStack() as st:
        wq = sb(k, st, "wq", [128, 8, 1536], BF16)
        wq_lt = lts(8)
        wv = w_qkv.rearrange("(c p) n -> p c n", p=128)
        for c in range(8):
            load_cast(k, wq[:, c, :], wv[:, c, :], wq_lt[c])
        gain = sb(k, st, "gain", [128, 1280], F32)
        gain_lt = LT()
        dma(k, gain[:], gain_ap.partition_broadcast(128), writes=[gain_lt], tile=gain_lt)
        rp = [sb(k, st, "rp", [128, 2, 64], F32) for _ in range(2)]
        rp_lt = lts(2)
        qkv = [pst(k, st, "qkv", [128, 1536], F32) for _ in range(2)]
        qkv_lt = lts(2)
        tq = pst(k, st, "tq", [128, 8, 128], BF16)
        tk = pst(k, st, "tk", [128, 4, 128], BF16)
        tq_lt, tk_lt = LT(), LT()
        sq = [sb(k, st, "sq", [128, 1280], F32) for _ in range(2)]
        qn = [sb(k, st, "qn", [128, 1280], F32) for _ in range(2)]
        t1 = [sb(k, st, "t1", [128, 1280], F32) for _ in range(2)]
        t2 = [sb(k, st, "t2", [128, 1280], F32) for _ in range(2)]
        ss = [sb(k, st, "ss", [128, 20], F32) for _ in range(2)]
        qb = [sb(k, st, "qb", [128, 1024], BF16) for _ in range(2)]
        kd = [sb(k, st, "kd", [128, 4, 128], BF16) for _ in range(2)]
        va = [sb(k, st, "va", [128, 4, 2, 128], BF16) for _ in range(2)]
        l_sq, l_qn, l_t1, l_t2, l_ss, l_qb, l_kd, l_va = (lts(2) for _ in range(8))
        for b in range(2):
            P.add("pool", lambda e, b=b: e.memset(va[b][:], 1.0), writes=[l_va[b]])
        qs = [sb(k, st, "qs", [128, 8, 512], BF16) for _ in range(2)]
        ks = [sb(k, st, "ks", [128, 4, 512], BF16) for _ in range(2)]
        l_qs, l_ks = lts(2), lts(2)
        XT = k.XT
        qtv = k.QT.rearrange("c p s -> p c s")
        ktv = k.KT.rearrange("c p s -> p c s")
        pending = []
        for t in range(S // 128):
            b = t % 2
            blk = t // 4
            sbb = blk % 2
            dma(k, rp[b][:], rope_ap[t * 128:(t + 1) * 128], writes=[rp_lt[b]], tile=rp_lt[b])
            for kc in range(8):
                for n in range(3):
                    P.add("pe", lambda e, kc=kc, n=n, b=b, t=t: e.matmul(
                        qkv[b][:, n * 512:(n + 1) * 512], XT[:, kc, t * 128:(t + 1) * 128],
                        wq[:, kc, n * 512:(n + 1) * 512], start=(kc == 0), stop=(kc == 7)),
                        reads=[k.XT_lt[t], wq_lt[kc]], writes=[qkv_lt[b]])
            if pending:
                pending.pop(0)()
            qk_ps = qkv[b][:, 0:1280]
            P.add("act", lambda e, b=b, qk_ps=qk_ps: e.activation(out=sq[b][:], in_=qk_ps, func=AF.Square),
                  reads=[qkv_lt[b]], writes=[l_sq[b]])
            P.add("dve", lambda e, b=b: e.tensor_reduce(out=ss[b][:], in_=sq[b][:].rearrange("p (h d) -> p h d", d=64),
                                                        axis=AX.X, op=ALU.add),
                  reads=[l_sq[b]], writes=[l_ss[b]])
            P.add("act", lambda e, b=b: e.activation(out=ss[b][:], in_=ss[b][:], func=AF.Sqrt, bias=k.eps_rms[:],
                                                     scale=1.0 / 64), reads=[l_ss[b], k.eps_lt], writes=[l_ss[b]])
            P.add("dve", lambda e, b=b: e.reciprocal(out=ss[b][:], in_=ss[b][:]), reads=[l_ss[b]], writes=[l_ss[b]])
            P.add("dve", lambda e, b=b, qk_ps=qk_ps: e.tensor_tensor(
                out=qn[b][:].rearrange("p (h d) -> p h d", d=64), in0=qk_ps.rearrange("p (h d) -> p h d", d=64),
                in1=ss[b][:].unsqueeze(2).broadcast_to([128, 20, 64]), op=ALU.mult),
                reads=[qkv_lt[b], l_ss[b]], writes=[l_qn[b]])
            P.add("pool", lambda e, b=b: e.tensor_tensor(out=qn[b][:], in0=qn[b][:], in1=gain[:], op=ALU.mult),
                  reads=[l_qn[b], gain_lt], writes=[l_qn[b]])
            P.add("pool", lambda e, b=b: e.tensor_tensor(
                out=t1[b][:].rearrange("p (h d) -> p h d", d=64), in0=qn[b][:].rearrange("p (h d) -> p h d", d=64),
                in1=rp[b][:, 0, :].unsqueeze(1).broadcast_to([128, 20, 64]), op=ALU.mult),
                reads=[l_qn[b], rp_lt[b]], writes=[l_t1[b]])
            for a in range(2):
                P.add("dve", lambda e, b=b, a=a: e.tensor_tensor(
                    out=t2[b][:].rearrange("p (h g a d) -> p h g a d", g=2, a=2, d=16)[:, :, :, a, :],
                    in0=qn[b][:].rearrange("p (h g a d) -> p h g a d", g=2, a=2, d=16)[:, :, :, 1 - a, :],
                    in1=rp[b][:, 1, :].rearrange("p (g a d) -> p g a d", g=2, a=2, d=16)[:, :, a, :].unsqueeze(1)
                    .broadcast_to([128, 20, 2, 16]),
                    op=ALU.mult), reads=[l_qn[b], rp_lt[b]], writes=[l_t2[b]])
            P.add("dve", lambda e, b=b: e.tensor_tensor(out=qb[b][:], in0=t1[b][:, 0:1024], in1=t2[b][:, 0:1024],
                                                        op=ALU.add), reads=[l_t1[b], l_t2[b]], writes=[l_qb[b]])
            for hh in range(2):
                P.add("pool", lambda e, b=b, hh=hh: e.tensor_tensor(
                    out=kd[b][:, :, hh * 64:(hh + 1) * 64], in0=t1[b][:, 1024:1280].rearrange("p (g d) -> p g d", d=64),
                    in1=t2[b][:, 1024:1280].rearrange("p (g d) -> p g d", d=64), op=ALU.add),
                    reads=[l_t1[b], l_t2[b]], writes=[l_kd[b]])
            v_ps = qkv[b][:, 1280:1536].rearrange("p (g d) -> p g d", d=64)
            P.add("act", lambda e, b=b, v_ps=v_ps: e.copy(out=va[b][:, :, 0, 0:64], in_=v_ps),
                  reads=[qkv_lt[b]], writes=[l_va[b]])
            P.add("act", lambda e, b=b, v_ps=v_ps: e.copy(out=va[b][:, :, 1, 64:128], in_=v_ps),
                  reads=[qkv_lt[b]], writes=[l_va[b]])
            dma(k, k.VA[t], va[b][:].rearrange("p g v d -> p (g v d)"), reads=[l_va[b]], tile=l_va[b])
            def tail(t=t, b=b, blk=blk, sbb=sbb):
                for c in range(8):
                    P.add("pe", lambda e, b=b, c=c: e.transpose(out=tq[:, c, :], in_=qb[b][:, c * 128:(c + 1) * 128],
                                                                identity=k.ident[:]),
                          reads=[l_qb[b], k.ident_lt], writes=[tq_lt])
                for g in range(4):
                    P.add("pe", lambda e, b=b, g=g: e.transpose(out=tk[:, g, :], in_=kd[b][:, g, :], identity=k.ident[:]),
                          reads=[l_kd[b], k.ident_lt], writes=[tk_lt])
                o = (t % 4) * 128
                P.add("act", lambda e, sbb=sbb, o=o: e.copy(out=qs[sbb][:, :, o:o + 128], in_=tq[:]),
                      reads=[tq_lt], writes=[l_qs[sbb]])
                P.add("dve", lambda e, sbb=sbb, o=o: e.tensor_copy(out=ks[sbb][:, :, o:o + 128], in_=tk[:]),
                      reads=[tk_lt], writes=[l_ks[sbb]])
                if t % 4 == 3 or t == S // 128 - 1:
                    dma(k, qtv[:, :, blk * 512:(blk + 1) * 512], qs[sbb][:], reads=[l_qs[sbb]], tile=l_qs[sbb])
                    dma(k, ktv[:, 0:4, blk * 512:(blk + 1) * 512], ks[sbb][:], reads=[l_ks[sbb]], tile=l_ks[sbb])
            pending.append(tail)
        pending.pop(0)()
        P.barrier()


def phase_attn_gqa(k):
    P = k.P
    S = k.S
    nkt = S // 128
    nqb = S // 512
    with ExitStack() as st:
        ktd = [sb(k, st, "ktd", [128, S], BF16) for _ in range(2)]
        vag = [sb(k, st, "vag", [128, nkt, 2, 128], BF16) for _ in range(2)]
        qtc = [sb(k, st, "qtc", [128, 2, S], BF16) for _ in range(2)]
        l_ktd, l_vag, l_qtc = lts(2), lts(2), lts(2)
        sps2 = [pst(k, st, "sps", [128, 1024], F32) for _ in range(2)]
        l_sps2 = lts(2)
        acc = [[pst(k, st, "acc", [128, 512], F32) for _ in range(2)] for _ in range(2)]
        l_acc = [lts(2), lts(2)]
        NPB = 3
        pT2 = [sb(k, st, "pT", [128, 1024], BF16) for _ in range(NPB)]
        l_pT2 = lts(NPB)
        rc = [sb(k, st, "rc", [128, 512], F32) for _ in range(2)]
        l_rc = lts(2)
        ost = [sb(k, st, "ost", [128, 512], BF16) for _ in range(2)]
        l_ost = lts(2)
        vav = k.VA.rearrange("t p (g x) -> p t g x", g=4)

        def load_g(g):
            gb = g % 2
            dma(k, ktd[gb][:], k.KT[g], writes=[l_ktd[gb]], tile=l_ktd[gb])
            dma(k, vag[gb][:].rearrange("p t v d -> p t (v d)"), vav[:, :, g, :], writes=[l_vag[gb]], tile=l_vag[gb])
            dma(k, qtc[gb][:], k.QT[2 * g:2 * g + 2].rearrange("c p s -> p c s"), writes=[l_qtc[gb]], tile=l_qtc[gb])

        load_g(0)
        it = 0
        for g in range(4):
            gb = g % 2
            if g + 1 < 4:
                load_g(g + 1)
            for pr in range(2):
                c = 2 * g + pr
                for qb_ in range(nqb):
                    ab = it % 2
                    it += 1

                    def qk(kt, gb=gb, pr=pr, qb_=qb_):
                        s_ = kt % 2
                        for h in range(2):
                            P.add("pe", lambda e, h=h, s_=s_, kt=kt: e.matmul(
                                sps2[s_][:, h * 512:(h + 1) * 512], ktd[gb][h * 64:(h + 1) * 64, kt * 128:(kt + 1) * 128],
                                qtc[gb][h * 64:(h + 1) * 64, pr, qb_ * 512:(qb_ + 1) * 512], start=True, stop=True),
                                reads=[l_ktd[gb], l_qtc[gb]], writes=[l_sps2[s_]])
                        pb = kt % NPB
                        P.add("act", lambda e, s_=s_, pb=pb: e.activation(
                            out=pT2[pb][:], in_=sps2[s_][:], func=AF.Exp, scale=0.125),
                            reads=[l_sps2[s_]], writes=[l_pT2[pb]])

                    def pv(kt, gb=gb, ab=ab):
                        pb = kt % NPB
                        for h in range(2):
                            P.add("pe", lambda e, h=h, pb=pb, kt=kt: e.matmul(
                                acc[h][ab][:], vag[gb][:, kt, h, :], pT2[pb][:, h * 512:(h + 1) * 512], start=(kt == 0),
                                stop=(kt == nkt - 1)),
                                reads=[l_vag[gb], l_pT2[pb]], writes=[l_acc[h][ab]])

                    for kt in range(nkt + 1):
                        if kt < nkt:
                            qk(kt)
                        if kt >= 1:
                            pv(kt - 1)
                    ob_ = ab
                    P.add("dve", lambda e, ab=ab: e.reciprocal(out=rc[ab][0:64, :], in_=acc[0][ab][64:128, :]),
                          reads=[l_acc[0][ab]], writes=[l_rc[ab]])
                    P.add("dve", lambda e, ab=ab: e.reciprocal(out=rc[ab][64:128, :], in_=acc[1][ab][0:64, :]),
                          reads=[l_acc[1][ab]], writes=[l_rc[ab]])
                    P.add("dve", lambda e, ab=ab: e.tensor_tensor(out=ost[ab][0:64, :], in0=acc[0][ab][0:64, :],
                                                                  in1=rc[ab][0:64, :], op=ALU.mult),
                          reads=[l_acc[0][ab], l_rc[ab]], writes=[l_ost[ab]])
                    P.add("dve", lambda e, ab=ab: e.tensor_tensor(out=ost[ab][64:128, :], in0=acc[1][ab][64:128, :],
                                                                  in1=rc[ab][64:128, :], op=ALU.mult),
                          reads=[l_acc[1][ab], l_rc[ab]], writes=[l_ost[ab]])
                    dma(k, k.OT[c][:, qb_ * 512:(qb_ + 1) * 512], ost[ab][:], reads=[l_ost[ab]], tile=l_ost[ab])
        P.barrier()


def phase_qkv_diff(k, w_qkv, rope_ap):
    P = k.P
    S = k.S
    with ExitStack() as st:
        wq = sb(k, st, "wq", [128, 8, 3072], BF16)
        wq_lt = lts(8)
        wv = w_qkv.rearrange("(c p) n -> p c n", p=128)
        for c in range(8):
            for q3 in range(2):
                load_cast(k, wq[:, c, q3 * 1536:(q3 + 1) * 1536], wv[:, c, q3 * 1536:(q3 + 1) * 1536], wq_lt[c])
        rp = [sb(k, st, "rp", [128, 2, 128], F32) for _ in range(2)]
        rp_lt = lts(2)
        ps_ = [pst(k, st, "qps", [128, 1024], F32) for _ in range(2)]
        ps_lt = lts(2)
        tq = [pst(k, st, "tq", [128, 8, 128], BF16) for _ in range(2)]
        tq_lt = lts(2)
        xb = [sb(k, st, "xb", [128, 1024], BF16) for _ in range(2)]
        l_xb = lts(2)
        xf = [sb(k, st, "xf", [128, 1024], F32) for _ in range(2)]
        l_xf = lts(2)
        tmp = [[sb(k, st, "tmp", [128, 16, 8], F32) for _ in range(4)] for _ in range(2)]
        l_tmp = [lts(4), lts(4)]
        stg = [[sb(k, st, "stg", [128, 8, 512], BF16) for _ in range(2)] for _ in range(2)]
        l_stg = [lts(2), lts(2)]
        XT = k.XT
        dstv = [k.QT.rearrange("c p s -> p c s"), k.KT.rearrange("c p s -> p c s")]
        cnt = 0
        pending = []
        for t in range(S // 128):
            rb = t % 2
            blk = t // 4
            sbb = blk % 2
            dma(k, rp[rb][:], rope_ap[t * 128:(t + 1) * 128], writes=[rp_lt[rb]], tile=rp_lt[rb])
            for part in range(3):
                b = cnt % 2
                cnt += 1
                for kc in range(8):
                    for n in range(2):
                        P.add("pe", lambda e, kc=kc, n=n, b=b, t=t, part=part: e.matmul(
                            ps_[b][:, n * 512:(n + 1) * 512], XT[:, kc, t * 128:(t + 1) * 128],
                            wq[:, kc, part * 1024 + n * 512:part * 1024 + (n + 1) * 512], start=(kc == 0),
                            stop=(kc == 7)), reads=[k.XT_lt[t], wq_lt[kc]], writes=[ps_lt[b]])
                if pending:
                    pending.pop(0)()
                if part == 2:
                    P.add("act", lambda e, b=b: e.copy(out=xb[b][:], in_=ps_[b][:]), reads=[ps_lt[b]], writes=[l_xb[b]])
                    dma(k, k.VA[t], xb[b][:], reads=[l_xb[b]], tile=l_xb[b])
                    continue
                P.add("act", lambda e, b=b: e.copy(out=xf[b][:], in_=ps_[b][:]), reads=[ps_lt[b]], writes=[l_xf[b]])
                xfv = xf[b][:].rearrange("p (m d) -> p m d", d=64)
                x1, x2 = xfv[:, :, 0:8], xfv[:, :, 8:16]
                cb = rp[rb][:, 0, :].rearrange("p (m d) -> p m d", d=8)
                sn = rp[rb][:, 1, :].rearrange("p (m d) -> p m d", d=8)
                tm = tmp[b]
                lt_ = l_tmp[b]
                for i_, (xx, tb_) in enumerate(((x1, cb), (x2, sn), (x2, cb), (x1, sn))):
                    P.add("dve", lambda e, xx=xx, tb_=tb_, i_=i_, tm=tm: e.tensor_tensor(
                        out=tm[i_][:], in0=xx, in1=tb_, op=ALU.mult),
                        reads=[l_xf[b], rp_lt[rb]], writes=[lt_[i_]])
                P.add("dve", lambda e, tm=tm, x1=x1: e.tensor_tensor(out=x1, in0=tm[0][:], in1=tm[1][:], op=ALU.subtract),
                      reads=[lt_[0], lt_[1], l_xf[b]], writes=[l_xf[b]])
                P.add("dve", lambda e, tm=tm, x2=x2: e.tensor_tensor(out=x2, in0=tm[2][:], in1=tm[3][:], op=ALU.add),
                      reads=[lt_[2], lt_[3], l_xf[b]], writes=[l_xf[b]])
                P.add("act", lambda e, b=b: e.copy(out=xb[b][:], in_=xf[b][:]), reads=[l_xf[b]], writes=[l_xb[b]])
                def tail(t=t, b=b, part=part, blk=blk, sbb=sbb):
                    for c in range(8):
                        P.add("pe", lambda e, b=b, c=c, part=part: e.transpose(
                            out=tq[part][:, c, :], in_=xb[b][:, c * 128:(c + 1) * 128], identity=k.ident[:]),
                            reads=[l_xb[b], k.ident_lt], writes=[tq_lt[part]])
                    o = (t % 4) * 128
                    eng = "act" if part == 0 else "dve"
                    if eng == "act":
                        P.add("act", lambda e, part=part, sbb=sbb, o=o: e.copy(out=stg[part][sbb][:, :, o:o + 128],
                                                                              in_=tq[part][:]),
                              reads=[tq_lt[part]], writes=[l_stg[part][sbb]])
                    else:
                        P.add("dve", lambda e, part=part, sbb=sbb, o=o: e.tensor_copy(out=stg[part][sbb][:, :, o:o + 128],
                                                                                     in_=tq[part][:]),
                              reads=[tq_lt[part]], writes=[l_stg[part][sbb]])
                    if t % 4 == 3 or t == S // 128 - 1:
                        dma(k, dstv[part][:, :, blk * 512:(blk + 1) * 512], stg[part][sbb][:],
                            reads=[l_stg[part][sbb]], tile=l_stg[part][sbb])
                pending.append(tail)
        while pending:
            pending.pop(0)()
        P.barrier()


def phase_attn_diff(k, lam_ap, subln_ap, lambda_init):
    P = k.P
    S = k.S
    nkt = S // 128
    nqb = S // 512
    with ExitStack() as st:
        lv = sb(k, st, "lv", [128, 4, 64], F32)
        lv_lt = LT()
        dma(k, lv[:].rearrange("p a d -> p (a d)"), lam_ap.partition_broadcast(128), writes=[lv_lt], tile=lv_lt)
        pr_ = sb(k, st, "lpr", [128, 2, 64], F32)
        ls = sb(k, st, "ls", [128, 2], F32)
        neglam = sb(k, st, "neglam", [128, 1], F32)
        l_pr, l_ls, l_nl = LT(), LT(), LT()
        P.add("dve", lambda e: e.tensor_tensor(out=pr_[:], in0=lv[:, 0:4:2, :], in1=lv[:, 1:4:2, :], op=ALU.mult),
              reads=[lv_lt], writes=[l_pr])
        P.add("dve", lambda e: e.tensor_reduce(out=ls[:], in_=pr_[:], axis=AX.X, op=ALU.add), reads=[l_pr],
              writes=[l_ls])
        P.add("act", lambda e: e.activation(out=ls[:], in_=ls[:], func=AF.Exp), reads=[l_ls], writes=[l_ls])
        P.add("dve", lambda e: e.scalar_tensor_tensor(out=neglam[:], in0=ls[:, 1:2], scalar=-lambda_init,
                                                      in1=ls[:, 0:1], op0=ALU.add, op1=ALU.subtract),
              reads=[l_ls], writes=[l_nl])
        sl = sb(k, st, "subln", [128, 1], F32)
        sl_lt = LT()
        dma(k, sl[:], subln_ap.unsqueeze(1), writes=[sl_lt], tile=sl_lt)
        P.add("dve", lambda e: e.tensor_single_scalar(out=sl[:], in_=sl[:], scalar=1.0 - lambda_init, op=ALU.mult),
              reads=[sl_lt], writes=[sl_lt])
        kth = [sb(k, st, "kth", [128, S], BF16) for _ in range(2)]
        vh = [sb(k, st, "vh", [128, nkt, 128], BF16) for _ in range(2)]
        qth = [sb(k, st, "qth", [128, S], BF16) for _ in range(2)]
        l_kth, l_vh, l_qth = lts(2), lts(2), lts(2)
        NSB = 4
        spsb = [pst(k, st, "sps", [128, 512], F32) for _ in range(NSB)]
        l_spsb = lts(NSB)
        scnt = [0]
        acc = [pst(k, st, "acc", [128, 512], F32) for _ in range(2)]
        rs_ = [pst(k, st, "rsum", [128, 512], F32) for _ in range(2)]
        l_acc, l_rs = lts(2), lts(2)
        NPB = 8
        pT = [[sb(k, st, "pT", [128, 512], BF16) for _ in range(NPB)] for _ in range(2)]
        l_pT = [lts(NPB), lts(NPB)]
        pp = [[sb(k, st, "pp", [128, 512], BF16) for _ in range(2)] for _ in range(2)]
        l_pp = [lts(2), lts(2)]
        qq = [[sb(k, st, "qq", [128, 512], BF16) for _ in range(2)] for _ in range(2)]
        l_qq = [lts(2), lts(2)]
        accS = [[sb(k, st, "accS", [128, 512], F32) for _ in range(2)] for _ in range(2)]
        rsS = [[sb(k, st, "rsS", [128, 512], F32) for _ in range(2)] for _ in range(2)]
        l_accS, l_rsS = [lts(2), lts(2)], [lts(2), lts(2)]
        o_ = [sb(k, st, "o", [128, 512], F32) for _ in range(2)]
        sqb = [sb(k, st, "sqb", [128, 512], BF16) for _ in range(2)]
        ssS = [sb(k, st, "ssS", [128, 512], F32) for _ in range(2)]
        l_o, l_sqb, l_ssS = lts(2), lts(2), lts(2)
        ost = [sb(k, st, "ost", [128, 512], BF16) for _ in range(2)]
        l_ost = lts(2)
        vav = k.VA.rearrange("t p (h d) -> p t h d", h=8)

        def load_h(h):
            hb = h % 2
            dma(k, kth[hb][:], k.KT[h], writes=[l_kth[hb]], tile=l_kth[hb])
            dma(k, vh[hb][:], vav[:, :, h, :], writes=[l_vh[hb]], tile=l_vh[hb])
            dma(k, qth[hb][:], k.QT[h], writes=[l_qth[hb]], tile=l_qth[hb])

        load_h(0)
        it = 0
        deferred = []

        def run_deferred(kt):
            while deferred and deferred[0][0] <= kt:
                deferred.pop(0)[1]()

        for h in range(8):
            hb = h % 2
            if h + 1 < 8:
                load_h(h + 1)
            for qb_ in range(nqb):
                par = it % 2
                it += 1

                def qk(kt, hb=hb, qb_=qb_):
                    sb_ = [(scnt[0] + m) % NSB for m in range(2)]
                    scnt[0] += 2
                    for m in range(2):
                        P.add("pe", lambda e, m=m, kt=kt, bk=sb_[m]: e.matmul(
                            spsb[bk][:], kth[hb][m * 64:(m + 1) * 64, kt * 128:(kt + 1) * 128],
                            qth[hb][m * 64:(m + 1) * 64, qb_ * 512:(qb_ + 1) * 512], start=True, stop=True),
                            reads=[l_kth[hb], l_qth[hb]], writes=[l_spsb[sb_[m]]])
                    for m in range(2):
                        pb = kt % NPB
                        P.add("act", lambda e, m=m, pb=pb, bk=sb_[m]: e.activation(
                            out=pT[m][pb][:], in_=spsb[bk][:], func=AF.Exp, scale=0.125),
                            reads=[l_spsb[sb_[m]]], writes=[l_pT[m][pb]])
                    if kt % 2 == 1:
                        pq = (kt // 2) % 2
                        for m in range(2):
                            P.add("dve", lambda e, m=m, pq=pq, kt=kt: e.tensor_tensor(
                                out=pp[m][pq][:], in0=pT[m][(kt - 1) % NPB][:], in1=pT[m][kt % NPB][:], op=ALU.add),
                                reads=[l_pT[m][(kt - 1) % NPB], l_pT[m][kt % NPB]], writes=[l_pp[m][pq]])
                        if kt % 4 == 3:
                            qi = (kt // 4) % 2
                            for m in range(2):
                                P.add("dve", lambda e, m=m, qi=qi: e.tensor_tensor(
                                    out=qq[m][qi][:], in0=pp[m][0][:], in1=pp[m][1][:], op=ALU.add),
                                    reads=[l_pp[m][0], l_pp[m][1]], writes=[l_qq[m][qi]])

                def pv(kt, hb=hb):
                    pb = kt % NPB
                    for m in range(2):
                        P.add("pe", lambda e, m=m, pb=pb, kt=kt: e.matmul(
                            acc[m][:], vh[hb][:, kt, :], pT[m][pb][:], start=(kt == 0), stop=(kt == nkt - 1)),
                            reads=[l_vh[hb], l_pT[m][pb]], writes=[l_acc[m]])
                    if kt % 4 == 3:
                        qi = (kt // 4) % 2

                        def ones_mm(qi=qi, kt=kt):
                            for m in range(2):
                                P.add("pe", lambda e, m=m: e.matmul(
                                    rs_[m][:], k.ones[:], qq[m][qi][:], start=(kt == 3), stop=(kt == nkt - 1)),
                                    reads=[k.ones_lt, l_qq[m][qi]], writes=[l_rs[m]])
                        pend_ones.append((kt + 3, ones_mm))

                pend_ones = []
                for kt in range(nkt + 1):
                    if kt < nkt:
                        qk(kt)
                    if kt >= 1:
                        pv(kt - 1)
                    while pend_ones and pend_ones[0][0] <= kt - 1:
                        pend_ones.pop(0)[1]()
                    run_deferred(kt)
                while pend_ones:
                    pend_ones.pop(0)[1]()
                run_deferred(10 ** 9)
                for m in range(2):
                    P.add("dve", lambda e, m=m, par=par: e.tensor_copy(out=accS[par][m][:], in_=acc[m][:]),
                          reads=[l_acc[m]], writes=[l_accS[par][m]])
                    P.add("dve", lambda e, m=m, par=par: e.tensor_copy(out=rsS[par][m][:], in_=rs_[m][:]),
                          reads=[l_rs[m]], writes=[l_rsS[par][m]])
                def d_r0(par=par):
                    P.add("dve", lambda e: e.reciprocal(out=rsS[par][0][:], in_=rsS[par][0][:]),
                          reads=[l_rsS[par][0]], writes=[l_rsS[par][0]])

                def d_r1(par=par):
                    P.add("dve", lambda e: e.reciprocal(out=rsS[par][1][:], in_=rsS[par][1][:]),
                          reads=[l_rsS[par][1]], writes=[l_rsS[par][1]])
                    P.add("pool", lambda e: e.tensor_tensor(out=accS[par][0][:], in0=accS[par][0][:],
                                                            in1=rsS[par][0][:], op=ALU.mult),
                          reads=[l_accS[par][0], l_rsS[par][0]], writes=[l_accS[par][0]])

                def d_m1(par=par):
                    P.add("pool", lambda e: e.tensor_tensor(out=accS[par][1][:], in0=accS[par][1][:],
                                                            in1=rsS[par][1][:], op=ALU.mult),
                          reads=[l_accS[par][1], l_rsS[par][1]], writes=[l_accS[par][1]])

                def d_o(par=par):
                    P.add("dve", lambda e: e.scalar_tensor_tensor(out=o_[par][:], in0=accS[par][1][:],
                                                                  scalar=neglam[:], in1=accS[par][0][:],
                                                                  op0=ALU.mult, op1=ALU.add),
                          reads=[l_accS[par][0], l_accS[par][1], l_nl], writes=[l_o[par]])
                    P.add("pool", lambda e: e.tensor_tensor(out=sqb[par][:], in0=o_[par][:], in1=o_[par][:],
                                                            op=ALU.mult), reads=[l_o[par]], writes=[l_sqb[par]])

                def d_ss(par=par):
                    bk = scnt[0] % NSB
                    scnt[0] += 1
                    P.add("pe", lambda e: e.matmul(spsb[bk][:], k.ones[:], sqb[par][:], start=True, stop=True),
                          reads=[k.ones_lt, l_sqb[par]], writes=[l_spsb[bk]])
                    P.add("act", lambda e: e.activation(out=ssS[par][:], in_=spsb[bk][:], func=AF.Ln, bias=k.eps_rms[:],
                                                        scale=1.0 / 128), reads=[l_spsb[bk], k.eps_lt],
                          writes=[l_ssS[par]])
                    P.add("act", lambda e: e.activation(out=ssS[par][:], in_=ssS[par][:], func=AF.Exp, scale=-0.5),
                          reads=[l_ssS[par]], writes=[l_ssS[par]])

                def d_on(par=par):
                    P.add("pool", lambda e: e.tensor_tensor(out=o_[par][:], in0=o_[par][:], in1=ssS[par][:],
                                                            op=ALU.mult), reads=[l_o[par], l_ssS[par]],
                          writes=[l_o[par]])

                def d_out(par=par, h=h, qb_=qb_):
                    P.add("act", lambda e: e.activation(out=ost[par][:], in_=o_[par][:], func=AF.Identity, scale=sl[:]),
                          reads=[l_o[par], sl_lt], writes=[l_ost[par]])
                    dma(k, k.OT[h][:, qb_ * 512:(qb_ + 1) * 512], ost[par][:], reads=[l_ost[par]], tile=l_ost[par])

                for trig, fn in ((1, d_r0), (4, d_r1), (7, d_m1), (9, d_o), (16, d_ss), (21, d_on), (24, d_out)):
                    deferred.append((min(trig, nkt - 1), fn))
        run_deferred(10 ** 9)
        P.barrier()


def phase_fnet(k, cc_ap, dft_ap):
    P = k.P
    S = k.S
    nt = S // 128
    nkb = S // 512
    with ExitStack() as st:
        ucs = sb(k, st, "ucs", [128, nt, 4, 512], BF16)
        ucs_lt = lts(nt)
        with ExitStack() as st1:
            ccs = sb(k, st1, "ccs", [128, 2, 512], BF16)
            ccs_lt = LT()
            load_cast(k, ccs[:], cc_ap.rearrange("(c p) n -> p c n", p=128), ccs_lt)
            p1 = [pst(k, st1, "p1", [128, 512], F32) for _ in range(4)]
            p1_lt = lts(4)
            XT = k.XT
            cnt = 0
            for t in range(nt):
                for gr in range(4):
                    b = cnt % 4
                    cnt += 1
                    for kc in range(2):
                        P.add("pe", lambda e, b=b, kc=kc, gr=gr, t=t: e.matmul(
                            p1[b][:], XT[:, gr * 2 + kc, t * 128:(t + 1) * 128], ccs[:, kc, :], start=(kc == 0),
                            stop=(kc == 1)), reads=[k.XT_lt[t], ccs_lt], writes=[p1_lt[b]])
                    if gr % 2 == 0:
                        P.add("act", lambda e, b=b, gr=gr, t=t: e.copy(out=ucs[:, t, gr, :], in_=p1[b][:]),
                              reads=[p1_lt[b]], writes=[ucs_lt[t]])
                    else:
                        P.add("dve", lambda e, b=b, gr=gr, t=t: e.tensor_copy(out=ucs[:, t, gr, :], in_=p1[b][:]),
                              reads=[p1_lt[b]], writes=[ucs_lt[t]])
            P.barrier()
        nq = 4 if nt >= 4 else 1
        jq = nt // nq
        dq = [k.XT_flat[:, q * (jq * 1024):(q + 1) * (jq * 1024)].rearrange("p (j c n) -> p j c n", c=2, n=512)
              for q in range(nq)]
        dq_lt = lts(nq)
        p2 = [pst(k, st, "p2", [128, 512], F32) for _ in range(2)]
        p2_lt = lts(2)
        fst = [sb(k, st, "fst", [128, 512], BF16) for _ in range(2)]
        fst_lt = lts(2)

        def load_d(kb, q):
            dma(k, dq[q], dft_ap[kb, :, q * jq:(q + 1) * jq], writes=[dq_lt[q]], tile=dq_lt[q])

        for q in range(nq):
            load_d(0, q)
        cnt = 0
        for kb in range(nkb):
            for fc in range(8):
                gr, hh = fc // 2, fc % 2
                b = cnt % 2
                cnt += 1
                for jt in range(nt):
                    q, jj = jt // jq, jt % jq
                    for cs in range(2):
                        P.add("pe", lambda e, b=b, jt=jt, q=q, jj=jj, cs=cs, gr=gr, hh=hh: e.matmul(
                            p2[b][:], ucs[:, jt, gr, cs * 256 + hh * 128:cs * 256 + (hh + 1) * 128], dq[q][:, jj, cs, :],
                            start=(jt == 0 and cs == 0), stop=(jt == nt - 1 and cs == 1)),
                            reads=[ucs_lt[jt], dq_lt[q]], writes=[p2_lt[b]])
                    if fc == 7 and jj == jq - 1 and kb + 1 < nkb:
                        load_d(kb + 1, q)
                P.add("act", lambda e, b=b: e.activation(out=fst[b][:], in_=p2[b][:], func=AF.Identity, scale=1.0 / math.sqrt(S * 256.0)),
                      reads=[p2_lt[b]], writes=[fst_lt[b]])
                dma(k, k.OT[fc][:, kb * 512:(kb + 1) * 512], fst[b][:], reads=[fst_lt[b]], tile=fst_lt[b])
        P.barrier()


def lambda_init_of(layer_idx):
    return 0.8 - 0.6 * math.exp(-0.3 * layer_idx)


def build(S, layers=(0, 1, 2, 3)):
    nc = bass.Bass("TRN2", target_bir_lowering=False)
    k = K()
    k.nc = nc
    k.S = S
    nt = S // 128

    def din(name, shape, dt=F32):
        return nc.dram_tensor(name, list(shape), dt, kind="ExternalInput").ap()

    x_in = din("x", [S, D])
    ident_in = din("ident", [128, 128])
    W = {}
    for l in layers:
        p = "l%d_" % l
        kind = l % 3
        if kind == 0:
            W[p + "wqkv"] = din(p + "wqkv", [D, 1536])
            W[p + "gain"] = din(p + "gain", [1280])
            W[p + "wo"] = din(p + "wo", [D, D])
        elif kind == 1:
            W[p + "wo"] = din(p + "wo", [D, D])
            W[p + "bo"] = din(p + "bo", [D])
        else:
            W[p + "wqkv"] = din(p + "wqkv", [D, 3072])
            W[p + "lam"] = din(p + "lam", [256])
            W[p + "subln"] = din(p + "subln", [128])
            W[p + "wo"] = din(p + "wo", [D, D])
        for nm, shp in (("ln1_g", [D]), ("ln1_b", [D]), ("wup", [D, 2 * DFF]), ("cp", [128, NJ, 8]),
                        ("wdown", [DFF, D]), ("ln2_g", [D]), ("ln2_b", [D])):
            W[p + nm] = din(p + nm, shp)
    kinds = set(l % 3 for l in layers)
    if 0 in kinds:
        rope_a = din("rope_a", [S, 2, 64])
    if 2 in kinds:
        rope_c = din("rope_c", [S, 2, 128])
    if 1 in kinds:
        cc_in = din("cc", [256, 512])
        dft_in = din("dft", [S // 512, 128, nt, 2, 512], BF16)
    y_out = nc.dram_tensor("y", [S, D], F32, kind="ExternalOutput").ap()

    def scr(name, shape, dt):
        return nc.dram_tensor(name, list(shape), dt, kind="Internal").ap()

    k.X32 = scr("x32s", [S, D], F32)
    k.AT = scr("at", [NJ, 128, S], BF16)
    k.QT = scr("qt", [8, 128, S], BF16)
    k.KT = scr("kt", [8, 128, S], BF16)
    k.VA = scr("va", [nt, 128, 1024], BF16)
    k.OT = scr("ot", [8, 128, S], BF16)

    with ExitStack() as gst:
        P = Prog(nc)
        k.P = P
        P.setup_sems(gst, 80)
        k.XT = gst.enter_context(nc.sbuf_tensor("XT_sb", [128, 8, S], BF16))
        k.XT_flat = k.XT[:].rearrange("p c s -> p (c s)")
        k.XT_lt = lts(nt, "XT")
        k.ident = gst.enter_context(nc.sbuf_tensor("ident_sb", [128, 128], BF16))
        k.ident_lt = LT()
        load_cast(k, k.ident[:], ident_in, k.ident_lt)
        k.ones = gst.enter_context(nc.sbuf_tensor("ones_sb", [128, 128], BF16))
        k.ones_lt = LT()
        P.add("pool", lambda e: e.memset(k.ones[:], 1.0), writes=[k.ones_lt])
        k.eps_ln = gst.enter_context(nc.sbuf_tensor("eps_ln", [128, 1], F32))
        k.eps_rms = gst.enter_context(nc.sbuf_tensor("eps_rms", [128, 1], F32))
        k.eps_lt = LT()
        P.add("pool", lambda e: e.memset(k.eps_ln[:], LN_EPS), writes=[k.eps_lt])
        P.add("pool", lambda e: e.memset(k.eps_rms[:], RMS_EPS), writes=[k.eps_lt])
        k.mhalf = gst.enter_context(nc.sbuf_tensor("mhalf", [128, 1], F32))
        P.add("pool", lambda e: e.memset(k.mhalf[:], -0.5), writes=[k.eps_lt])
        k.need_xt = True
        phase_prep(k, x_in)
        x_cur = x_in
        for li, l in enumerate(layers):
            p = "l%d_" % l
            kind = l % 3
            last = (li == len(layers) - 1)
            with ExitStack() as wst:
                pre = None
                if kind == 0:
                    phase_qkv_gqa(k, W[p + "wqkv"], W[p + "gain"], rope_a)
                    pre = load_wo(k, wst, W[p + "wo"], None)
                    phase_attn_gqa(k)
                    bias = None
                elif kind == 1:
                    phase_fnet(k, cc_in, dft_in)
                    bias = W[p + "bo"]
                else:
                    phase_qkv_diff(k, W[p + "wqkv"], rope_c)
                    pre = load_wo(k, wst, W[p + "wo"], None)
                    phase_attn_diff(k, W[p + "lam"], W[p + "subln"], lambda_init_of(l))
                    bias = None
                phase_proj_postnorm(k, W[p + "wo"], bias, W[p + "ln1_g"], W[p + "ln1_b"], x_cur, k.X32, pre=pre)
            x_cur = k.X32
            phase_ffn_up(k, W[p + "wup"], W[p + "cp"])
            k.need_xt = not last
            phase_ffn_down(k, W[p + "wdown"], W[p + "ln2_g"], W[p + "ln2_b"], x_cur, y_out if last else k.X32)
        with nc.Block() as block:
            P.emit(block)
    return nc


def _rope_tables(S):
    def cs(pos, dim, theta):
        inv = theta ** (-np.arange(0, dim, 2, dtype=np.float32) / np.float32(dim))
        ang = pos.astype(np.float32)[:, None] * inv[None, :].astype(np.float32)
        return np.cos(ang).astype(np.float32), np.sin(ang).astype(np.float32)
    rows = S // 64
    t_row = np.repeat(np.arange(rows), 64)
    t_col = np.tile(np.arange(64), rows)
    cr, sr = cs(t_row, 32, 10000.0)
    cc, sc = cs(t_col, 32, 10000.0)
    C = np.concatenate([cr, cr, cc, cc], axis=1)
    Sg = np.concatenate([-sr, sr, -sc, sc], axis=1)
    rope_a = np.stack([C, Sg], axis=1).astype(np.float32)
    c8, s8 = cs(np.arange(S), 16, 500000.0)
    rope_c = np.stack([np.tile(c8, (1, 16)), np.tile(s8, (1, 16))], axis=1).astype(np.float32)
    return rope_a, rope_c


def _dft_tables(S):
    n = np.arange(256)
    ang = 2.0 * np.pi * ((n[:, None] * n[None, :]) % 256) / 256.0
    cc = np.concatenate([np.cos(ang), -np.sin(ang)], axis=1).astype(np.float32)
    j = np.arange(S, dtype=np.int64)
    m = (j[:, None] * j[None, :]) % S
    ang = (2.0 * np.pi / S) * m
    Cs = np.cos(ang).astype(ml_dtypes.bfloat16)
    Ss = np.sin(ang).astype(ml_dtypes.bfloat16)
    nt = S // 128
    d = np.stack([Cs, Ss], axis=0).reshape(2, nt, 128, S // 512, 512)
    d = np.ascontiguousarray(d.transpose(3, 2, 1, 0, 4))
    return cc, d


def make_in_maps(inputs, S, layers=(0, 1, 2, 3), n_cores=8):
    f = lambda a: np.ascontiguousarray(np.asarray(a, dtype=np.float32))
    shared = {"ident": np.eye(128, dtype=np.float32)}
    kinds = set(l % 3 for l in layers)
    rope_a, rope_c = _rope_tables(S)
    if 0 in kinds:
        shared["rope_a"] = rope_a
    if 2 in kinds:
        shared["rope_c"] = rope_c
    if 1 in kinds:
        cc, d = _dft_tables(S)
        shared["cc"] = cc
        shared["dft"] = d
    for l in layers:
        p = "l%d_" % l
        kind = l % 3
        if kind == 0:
            shared[p + "wqkv"] = f(inputs[p + "a_wqkv"])
            qn, kn = f(inputs[p + "a_qnorm"]), f(inputs[p + "a_knorm"])
            shared[p + "gain"] = np.concatenate([np.tile(qn, 16), np.tile(kn, 4)]).astype(np.float32)
            shared[p + "wo"] = f(inputs[p + "a_wo"])
        elif kind == 1:
            shared[p + "wo"] = f(inputs[p + "f_wo"])
            shared[p + "bo"] = f(inputs[p + "f_bo"])
        else:
            shared[p + "wqkv"] = f(inputs[p + "c_wqkv"])
            shared[p + "lam"] = np.concatenate([f(inputs[p + "c_lq1"]), f(inputs[p + "c_lk1"]),
                                                f(inputs[p + "c_lq2"]), f(inputs[p + "c_lk2"])]).astype(np.float32)
            shared[p + "subln"] = f(inputs[p + "c_subln"])
            shared[p + "wo"] = f(inputs[p + "c_wo"])
        cw, cb = f(inputs[p + "ffn_conv_w"]), f(inputs[p + "ffn_conv_b"])
        cp = np.zeros((128, NJ, 8), np.float32)
        for hf in range(2):
            sl = slice(hf * DFF, (hf + 1) * DFF)
            cp[:, :, hf * 4 + 0] = cw[0, sl].reshape(NJ, 128).T
            cp[:, :, hf * 4 + 1] = cw[1, sl].reshape(NJ, 128).T
            cp[:, :, hf * 4 + 2] = cw[2, sl].reshape(NJ, 128).T
            cp[:, :, hf * 4 + 3] = cb[sl].reshape(NJ, 128).T
        shared[p + "cp"] = cp
        shared[p + "wup"] = f(inputs[p + "ffn_wup"])
        shared[p + "wdown"] = f(inputs[p + "ffn_wdown"])
        for nm in ("ln1_g", "ln1_b", "ln2_g", "ln2_b"):
            shared[p + nm] = f(inputs[p + nm])
    x = f(inputs["x"])
    maps = []
    for c in range(n_cores):
        m = dict(shared)
        m["x"] = np.ascontiguousarray(x[c])
        maps.append(m)
    return maps


_CACHE = {}


def kernel(**inputs):
    x = np.asarray(inputs["x"])
    B, S, _ = x.shape
    key = (S,)
    if key not in _CACHE:
        _CACHE[key] = build(S)
    nc = _CACHE[key]
    maps = make_in_maps(inputs, S, n_cores=B)
    res = run_bass_kernel_spmd(nc, maps, core_ids=list(range(B)))
    out = np.stack([np.asarray(r["y"], dtype=np.float32) for r in res.results], axis=0)
    return out
```

```python
import math
from contextlib import ExitStack
import numpy as np
import ml_dtypes
import concourse.bass as bass
import concourse.mybir as mybir
from concourse.bass_utils import run_bass_kernel_spmd

F32 = mybir.dt.float32
BF16 = mybir.dt.bfloat16
AF = mybir.ActivationFunctionType
ALU = mybir.AluOpType
AX = mybir.AxisListType

D = 1024
DFF = 2816
NJ = DFF // 128
DEPTH = 4
ALPHA = (2.0 * DEPTH) ** 0.25
LN_EPS = 1e-5
RMS_EPS = 1e-6
ENGINES = ("pe", "act", "dve", "pool", "sp")


class LT:
    __slots__ = ("name", "last_w", "readers", "dsem")

    def __init__(self, name=""):
        self.name = name
        self.last_w = None
        self.readers = []
        self.dsem = None


def lts(n, name=""):
    return [LT(name + str(i)) for i in range(n)]


class DmaSem:
    __slots__ = ("sem", "count")

    def __init__(self, sem):
        self.sem = sem
        self.count = 0


class Op:
    __slots__ = ("eng", "fn", "pos", "needs_inc", "tick", "is_dma", "dsem", "dtarget",
                 "waits_eng", "waits_dma", "clock")

    def __init__(self, eng, fn):
        self.eng = eng
        self.fn = fn
        self.pos = -1
        self.needs_inc = False
        self.tick = 0
        self.is_dma = False
        self.dsem = None
        self.dtarget = 0
        self.waits_eng = {}
        self.waits_dma = {}
        self.clock = None


class Prog:
    def __init__(self, nc):
        self.nc = nc
        self.ops = {e: [] for e in ENGINES}
        self.known = {e: {x: -1 for x in ENGINES} for e in ENGINES}
        self.known_dma = {e: {} for e in ENGINES}
        self.free_dsems = []
        self.all_dsems = []
        self.esem = {}
        self.phase_tiles = []
        self.swq = []

    def setup_sems(self, stack, n_dma):
        for e in ("pe", "act", "dve", "pool"):
            self.esem[e] = stack.enter_context(self.nc.semaphore("es_" + e))
        self.free_dsems = {"pool": [], "sp": []}
        for i in range(n_dma):
            d = DmaSem(stack.enter_context(self.nc.semaphore("ds_%d" % i)))
            self.free_dsems["pool" if i < 30 else "sp"].append(d)
            self.all_dsems.append(d)
        self.dsem_owner = {}

    def _get_dsem(self, tile, eng):
        if tile.dsem is None:
            tile.dsem = self.free_dsems[eng].pop()
            self.dsem_owner[tile.dsem] = eng
            self.phase_tiles.append(tile)
        assert self.dsem_owner[tile.dsem] == eng
        return tile.dsem

    def _add_dep(self, op, dep):
        if dep is None or dep is op:
            return
        e = op.eng
        if dep.is_dma:
            if self.known_dma[e].get(dep.dsem, 0) >= dep.dtarget:
                return
            if op.waits_dma.get(dep.dsem, 0) < dep.dtarget:
                op.waits_dma[dep.dsem] = dep.dtarget
            return
        x = dep.eng
        if x == "pe" and e == "pe":
            return
        if self.known[e][x] >= dep.pos:
            return
        cur = op.waits_eng.get(x)
        if cur is None or cur.pos < dep.pos:
            op.waits_eng[x] = dep

    def add(self, eng, fn, reads=(), writes=(), dma_tile=None):
        op = Op(eng, fn)
        for t in reads:
            self._add_dep(op, t.last_w)
        for t in writes:
            self._add_dep(op, t.last_w)
            for r in t.readers:
                self._add_dep(op, r)
        kn = self.known[eng]
        for x, dep in op.waits_eng.items():
            dep.needs_inc = True
            if kn[x] < dep.pos:
                kn[x] = dep.pos
            if dep.clock is not None:
                for y, p in dep.clock.items():
                    if kn[y] < p:
                        kn[y] = p
        for ds, tgt in op.waits_dma.items():
            self.known_dma[eng][ds] = tgt
        op.pos = len(self.ops[eng])
        self.ops[eng].append(op)
        if dma_tile is not None:
            op.is_dma = True
            op.dsem = self._get_dsem(dma_tile, eng)
            op.dsem.count += 16
            op.dtarget = op.dsem.count
        else:
            op.clock = dict(kn)
            op.clock[eng] = op.pos
        for t in reads:
            t.readers.append(op)
        for t in writes:
            t.last_w = op
            t.readers = []
        return op

    def barrier(self):
        lasts = {}
        for e in ("pe", "act", "dve", "pool"):
            for op in reversed(self.ops[e]):
                if op.fn is not None and not op.is_dma:
                    lasts[e] = op
                    break
        dtargets = {d: d.count for d in self.all_dsems if d.count > 0}
        for e in ENGINES:
            op = Op(e, None)
            for x, dep in lasts.items():
                if x == e:
                    continue
                if self.known[e][x] < dep.pos:
                    op.waits_eng[x] = dep
                    dep.needs_inc = True
                    self.known[e][x] = dep.pos
            for d, tgt in dtargets.items():
                if self.known_dma[e].get(d, 0) < tgt:
                    op.waits_dma[d] = tgt
                    self.known_dma[e][d] = tgt
            op.pos = len(self.ops[e])
            self.ops[e].append(op)
            op.clock = dict(self.known[e])
        for t in self.phase_tiles:
            if t.dsem is not None:
                self.free_dsems[self.dsem_owner[t.dsem]].append(t.dsem)
                t.dsem = None
        self.phase_tiles = []

    def emit(self, block):
        for e in ("pe", "act", "dve", "pool"):
            t = 0
            for op in self.ops[e]:
                if op.needs_inc:
                    t += 1
                    op.tick = t
        esem = self.esem

        def run(e, eng):
            for op in self.ops[e]:
                for x, dep in op.waits_eng.items():
                    eng.wait_ge(esem[x], dep.tick)
                for ds, tgt in op.waits_dma.items():
                    eng.wait_ge(ds.sem, tgt)
                if op.fn is None:
                    continue
                ins = op.fn(eng)
                if op.is_dma:
                    ins.then_inc(op.dsem.sem, 16)
                elif op.needs_inc:
                    ins.then_inc(esem[e], 1)

        @block.tensor
        def _(eng):
            run("pe", eng)

        @block.scalar
        def _(eng):
            run("act", eng)

        @block.vector
        def _(eng):
            run("dve", eng)

        @block.gpsimd
        def _(eng):
            run("pool", eng)

        @block.sync
        def _(eng):
            run("sp", eng)


class K:
    pass


_uid = [0]


def _nm(s):
    _uid[0] += 1
    return "%s_%d" % (s, _uid[0])


def sb(k, st, name, shape, dt):
    return st.enter_context(k.nc.sbuf_tensor(_nm(name), list(shape), dt))


def pst(k, st, name, shape, dt):
    return st.enter_context(k.nc.psum_tensor(_nm(name), list(shape), dt))


def load_cast(k, dst_ap, src_ap, lt):
    P = k.P
    if len(P.swq) >= 8:
        old = P.swq.pop(0)
        w = Op("pool", None)
        if P.known_dma["pool"].get(old.dsem, 0) < old.dtarget:
            w.waits_dma[old.dsem] = old.dtarget
            P.known_dma["pool"][old.dsem] = old.dtarget
        w.pos = len(P.ops["pool"])
        P.ops["pool"].append(w)
    op = P.add("pool", lambda e: e.dma_start(out=dst_ap, in_=src_ap), writes=[lt], dma_tile=lt)
    P.swq.append(op)


def dma(k, dst_ap, src_ap, reads=(), writes=(), tile=None):
    k.P.add("sp", lambda e: e.dma_start(out=dst_ap, in_=src_ap), reads=reads, writes=writes, dma_tile=tile)


class PostNorm:
    NB = 3

    def __init__(self, k, st, g_ap, b_ap, x_src, x_dst, nt):
        self.k = k
        self.nt = nt
        self.x_src = x_src
        self.x_dst = x_dst
        NB = self.NB
        self.G = sb(k, st, "lnG", [128, D], F32)
        self.B = sb(k, st, "lnB", [128, D], F32)
        self.G_lt, self.B_lt = LT(), LT()
        dma(k, self.G[:], g_ap.partition_broadcast(128), writes=[self.G_lt], tile=self.G_lt)
        dma(k, self.B[:], b_ap.partition_broadcast(128), writes=[self.B_lt], tile=self.B_lt)
        self.xin = [sb(k, st, "pn_x", [128, D], F32) for _ in range(NB)]
        self.u = [sb(k, st, "pn_u", [128, D], F32) for _ in range(NB)]
        self.y = [sb(k, st, "pn_y", [128, D], F32) for _ in range(NB)]
        self.ybf = [sb(k, st, "pn_yb", [128, D], BF16) for _ in range(2)]
        self.stt = [sb(k, st, "pn_st", [128, 2, 6], F32) for _ in range(NB)]
        self.mv = [sb(k, st, "pn_mv", [128, 2], F32) for _ in range(NB)]
        self.rstd = [sb(k, st, "pn_rs", [128, 1], F32) for _ in range(NB)]
        self.nmr = [sb(k, st, "pn_nm", [128, 1], F32) for _ in range(NB)]
        self.tp = [pst(k, st, "pn_tp", [128, 8, 128], BF16) for _ in range(2)]
        self.l_xin, self.l_u, self.l_y = lts(NB), lts(NB), lts(NB)
        self.l_st, self.l_mv, self.l_rstd, self.l_nmr = lts(NB), lts(NB), lts(NB), lts(NB)
        self.l_ybf, self.l_tp = lts(2), lts(2)
        self.subs = {}
        self.prefetch(0)
        if nt > 1:
            self.prefetch(1)

    def prefetch(self, t):
        k = self.k
        b = t % self.NB
        dma(k, self.xin[b][:], self.x_src[t * 128:(t + 1) * 128, :], writes=[self.l_xin[b]], tile=self.l_xin[b])

    def pre(self, t):
        pass

    def step(self, t, sub_ps, sub_lt):
        self._s1(t, sub_ps, sub_lt)
        if t - 1 >= 0:
            self._s2(t - 1)
        if t - 2 >= 0:
            self._s3a(t - 2)
            self._s3b(t - 2)
        if t + 2 < self.nt:
            self.prefetch(t + 2)

    def drain(self):
        nt = self.nt
        if nt - 2 >= 0:
            self._s3a(nt - 2)
            self._s3b(nt - 2)
        self._s2(nt - 1)
        self._s3a(nt - 1)
        self._s3b(nt - 1)

    def _s1(self, t, sub_ps, sub_lt):
        k = self.k
        P = k.P
        b = t % self.NB
        xin, u, stt, mv, rstd, nmr = self.xin[b], self.u[b], self.stt[b], self.mv[b], self.rstd[b], self.nmr[b]
        l_xin, l_u, l_st, l_mv, l_rstd, l_nmr = (self.l_xin[b], self.l_u[b], self.l_st[b], self.l_mv[b],
                                                  self.l_rstd[b], self.l_nmr[b])
        P.add("dve", lambda e: e.scalar_tensor_tensor(out=u[:], in0=xin[:], scalar=ALPHA, in1=sub_ps,
                                                      op0=ALU.mult, op1=ALU.add),
              reads=[l_xin, sub_lt], writes=[l_u])
        for c in range(2):
            P.add("dve", lambda e, c=c: e.bn_stats(out=stt[:, c, :], in_=u[:, c * 512:(c + 1) * 512]),
                  reads=[l_u], writes=[l_st])
        P.add("dve", lambda e: e.bn_aggr(out=mv[:], in_=stt[:]), reads=[l_st], writes=[l_mv])
        P.add("pool", lambda e: e.tensor_scalar(out=rstd[:], in0=mv[:, 1:2], scalar1=LN_EPS, scalar2=1.0,
                                                op0=ALU.add, op1=ALU.mult), reads=[l_mv], writes=[l_rstd])
        P.add("pool", lambda e: e.tensor_tensor(out=rstd[:], in0=rstd[:], in1=k.mhalf[:], op=ALU.pow),
              reads=[l_rstd, k.eps_lt], writes=[l_rstd])
        P.add("pool", lambda e: e.tensor_scalar(out=nmr[:], in0=mv[:, 0:1], scalar1=-1.0, scalar2=1.0,
                                                op0=ALU.mult, op1=ALU.mult), reads=[l_mv], writes=[l_nmr])
        P.add("pool", lambda e: e.tensor_tensor(out=nmr[:], in0=nmr[:], in1=rstd[:], op=ALU.mult),
              reads=[l_nmr, l_rstd], writes=[l_nmr])

    def _s2(self, t):
        k = self.k
        P = k.P
        b = t % self.NB
        u, y, rstd, nmr = self.u[b], self.y[b], self.rstd[b], self.nmr[b]
        l_u, l_y, l_rstd, l_nmr = self.l_u[b], self.l_y[b], self.l_rstd[b], self.l_nmr[b]
        P.add("act", lambda e: e.activation(out=u[:], in_=u[:], func=AF.Identity, bias=nmr[:], scale=rstd[:]),
              reads=[l_u, l_nmr, l_rstd], writes=[l_u])
        P.add("dve", lambda e: e.tensor_tensor(out=u[:], in0=u[:], in1=self.G[:], op=ALU.mult),
              reads=[l_u, self.G_lt], writes=[l_u])
        P.add("pool", lambda e: e.tensor_tensor(out=y[:], in0=u[:], in1=self.B[:], op=ALU.add),
              reads=[l_u, self.B_lt], writes=[l_y])
        dma(k, self.x_dst[t * 128:(t + 1) * 128, :], y[:], reads=[l_y], tile=l_y)

    def _s3a(self, t):
        k = self.k
        P = k.P
        if not k.need_xt:
            return
        b = t % self.NB
        y, l_y = self.y[b], self.l_y[b]
        ybf, l_ybf = self.ybf[t % 2], self.l_ybf[t % 2]
        P.add("act", lambda e: e.copy(out=ybf[:], in_=y[:]), reads=[l_y], writes=[l_ybf])

    def _s3b(self, t):
        k = self.k
        P = k.P
        if not k.need_xt:
            return
        ybf, l_ybf = self.ybf[t % 2], self.l_ybf[t % 2]
        tp, l_tp = self.tp[t % 2], self.l_tp[t % 2]
        for c in range(8):
            P.add("pe", lambda e, c=c: e.transpose(out=tp[:, c, :], in_=ybf[:, c * 128:(c + 1) * 128],
                                                   identity=k.ident[:]),
                  reads=[l_ybf, k.ident_lt], writes=[l_tp])
        XT = k.XT
        P.add("dve", lambda e: e.tensor_copy(out=XT[:, :, t * 128:(t + 1) * 128], in_=tp[:]),
              reads=[l_tp], writes=[k.XT_lt[t]])


def phase_prep(k, x_in):
    P = k.P
    S = k.S
    with ExitStack() as st:
        xin = [sb(k, st, "pp_x", [128, D], F32) for _ in range(2)]
        xbf = [sb(k, st, "pp_xb", [128, D], BF16) for _ in range(2)]
        tp = [pst(k, st, "pp_tp", [128, 8, 128], BF16) for _ in range(2)]
        l_x, l_xb, l_tp = lts(2), lts(2), lts(2)
        for t in range(S // 128):
            b = t % 2
            dma(k, xin[b][:], x_in[t * 128:(t + 1) * 128, :], writes=[l_x[b]], tile=l_x[b])
            P.add("act", lambda e, b=b: e.copy(out=xbf[b][:], in_=xin[b][:]), reads=[l_x[b]], writes=[l_xb[b]])
            for c in range(8):
                P.add("pe", lambda e, b=b, c=c: e.transpose(out=tp[b][:, c, :], in_=xbf[b][:, c * 128:(c + 1) * 128],
                                                            identity=k.ident[:]),
                      reads=[l_xb[b], k.ident_lt], writes=[l_tp[b]])
            XT = k.XT
            P.add("dve", lambda e, b=b, t=t: e.tensor_copy(out=XT[:, :, t * 128:(t + 1) * 128], in_=tp[b][:]),
                  reads=[l_tp[b]], writes=[k.XT_lt[t]])
        P.barrier()


def load_wo(k, st, w_ap, bias_ap):
    wo = sb(k, st, "wo", [128, 8, D], BF16)
    wo_lt = lts(8)
    wv = w_ap.rearrange("(c p) n -> p c n", p=128)
    for c in range(8):
        load_cast(k, wo[:, c, :], wv[:, c, :], wo_lt[c])
    bo, bo_lt = None, None
    if bias_ap is not None:
        bo = sb(k, st, "bo", [1, D], BF16)
        bo_lt = LT()
        load_cast(k, bo[:], bias_ap.unsqueeze(0), bo_lt)
    return wo, wo_lt, bo, bo_lt


def phase_proj_postnorm(k, w_ap, bias_ap, g_ap, b_ap, x_src, x_dst, pre=None):
    P = k.P
    S = k.S
    with ExitStack() as st:
        if pre is None:
            pre = load_wo(k, st, w_ap, bias_ap)
        wo, wo_lt, bo, bo_lt = pre
        ob = [sb(k, st, "ob", [128, 8, 512], BF16) for _ in range(2)]
        ob_lt = lts(2)
        sub = [pst(k, st, "sub", [128, D], F32) for _ in range(2)]
        sub_lt = lts(2)
        nt = S // 128
        pn = PostNorm(k, st, g_ap, b_ap, x_src, x_dst, nt)
        otv = k.OT.rearrange("c p s -> p c s")
        def load_ob(blk):
            bb = blk % 2
            dma(k, ob[bb][:], otv[:, :, blk * 512:(blk + 1) * 512], writes=[ob_lt[bb]], tile=ob_lt[bb])

        load_ob(0)
        for t in range(nt):
            blk = t // 4
            bb = blk % 2
            if t % 4 == 0 and (blk + 1) * 512 < S:
                load_ob(blk + 1)
            s_ = t % 2
            pn.pre(t)
            for c in range(8):
                for hf in range(2):
                    P.add("pe", lambda e, c=c, hf=hf, bb=bb, s_=s_, t=t: e.matmul(
                        sub[s_][:, hf * 512:(hf + 1) * 512], ob[bb][:, c, (t % 4) * 128:(t % 4 + 1) * 128],
                        wo[:, c, hf * 512:(hf + 1) * 512], start=(c == 0), stop=(c == 7 and bias_ap is None)),
                        reads=[ob_lt[bb], wo_lt[c]], writes=[sub_lt[s_]])
            if bias_ap is not None:
                for hf in range(2):
                    P.add("pe", lambda e, hf=hf, s_=s_: e.matmul(
                        sub[s_][:, hf * 512:(hf + 1) * 512], k.ones[0:1, :], bo[0:1, hf * 512:(hf + 1) * 512],
                        start=False, stop=True), reads=[bo_lt, k.ones_lt], writes=[sub_lt[s_]])
            pn.step(t, sub[s_][:], sub_lt[s_])
        pn.drain()
        P.barrier()


def phase_ffn_up(k, w_up, cp_ap):
    P = k.P
    S = k.S
    nblk = S // 512
    with ExitStack() as st:
        cp = sb(k, st, "cp", [128, NJ, 8], F32)
        cp_lt = LT()
        dma(k, cp[:], cp_ap, writes=[cp_lt], tile=cp_lt)
        wu = [sb(k, st, "wu", [128, 8, 256], BF16) for _ in range(3)]
        wu_lt = lts(3)
        H = [[sb(k, st, "H", [128, S + 2], F32) for _ in range(2)] for _ in range(2)]
        H_lt = [[lts(nblk) for _ in range(2)] for _ in range(2)]
        pad_lt = LT()
        for hf in range(2):
            for jb in range(2):
                P.add("pool", lambda e, hf=hf, jb=jb: e.memset(H[hf][jb][:, 0:1], 0.0), writes=[pad_lt])
                P.add("pool", lambda e, hf=hf, jb=jb: e.memset(H[hf][jb][:, S + 1:S + 2], 0.0), writes=[pad_lt])
        NTB = 3
        Tg = [sb(k, st, "Tg", [128, 512], F32) for _ in range(NTB)]
        Tv = [sb(k, st, "Tv", [128, 512], F32) for _ in range(NTB)]
        Tg_lt, Tv_lt = lts(NTB), lts(NTB)
        TT = [Tg, Tv]
        TT_lt = [Tg_lt, Tv_lt]
        A = [sb(k, st, "A", [128, S], BF16) for _ in range(2)]
        A_lt = lts(2)
        hp = [[pst(k, st, "hp", [128, 512], F32) for _ in range(2)] for _ in range(2)]
        hp_lt = [lts(2), lts(2)]
        wv = w_up.rearrange("(kc p) f -> p kc f", p=128)
        XT = k.XT

        def load_w(j):
            wb = j % 3
            load_cast(k, wu[wb][:, :, 0:128], wv[:, :, j * 128:(j + 1) * 128], wu_lt[wb])
            load_cast(k, wu[wb][:, :, 128:256], wv[:, :, DFF + j * 128:DFF + (j + 1) * 128], wu_lt[wb])

        load_w(0)
        if NJ > 1:
            load_w(1)
        cnt = 0
        for j in range(NJ):
            if j + 2 < NJ:
                load_w(j + 2)
            wb = j % 3
            jb = j % 2

            def conv(blk, j=j, jb=jb):
                o = blk * 512
                tb = blk % NTB
                rl = [H_lt[0][jb][b2] for b2 in (blk - 1, blk, blk + 1) if 0 <= b2 < nblk] + [pad_lt, cp_lt]
                rv = [H_lt[1][jb][b2] for b2 in (blk - 1, blk, blk + 1) if 0 <= b2 < nblk] + [pad_lt, cp_lt]
                Hg, Hv = H[0][jb], H[1][jb]
                tg, tv = Tg[tb], Tv[tb]
                P.add("dve", lambda e: e.scalar_tensor_tensor(out=tg[:], in0=Hg[:, o:o + 512], scalar=cp[:, j, 0:1],
                                                              in1=tg[:], op0=ALU.mult, op1=ALU.add),
                      reads=rl + [Tg_lt[tb]], writes=[Tg_lt[tb]])
                P.add("dve", lambda e: e.scalar_tensor_tensor(out=tg[:], in0=Hg[:, 2 + o:2 + o + 512],
                                                              scalar=cp[:, j, 2:3], in1=tg[:], op0=ALU.mult,
                                                              op1=ALU.add),
                      reads=rl + [Tg_lt[tb]], writes=[Tg_lt[tb]])
                P.add("dve", lambda e: e.scalar_tensor_tensor(out=tv[:], in0=Hv[:, o:o + 512], scalar=cp[:, j, 4:5],
                                                               in1=tv[:], op0=ALU.mult, op1=ALU.add),
                      reads=rv + [Tv_lt[tb]], writes=[Tv_lt[tb]])
                P.add("dve", lambda e: e.scalar_tensor_tensor(out=tv[:], in0=Hv[:, 2 + o:2 + o + 512],
                                                               scalar=cp[:, j, 6:7], in1=tv[:], op0=ALU.mult,
                                                               op1=ALU.add),
                      reads=rv + [Tv_lt[tb]], writes=[Tv_lt[tb]])
                P.add("act", lambda e: e.activation(out=tg[:], in_=tg[:], func=AF.Silu),
                      reads=[Tg_lt[tb]], writes=[Tg_lt[tb]])
                P.add("pool", lambda e: e.tensor_tensor(out=A[jb][:, o:o + 512], in0=tg[:], in1=tv[:], op=ALU.mult),
                      reads=[Tg_lt[tb], Tv_lt[tb]], writes=[A_lt[jb]])

            for blk in range(nblk):
                for hf in range(2):
                    pb = cnt % 2
                    for kc in range(8):
                        P.add("pe", lambda e, hf=hf, pb=pb, kc=kc, blk=blk, wb=wb: e.matmul(
                            hp[hf][pb][:], wu[wb][:, kc, hf * 128:(hf + 1) * 128],
                            XT[:, kc, blk * 512:(blk + 1) * 512], start=(kc == 0), stop=(kc == 7)),
                            reads=[wu_lt[wb]] + k.XT_lt[blk * 4:blk * 4 + 4], writes=[hp_lt[hf][pb]])
                    P.add("act", lambda e, hf=hf, pb=pb, blk=blk, jb=jb: e.copy(
                        out=H[hf][jb][:, 1 + blk * 512:1 + (blk + 1) * 512], in_=hp[hf][pb][:]),
                        reads=[hp_lt[hf][pb]], writes=[H_lt[hf][jb][blk]])
                    P.add("act", lambda e, hf=hf, pb=pb, blk=blk, j=j: e.activation(
                        out=TT[hf][blk % NTB][:], in_=hp[hf][pb][:], func=AF.Identity,
                        bias=cp[:, j, hf * 4 + 3:hf * 4 + 4], scale=cp[:, j, hf * 4 + 1:hf * 4 + 2]),
                        reads=[hp_lt[hf][pb], cp_lt], writes=[TT_lt[hf][blk % NTB]])
                cnt += 1
                if blk >= 1:
                    conv(blk - 1)
            conv(nblk - 1)
            dma(k, k.AT[j], A[jb][:], reads=[A_lt[jb]], tile=A_lt[jb])
        P.barrier()


def phase_ffn_down(k, w_down, g_ap, b_ap, x_src, x_dst):
    P = k.P
    S = k.S
    with ExitStack() as st:
        wd = sb(k, st, "wd", [128, NJ, D], BF16)
        wd_lt = lts(NJ)
        wv = w_down.rearrange("(c p) n -> p c n", p=128)
        for c in range(NJ):
            load_cast(k, wd[:, c, :], wv[:, c, :], wd_lt[c])
        ab = [sb(k, st, "ab", [128, NJ, 256], BF16) for _ in range(2)]
        ab_lt = lts(2)
        sub = [pst(k, st, "sub", [128, D], F32) for _ in range(2)]
        sub_lt = lts(2)
        nt = S // 128
        pn = PostNorm(k, st, g_ap, b_ap, x_src, x_dst, nt)
        atv = k.AT.rearrange("c p s -> p c s")

        def load_ab(blk):
            bb = blk % 2
            dma(k, ab[bb][:], atv[:, :, blk * 256:(blk + 1) * 256], writes=[ab_lt[bb]], tile=ab_lt[bb])

        load_ab(0)
        for t in range(nt):
            blk = t // 2
            bb = blk % 2
            if t % 2 == 0 and (blk + 1) * 256 < S:
                load_ab(blk + 1)
            s_ = t % 2
            pn.pre(t)
            for c in range(NJ):
                for hf in range(2):
                    P.add("pe", lambda e, c=c, hf=hf, bb=bb, s_=s_, t=t: e.matmul(
                        sub[s_][:, hf * 512:(hf + 1) * 512], ab[bb][:, c, (t % 2) * 128:(t % 2 + 1) * 128],
                        wd[:, c, hf * 512:(hf + 1) * 512], start=(c == 0), stop=(c == NJ - 1)),
                        reads=[ab_lt[bb], wd_lt[c]], writes=[sub_lt[s_]])
            pn.step(t, sub[s_][:], sub_lt[s_])
        pn.drain()
        P.barrier()


def phase_qkv_gqa(k, w_qkv, gain_ap, rope_ap):
    P = k.P
    S = k.S
    with ExitStack() as st:
        wq = sb(k, st, "wq", [128, 8, 1536], BF16)
        wq_lt = lts(8)
        wv = w_qkv.rearrange("(c p) n -> p c n", p=128)
        for c in range(8):
            load_cast(k, wq[:, c, :], wv[:, c, :], wq_lt[c])
        gain = sb(k, st, "gain", [128, 1280], F32)
        gain_lt = LT()
        dma(k, gain[:], gain_ap.partition_broadcast(128), writes=[gain_lt], tile=gain_lt)
        rp = [sb(k, st, "rp", [128, 2, 64], F32) for _ in range(2)]
        rp_lt = lts(2)
        qkv = [pst(k, st, "qkv", [128, 1536], F32) for _ in range(2)]
        qkv_lt = lts(2)
        tq = pst(k, st, "tq", [128, 8, 128], BF16)
        tk = pst(k, st, "tk", [128, 4, 128], BF16)
        tq_lt, tk_lt = LT(), LT()
        sq = [sb(k, st, "sq", [128, 1280], F32) for _ in range(2)]
        qn = [sb(k, st, "qn", [128, 1280], F32) for _ in range(2)]
        t1 = [sb(k, st, "t1", [128, 1280], F32) for _ in range(2)]
        t2 = [sb(k, st, "t2", [128, 1280], F32) for _ in range(2)]
        ss = [sb(k, st, "ss", [128, 20], F32) for _ in range(2)]
        qb = [sb(k, st, "qb", [128, 1024], BF16) for _ in range(2)]
        kd = [sb(k, st, "kd", [128, 4, 128], BF16) for _ in range(2)]
        va = [sb(k, st, "va", [128, 4, 2, 128], BF16) for _ in range(2)]
        l_sq, l_qn, l_t1, l_t2, l_ss, l_qb, l_kd, l_va = (lts(2) for _ in range(8))
        for b in range(2):
            P.add("pool", lambda e, b=b: e.memset(va[b][:], 1.0), writes=[l_va[b]])
        qs = [sb(k, st, "qs", [128, 8, 512], BF16) for _ in range(2)]
        ks = [sb(k, st, "ks", [128, 4, 512], BF16) for _ in range(2)]
        l_qs, l_ks = lts(2), lts(2)
        XT = k.XT
        qtv = k.QT.rearrange("c p s -> p c s")
        ktv = k.KT.rearrange("c p s -> p c s")
        nt = S // 128

        def stage_a(t):
            b = t % 2
            qk_ps = qkv[b][:, 0:1280]
            P.add("act", lambda e: e.activation(out=sq[b][:], in_=qk_ps, func=AF.Square),
                  reads=[qkv_lt[b]], writes=[l_sq[b]])
            P.add("dve", lambda e: e.tensor_reduce(out=ss[b][:], in_=sq[b][:].rearrange("p (h d) -> p h d", d=64),
                                                   axis=AX.X, op=ALU.add), reads=[l_sq[b]], writes=[l_ss[b]])
            P.add("act", lambda e: e.activation(out=ss[b][:], in_=ss[b][:], func=AF.Sqrt, bias=k.eps_rms[:],
                                                scale=1.0 / 64), reads=[l_ss[b], k.eps_lt], writes=[l_ss[b]])
            P.add("dve", lambda e: e.reciprocal(out=ss[b][:], in_=ss[b][:]), reads=[l_ss[b]], writes=[l_ss[b]])
            P.add("dve", lambda e: e.tensor_tensor(
                out=qn[b][:].rearrange("p (h d) -> p h d", d=64), in0=qk_ps.rearrange("p (h d) -> p h d", d=64),
                in1=ss[b][:].unsqueeze(2).broadcast_to([128, 20, 64]), op=ALU.mult),
                reads=[qkv_lt[b], l_ss[b]], writes=[l_qn[b]])
            v_ps = qkv[b][:, 1280:1536].rearrange("p (g d) -> p g d", d=64)
            P.add("act", lambda e: e.copy(out=va[b][:, :, 0, 0:64], in_=v_ps), reads=[qkv_lt[b]], writes=[l_va[b]])
            P.add("act", lambda e: e.copy(out=va[b][:, :, 1, 64:128], in_=v_ps), reads=[qkv_lt[b]], writes=[l_va[b]])
            dma(k, k.VA[t], va[b][:].rearrange("p g v d -> p (g v d)"), reads=[l_va[b]], tile=l_va[b])

        def stage_b(t):
            b = t % 2
            P.add("pool", lambda e: e.tensor_tensor(out=qn[b][:], in0=qn[b][:], in1=gain[:], op=ALU.mult),
                  reads=[l_qn[b], gain_lt], writes=[l_qn[b]])
            P.add("pool", lambda e: e.tensor_tensor(
                out=t1[b][:].rearrange("p (h d) -> p h d", d=64), in0=qn[b][:].rearrange("p (h d) -> p h d", d=64),
                in1=rp[b][:, 0, :].unsqueeze(1).broadcast_to([128, 20, 64]), op=ALU.mult),
                reads=[l_qn[b], rp_lt[b]], writes=[l_t1[b]])
            for a in range(2):
                P.add("dve", lambda e, a=a: e.tensor_tensor(
                    out=t2[b][:].rearrange("p (h g a d) -> p h g a d", g=2, a=2, d=16)[:, :, :, a, :],
                    in0=qn[b][:].rearrange("p (h g a d) -> p h g a d", g=2, a=2, d=16)[:, :, :, 1 - a, :],
                    in1=rp[b][:, 1, :].rearrange("p (g a d) -> p g a d", g=2, a=2, d=16)[:, :, a, :].unsqueeze(1)
                    .broadcast_to([128, 20, 2, 16]),
                    op=ALU.mult), reads=[l_qn[b], rp_lt[b]], writes=[l_t2[b]])
            P.add("dve", lambda e: e.tensor_tensor(out=qb[b][:], in0=t1[b][:, 0:1024], in1=t2[b][:, 0:1024],
                                                   op=ALU.add), reads=[l_t1[b], l_t2[b]], writes=[l_qb[b]])
            for hh in range(2):
                P.add("pool", lambda e, hh=hh: e.tensor_tensor(
                    out=kd[b][:, :, hh * 64:(hh + 1) * 64], in0=t1[b][:, 1024:1280].rearrange("p (g d) -> p g d", d=64),
                    in1=t2[b][:, 1024:1280].rearrange("p (g d) -> p g d", d=64), op=ALU.add),
                    reads=[l_t1[b], l_t2[b]], writes=[l_kd[b]])

        def tail(t):
            b = t % 2
            blk = t // 4
            sbb = blk % 2
            for c in range(8):
                P.add("pe", lambda e, c=c: e.transpose(out=tq[:, c, :], in_=qb[b][:, c * 128:(c + 1) * 128],
                                                       identity=k.ident[:]),
                      reads=[l_qb[b], k.ident_lt], writes=[tq_lt])
            for g in range(4):
                P.add("pe", lambda e, g=g: e.transpose(out=tk[:, g, :], in_=kd[b][:, g, :], identity=k.ident[:]),
                      reads=[l_kd[b], k.ident_lt], writes=[tk_lt])
            o = (t % 4) * 128
            P.add("act", lambda e: e.copy(out=qs[sbb][:, :, o:o + 128], in_=tq[:]),
                  reads=[tq_lt], writes=[l_qs[sbb]])
            P.add("dve", lambda e: e.tensor_copy(out=ks[sbb][:, :, o:o + 128], in_=tk[:]),
                  reads=[tk_lt], writes=[l_ks[sbb]])
            if t % 4 == 3 or t == nt - 1:
                dma(k, qtv[:, :, blk * 512:(blk + 1) * 512], qs[sbb][:], reads=[l_qs[sbb]], tile=l_qs[sbb])
                dma(k, ktv[:, 0:4, blk * 512:(blk + 1) * 512], ks[sbb][:], reads=[l_ks[sbb]], tile=l_ks[sbb])

        for t in range(nt):
            b = t % 2
            dma(k, rp[b][:], rope_ap[t * 128:(t + 1) * 128], writes=[rp_lt[b]], tile=rp_lt[b])
            for kc in range(8):
                for n in range(3):
                    P.add("pe", lambda e, kc=kc, n=n, b=b, t=t: e.matmul(
                        qkv[b][:, n * 512:(n + 1) * 512], XT[:, kc, t * 128:(t + 1) * 128],
                        wq[:, kc, n * 512:(n + 1) * 512], start=(kc == 0), stop=(kc == 7)),
                        reads=[k.XT_lt[t], wq_lt[kc]], writes=[qkv_lt[b]])
            if t >= 2:
                tail(t - 2)
            stage_a(t)
            if t >= 1:
                stage_b(t - 1)
        stage_b(nt - 1)
        if nt >= 2:
            tail(nt - 2)
        tail(nt - 1)
        P.barrier()


def phase_attn_gqa(k):
    P = k.P
    S = k.S
    nkt = S // 128
    nqb = S // 512
    with ExitStack() as st:
        ktd = [sb(k, st, "ktd", [128, S], BF16) for _ in range(2)]
        vag = [sb(k, st, "vag", [128, nkt, 2, 128], BF16) for _ in range(2)]
        qtc = [sb(k, st, "qtc", [128, 2, S], BF16) for _ in range(2)]
        l_ktd, l_vag, l_qtc = lts(2), lts(2), lts(2)
        sps = [[pst(k, st, "sps", [128, 512], F32) for _ in range(2)] for _ in range(2)]
        l_sps = [lts(2), lts(2)]
        acc = [[pst(k, st, "acc", [128, 512], F32) for _ in range(2)] for _ in range(2)]
        l_acc = [lts(2), lts(2)]
        NPB = 3
        pT = [[sb(k, st, "pT", [128, 512], BF16) for _ in range(NPB)] for _ in range(2)]
        l_pT = [lts(NPB), lts(NPB)]
        rc = [sb(k, st, "rc", [128, 512], F32) for _ in range(2)]
        l_rc = lts(2)
        ost = [sb(k, st, "ost", [128, 512], BF16) for _ in range(2)]
        l_ost = lts(2)
        vav = k.VA.rearrange("t p (g x) -> p t g x", g=4)

        def load_g(g):
            gb = g % 2
            dma(k, ktd[gb][:], k.KT[g], writes=[l_ktd[gb]], tile=l_ktd[gb])
            dma(k, vag[gb][:].rearrange("p t v d -> p t (v d)"), vav[:, :, g, :], writes=[l_vag[gb]], tile=l_vag[gb])
            dma(k, qtc[gb][:], k.QT[2 * g:2 * g + 2].rearrange("c p s -> p c s"), writes=[l_qtc[gb]], tile=l_qtc[gb])

        load_g(0)
        it = 0
        for g in range(4):
            gb = g % 2
            if g + 1 < 4:
                load_g(g + 1)
            for pr in range(2):
                c = 2 * g + pr
                for qb_ in range(nqb):
                    ab = it % 2
                    it += 1

                    def qk(kt, gb=gb, pr=pr, qb_=qb_):
                        s_ = kt % 2
                        for h in range(2):
                            P.add("pe", lambda e, h=h, s_=s_, kt=kt: e.matmul(
                                sps[h][s_][:], ktd[gb][h * 64:(h + 1) * 64, kt * 128:(kt + 1) * 128],
                                qtc[gb][h * 64:(h + 1) * 64, pr, qb_ * 512:(qb_ + 1) * 512], start=True, stop=True),
                                reads=[l_ktd[gb], l_qtc[gb]], writes=[l_sps[h][s_]])
                        for h in range(2):
                            pb = kt % NPB
                            P.add("act", lambda e, h=h, s_=s_, pb=pb: e.activation(
                                out=pT[h][pb][:], in_=sps[h][s_][:], func=AF.Exp, scale=0.125),
                                reads=[l_sps[h][s_]], writes=[l_pT[h][pb]])

                    def pv(kt, gb=gb, ab=ab):
                        pb = kt % NPB
                        for h in range(2):
                            P.add("pe", lambda e, h=h, pb=pb, kt=kt: e.matmul(
                                acc[h][ab][:], vag[gb][:, kt, h, :], pT[h][pb][:], start=(kt == 0),
                                stop=(kt == nkt - 1)),
                                reads=[l_vag[gb], l_pT[h][pb]], writes=[l_acc[h][ab]])

                    for kt in range(nkt + 1):
                        if kt < nkt:
                            qk(kt)
                        if kt >= 1:
                            pv(kt - 1)
                    ob_ = ab
                    P.add("dve", lambda e, ab=ab: e.reciprocal(out=rc[ab][0:64, :], in_=acc[0][ab][64:128, :]),
                          reads=[l_acc[0][ab]], writes=[l_rc[ab]])
                    P.add("dve", lambda e, ab=ab: e.reciprocal(out=rc[ab][64:128, :], in_=acc[1][ab][0:64, :]),
                          reads=[l_acc[1][ab]], writes=[l_rc[ab]])
                    P.add("dve", lambda e, ab=ab: e.tensor_tensor(out=ost[ab][0:64, :], in0=acc[0][ab][0:64, :],
                                                                  in1=rc[ab][0:64, :], op=ALU.mult),
                          reads=[l_acc[0][ab], l_rc[ab]], writes=[l_ost[ab]])
                    P.add("dve", lambda e, ab=ab: e.tensor_tensor(out=ost[ab][64:128, :], in0=acc[1][ab][64:128, :],
                                                                  in1=rc[ab][64:128, :], op=ALU.mult),
                          reads=[l_acc[1][ab], l_rc[ab]], writes=[l_ost[ab]])
                    dma(k, k.OT[c][:, qb_ * 512:(qb_ + 1) * 512], ost[ab][:], reads=[l_ost[ab]], tile=l_ost[ab])
        P.barrier()


def phase_qkv_diff(k, w_qkv, rope_ap):
    P = k.P
    S = k.S
    with ExitStack() as st:
        wq = sb(k, st, "wq", [128, 8, 3072], BF16)
        wq_lt = lts(8)
        wv = w_qkv.rearrange("(c p) n -> p c n", p=128)
        for c in range(8):
            for q3 in range(2):
                load_cast(k, wq[:, c, q3 * 1536:(q3 + 1) * 1536], wv[:, c, q3 * 1536:(q3 + 1) * 1536], wq_lt[c])
        rp = [sb(k, st, "rp", [128, 2, 128], F32) for _ in range(2)]
        rp_lt = lts(2)
        ps_ = [pst(k, st, "qps", [128, 1024], F32) for _ in range(2)]
        ps_lt = lts(2)
        tq = [pst(k, st, "tq", [128, 8, 128], BF16) for _ in range(2)]
        tq_lt = lts(2)
        xb = [sb(k, st, "xb", [128, 1024], BF16) for _ in range(2)]
        l_xb = lts(2)
        xf = [sb(k, st, "xf", [128, 1024], F32) for _ in range(2)]
        l_xf = lts(2)
        tmp = [[sb(k, st, "tmp", [128, 16, 8], F32) for _ in range(4)] for _ in range(2)]
        l_tmp = [lts(4), lts(4)]
        stg = [[sb(k, st, "stg", [128, 8, 512], BF16) for _ in range(2)] for _ in range(2)]
        l_stg = [lts(2), lts(2)]
        XT = k.XT
        dstv = [k.QT.rearrange("c p s -> p c s"), k.KT.rearrange("c p s -> p c s")]
        cnt = 0
        pending = []
        for t in range(S // 128):
            rb = t % 2
            blk = t // 4
            sbb = blk % 2
            dma(k, rp[rb][:], rope_ap[t * 128:(t + 1) * 128], writes=[rp_lt[rb]], tile=rp_lt[rb])
            for part in range(3):
                b = cnt % 2
                cnt += 1
                for kc in range(8):
                    for n in range(2):
                        P.add("pe", lambda e, kc=kc, n=n, b=b, t=t, part=part: e.matmul(
                            ps_[b][:, n * 512:(n + 1) * 512], XT[:, kc, t * 128:(t + 1) * 128],
                            wq[:, kc, part * 1024 + n * 512:part * 1024 + (n + 1) * 512], start=(kc == 0),
                            stop=(kc == 7)), reads=[k.XT_lt[t], wq_lt[kc]], writes=[ps_lt[b]])
                if pending:
                    pending.pop(0)()
                if part == 2:
                    P.add("act", lambda e, b=b: e.copy(out=xb[b][:], in_=ps_[b][:]), reads=[ps_lt[b]], writes=[l_xb[b]])
                    dma(k, k.VA[t], xb[b][:], reads=[l_xb[b]], tile=l_xb[b])
                    continue
                P.add("act", lambda e, b=b: e.copy(out=xf[b][:], in_=ps_[b][:]), reads=[ps_lt[b]], writes=[l_xf[b]])
                xfv = xf[b][:].rearrange("p (m d) -> p m d", d=64)
                x1, x2 = xfv[:, :, 0:8], xfv[:, :, 8:16]
                cb = rp[rb][:, 0, :].rearrange("p (m d) -> p m d", d=8)
                sn = rp[rb][:, 1, :].rearrange("p (m d) -> p m d", d=8)
                tm = tmp[b]
                lt_ = l_tmp[b]
                for i_, (xx, tb_) in enumerate(((x1, cb), (x2, sn), (x2, cb), (x1, sn))):
                    P.add("dve", lambda e, xx=xx, tb_=tb_, i_=i_, tm=tm: e.tensor_tensor(
                        out=tm[i_][:], in0=xx, in1=tb_, op=ALU.mult),
                        reads=[l_xf[b], rp_lt[rb]], writes=[lt_[i_]])
                P.add("dve", lambda e, tm=tm, x1=x1: e.tensor_tensor(out=x1, in0=tm[0][:], in1=tm[1][:], op=ALU.subtract),
                      reads=[lt_[0], lt_[1], l_xf[b]], writes=[l_xf[b]])
                P.add("dve", lambda e, tm=tm, x2=x2: e.tensor_tensor(out=x2, in0=tm[2][:], in1=tm[3][:], op=ALU.add),
                      reads=[lt_[2], lt_[3], l_xf[b]], writes=[l_xf[b]])
                P.add("act", lambda e, b=b: e.copy(out=xb[b][:], in_=xf[b][:]), reads=[l_xf[b]], writes=[l_xb[b]])
                def tail(t=t, b=b, part=part, blk=blk, sbb=sbb):
                    for c in range(8):
                        P.add("pe", lambda e, b=b, c=c, part=part: e.transpose(
                            out=tq[part][:, c, :], in_=xb[b][:, c * 128:(c + 1) * 128], identity=k.ident[:]),
                            reads=[l_xb[b], k.ident_lt], writes=[tq_lt[part]])
                    o = (t % 4) * 128
                    eng = "act" if part == 0 else "dve"
                    if eng == "act":
                        P.add("act", lambda e, part=part, sbb=sbb, o=o: e.copy(out=stg[part][sbb][:, :, o:o + 128],
                                                                              in_=tq[part][:]),
                              reads=[tq_lt[part]], writes=[l_stg[part][sbb]])
                    else:
                        P.add("dve", lambda e, part=part, sbb=sbb, o=o: e.tensor_copy(out=stg[part][sbb][:, :, o:o + 128],
                                                                                     in_=tq[part][:]),
                              reads=[tq_lt[part]], writes=[l_stg[part][sbb]])
                    if t % 4 == 3 or t == S // 128 - 1:
                        dma(k, dstv[part][:, :, blk * 512:(blk + 1) * 512], stg[part][sbb][:],
                            reads=[l_stg[part][sbb]], tile=l_stg[part][sbb])
                pending.append(tail)
        while pending:
            pending.pop(0)()
        P.barrier()


def phase_attn_diff(k, lam_ap, subln_ap, lambda_init):
    P = k.P
    S = k.S
    nkt = S // 128
    nqb = S // 512
    with ExitStack() as st:
        lv = sb(k, st, "lv", [128, 4, 64], F32)
        lv_lt = LT()
        dma(k, lv[:].rearrange("p a d -> p (a d)"), lam_ap.partition_broadcast(128), writes=[lv_lt], tile=lv_lt)
        pr_ = sb(k, st, "lpr", [128, 2, 64], F32)
        ls = sb(k, st, "ls", [128, 2], F32)
        neglam = sb(k, st, "neglam", [128, 1], F32)
        l_pr, l_ls, l_nl = LT(), LT(), LT()
        P.add("dve", lambda e: e.tensor_tensor(out=pr_[:], in0=lv[:, 0:4:2, :], in1=lv[:, 1:4:2, :], op=ALU.mult),
              reads=[lv_lt], writes=[l_pr])
        P.add("dve", lambda e: e.tensor_reduce(out=ls[:], in_=pr_[:], axis=AX.X, op=ALU.add), reads=[l_pr],
              writes=[l_ls])
        P.add("act", lambda e: e.activation(out=ls[:], in_=ls[:], func=AF.Exp), reads=[l_ls], writes=[l_ls])
        P.add("dve", lambda e: e.scalar_tensor_tensor(out=neglam[:], in0=ls[:, 1:2], scalar=-lambda_init,
                                                      in1=ls[:, 0:1], op0=ALU.add, op1=ALU.subtract),
              reads=[l_ls], writes=[l_nl])
        sl = sb(k, st, "subln", [128, 1], F32)
        sl_lt = LT()
        dma(k, sl[:], subln_ap.unsqueeze(1), writes=[sl_lt], tile=sl_lt)
        P.add("dve", lambda e: e.tensor_single_scalar(out=sl[:], in_=sl[:], scalar=1.0 - lambda_init, op=ALU.mult),
              reads=[sl_lt], writes=[sl_lt])
        kth = [sb(k, st, "kth", [128, S], BF16) for _ in range(2)]
        vh = [sb(k, st, "vh", [128, nkt, 128], BF16) for _ in range(2)]
        qth = [sb(k, st, "qth", [128, S], BF16) for _ in range(2)]
        l_kth, l_vh, l_qth = lts(2), lts(2), lts(2)
        NSB = 4
        spsb = [pst(k, st, "sps", [128, 512], F32) for _ in range(NSB)]
        l_spsb = lts(NSB)
        scnt = [0]
        acc = [pst(k, st, "acc", [128, 512], F32) for _ in range(2)]
        rs_ = [pst(k, st, "rsum", [128, 512], F32) for _ in range(2)]
        l_acc, l_rs = lts(2), lts(2)
        NPB = 8
        pT = [[sb(k, st, "pT", [128, 512], BF16) for _ in range(NPB)] for _ in range(2)]
        l_pT = [lts(NPB), lts(NPB)]
        pp = [[sb(k, st, "pp", [128, 512], BF16) for _ in range(2)] for _ in range(2)]
        l_pp = [lts(2), lts(2)]
        qq = [[sb(k, st, "qq", [128, 512], BF16) for _ in range(2)] for _ in range(2)]
        l_qq = [lts(2), lts(2)]
        accS = [[sb(k, st, "accS", [128, 512], F32) for _ in range(2)] for _ in range(2)]
        rsS = [[sb(k, st, "rsS", [128, 512], F32) for _ in range(2)] for _ in range(2)]
        l_accS, l_rsS = [lts(2), lts(2)], [lts(2), lts(2)]
        o_ = [sb(k, st, "o", [128, 512], F32) for _ in range(2)]
        sqb = [sb(k, st, "sqb", [128, 512], BF16) for _ in range(2)]
        ssS = [sb(k, st, "ssS", [128, 512], F32) for _ in range(2)]
        l_o, l_sqb, l_ssS = lts(2), lts(2), lts(2)
        ost = [sb(k, st, "ost", [128, 512], BF16) for _ in range(2)]
        l_ost = lts(2)
        vav = k.VA.rearrange("t p (h d) -> p t h d", h=8)

        def load_h(h):
            hb = h % 2
            dma(k, kth[hb][:], k.KT[h], writes=[l_kth[hb]], tile=l_kth[hb])
            dma(k, vh[hb][:], vav[:, :, h, :], writes=[l_vh[hb]], tile=l_vh[hb])
            dma(k, qth[hb][:], k.QT[h], writes=[l_qth[hb]], tile=l_qth[hb])

        load_h(0)
        it = 0
        deferred = []

        def run_deferred(kt):
            while deferred and deferred[0][0] <= kt:
                deferred.pop(0)[1]()

        for h in range(8):
            hb = h % 2
            if h + 1 < 8:
                load_h(h + 1)
            for qb_ in range(nqb):
                par = it % 2
                it += 1

                def qk(kt, hb=hb, qb_=qb_):
                    sb_ = [(scnt[0] + m) % NSB for m in range(2)]
                    scnt[0] += 2
                    for m in range(2):
                        P.add("pe", lambda e, m=m, kt=kt, bk=sb_[m]: e.matmul(
                            spsb[bk][:], kth[hb][m * 64:(m + 1) * 64, kt * 128:(kt + 1) * 128],
                            qth[hb][m * 64:(m + 1) * 64, qb_ * 512:(qb_ + 1) * 512], start=True, stop=True),
                            reads=[l_kth[hb], l_qth[hb]], writes=[l_spsb[sb_[m]]])
                    for m in range(2):
                        pb = kt % NPB
                        P.add("act", lambda e, m=m, pb=pb, bk=sb_[m]: e.activation(
                            out=pT[m][pb][:], in_=spsb[bk][:], func=AF.Exp, scale=0.125),
                            reads=[l_spsb[sb_[m]]], writes=[l_pT[m][pb]])
                    if kt % 2 == 1:
                        pq = (kt // 2) % 2
                        for m in range(2):
                            P.add("dve", lambda e, m=m, pq=pq, kt=kt: e.tensor_tensor(
                                out=pp[m][pq][:], in0=pT[m][(kt - 1) % NPB][:], in1=pT[m][kt % NPB][:], op=ALU.add),
                                reads=[l_pT[m][(kt - 1) % NPB], l_pT[m][kt % NPB]], writes=[l_pp[m][pq]])
                        if kt % 4 == 3:
                            qi = (kt // 4) % 2
                            for m in range(2):
                                P.add("dve", lambda e, m=m, qi=qi: e.tensor_tensor(
                                    out=qq[m][qi][:], in0=pp[m][0][:], in1=pp[m][1][:], op=ALU.add),
                                    reads=[l_pp[m][0], l_pp[m][1]], writes=[l_qq[m][qi]])

                def pv(kt, hb=hb):
                    pb = kt % NPB
                    for m in range(2):
                        P.add("pe", lambda e, m=m, pb=pb, kt=kt: e.matmul(
                            acc[m][:], vh[hb][:, kt, :], pT[m][pb][:], start=(kt == 0), stop=(kt == nkt - 1)),
                            reads=[l_vh[hb], l_pT[m][pb]], writes=[l_acc[m]])
                    if kt % 4 == 3:
                        qi = (kt // 4) % 2

                        def ones_mm(qi=qi, kt=kt):
                            for m in range(2):
                                P.add("pe", lambda e, m=m: e.matmul(
                                    rs_[m][:], k.ones[:], qq[m][qi][:], start=(kt == 3), stop=(kt == nkt - 1)),
                                    reads=[k.ones_lt, l_qq[m][qi]], writes=[l_rs[m]])
                        pend_ones.append((kt + 3, ones_mm))

                pend_ones = []
                for kt in range(nkt + 1):
                    if kt < nkt:
                        qk(kt)
                    if kt >= 1:
                        pv(kt - 1)
                    while pend_ones and pend_ones[0][0] <= kt - 1:
                        pend_ones.pop(0)[1]()
                    run_deferred(kt)
                while pend_ones:
                    pend_ones.pop(0)[1]()
                run_deferred(10 ** 9)
                for m in range(2):
                    P.add("dve", lambda e, m=m, par=par: e.tensor_copy(out=accS[par][m][:], in_=acc[m][:]),
                          reads=[l_acc[m]], writes=[l_accS[par][m]])
                    P.add("dve", lambda e, m=m, par=par: e.tensor_copy(out=rsS[par][m][:], in_=rs_[m][:]),
                          reads=[l_rs[m]], writes=[l_rsS[par][m]])
                def d_r0(par=par):
                    P.add("dve", lambda e: e.reciprocal(out=rsS[par][0][:], in_=rsS[par][0][:]),
                          reads=[l_rsS[par][0]], writes=[l_rsS[par][0]])

                def d_r1(par=par):
                    P.add("dve", lambda e: e.reciprocal(out=rsS[par][1][:], in_=rsS[par][1][:]),
                          reads=[l_rsS[par][1]], writes=[l_rsS[par][1]])
                    P.add("pool", lambda e: e.tensor_tensor(out=accS[par][0][:], in0=accS[par][0][:],
                                                            in1=rsS[par][0][:], op=ALU.mult),
                          reads=[l_accS[par][0], l_rsS[par][0]], writes=[l_accS[par][0]])

                def d_m1(par=par):
                    P.add("pool", lambda e: e.tensor_tensor(out=accS[par][1][:], in0=accS[par][1][:],
                                                            in1=rsS[par][1][:], op=ALU.mult),
                          reads=[l_accS[par][1], l_rsS[par][1]], writes=[l_accS[par][1]])

                def d_o(par=par):
                    P.add("dve", lambda e: e.scalar_tensor_tensor(out=o_[par][:], in0=accS[par][1][:],
                                                                  scalar=neglam[:], in1=accS[par][0][:],
                                                                  op0=ALU.mult, op1=ALU.add),
                          reads=[l_accS[par][0], l_accS[par][1], l_nl], writes=[l_o[par]])
                    P.add("pool", lambda e: e.tensor_tensor(out=sqb[par][:], in0=o_[par][:], in1=o_[par][:],
                                                            op=ALU.mult), reads=[l_o[par]], writes=[l_sqb[par]])

                def d_ss(par=par):
                    bk = scnt[0] % NSB
                    scnt[0] += 1
                    P.add("pe", lambda e: e.matmul(spsb[bk][:], k.ones[:], sqb[par][:], start=True, stop=True),
                          reads=[k.ones_lt, l_sqb[par]], writes=[l_spsb[bk]])
                    P.add("act", lambda e: e.activation(out=ssS[par][:], in_=spsb[bk][:], func=AF.Ln, bias=k.eps_rms[:],
                                                        scale=1.0 / 128), reads=[l_spsb[bk], k.eps_lt],
                          writes=[l_ssS[par]])
                    P.add("act", lambda e: e.activation(out=ssS[par][:], in_=ssS[par][:], func=AF.Exp, scale=-0.5),
                          reads=[l_ssS[par]], writes=[l_ssS[par]])

                def d_on(par=par):
                    P.add("pool", lambda e: e.tensor_tensor(out=o_[par][:], in0=o_[par][:], in1=ssS[par][:],
                                                            op=ALU.mult), reads=[l_o[par], l_ssS[par]],
                          writes=[l_o[par]])

                def d_out(par=par, h=h, qb_=qb_):
                    P.add("act", lambda e: e.activation(out=ost[par][:], in_=o_[par][:], func=AF.Identity, scale=sl[:]),
                          reads=[l_o[par], sl_lt], writes=[l_ost[par]])
                    dma(k, k.OT[h][:, qb_ * 512:(qb_ + 1) * 512], ost[par][:], reads=[l_ost[par]], tile=l_ost[par])

                for trig, fn in ((1, d_r0), (4, d_r1), (7, d_m1), (9, d_o), (16, d_ss), (21, d_on), (24, d_out)):
                    deferred.append((min(trig, nkt - 1), fn))
        run_deferred(10 ** 9)
        P.barrier()


def phase_fnet(k, cc_ap, dft_ap):
    P = k.P
    S = k.S
    nt = S // 128
    nkb = S // 512
    with ExitStack() as st:
        ucs = sb(k, st, "ucs", [128, nt, 4, 512], BF16)
        ucs_lt = lts(nt)
        with ExitStack() as st1:
            ccs = sb(k, st1, "ccs", [128, 2, 512], BF16)
            ccs_lt = LT()
            load_cast(k, ccs[:], cc_ap.rearrange("(c p) n -> p c n", p=128), ccs_lt)
            p1 = [pst(k, st1, "p1", [128, 512], F32) for _ in range(4)]
            p1_lt = lts(4)
            XT = k.XT
            cnt = 0
            for t in range(nt):
                for gr in range(4):
                    b = cnt % 4
                    cnt += 1
                    for kc in range(2):
                        P.add("pe", lambda e, b=b, kc=kc, gr=gr, t=t: e.matmul(
                            p1[b][:], XT[:, gr * 2 + kc, t * 128:(t + 1) * 128], ccs[:, kc, :], start=(kc == 0),
                            stop=(kc == 1)), reads=[k.XT_lt[t], ccs_lt], writes=[p1_lt[b]])
                    if gr % 2 == 0:
                        P.add("act", lambda e, b=b, gr=gr, t=t: e.copy(out=ucs[:, t, gr, :], in_=p1[b][:]),
                              reads=[p1_lt[b]], writes=[ucs_lt[t]])
                    else:
                        P.add("dve", lambda e, b=b, gr=gr, t=t: e.tensor_copy(out=ucs[:, t, gr, :], in_=p1[b][:]),
                              reads=[p1_lt[b]], writes=[ucs_lt[t]])
            P.barrier()
        nq = 4 if nt >= 4 else 1
        jq = nt // nq
        dq = [k.XT_flat[:, q * (jq * 1024):(q + 1) * (jq * 1024)].rearrange("p (j c n) -> p j c n", c=2, n=512)
              for q in range(nq)]
        dq_lt = lts(nq)
        p2 = [pst(k, st, "p2", [128, 512], F32) for _ in range(2)]
        p2_lt = lts(2)
        fst = [sb(k, st, "fst", [128, 512], BF16) for _ in range(2)]
        fst_lt = lts(2)

        def load_d(kb, q):
            dma(k, dq[q], dft_ap[kb, :, q * jq:(q + 1) * jq], writes=[dq_lt[q]], tile=dq_lt[q])

        for q in range(nq):
            load_d(0, q)
        cnt = 0
        for kb in range(nkb):
            for fc in range(8):
                gr, hh = fc // 2, fc % 2
                b = cnt % 2
                cnt += 1
                for jt in range(nt):
                    q, jj = jt // jq, jt % jq
                    for cs in range(2):
                        P.add("pe", lambda e, b=b, jt=jt, q=q, jj=jj, cs=cs, gr=gr, hh=hh: e.matmul(
                            p2[b][:], ucs[:, jt, gr, cs * 256 + hh * 128:cs * 256 + (hh + 1) * 128], dq[q][:, jj, cs, :],
                            start=(jt == 0 and cs == 0), stop=(jt == nt - 1 and cs == 1)),
                            reads=[ucs_lt[jt], dq_lt[q]], writes=[p2_lt[b]])
                    if fc == 7 and jj == jq - 1 and kb + 1 < nkb:
                        load_d(kb + 1, q)
                P.add("act", lambda e, b=b: e.activation(out=fst[b][:], in_=p2[b][:], func=AF.Identity, scale=1.0 / math.sqrt(S * 256.0)),
                      reads=[p2_lt[b]], writes=[fst_lt[b]])
                dma(k, k.OT[fc][:, kb * 512:(kb + 1) * 512], fst[b][:], reads=[fst_lt[b]], tile=fst_lt[b])
        P.barrier()


def lambda_init_of(layer_idx):
    return 0.8 - 0.6 * math.exp(-0.3 * layer_idx)


def build(S, layers=(0, 1, 2, 3)):
    nc = bass.Bass("TRN2", target_bir_lowering=False)
    k = K()
    k.nc = nc
    k.S = S
    nt = S // 128

    def din(name, shape, dt=F32):
        return nc.dram_tensor(name, list(shape), dt, kind="ExternalInput").ap()

    x_in = din("x", [S, D])
    ident_in = din("ident", [128, 128])
    W = {}
    for l in layers:
        p = "l%d_" % l
        kind = l % 3
        if kind == 0:
            W[p + "wqkv"] = din(p + "wqkv", [D, 1536])
            W[p + "gain"] = din(p + "gain", [1280])
            W[p + "wo"] = din(p + "wo", [D, D])
        elif kind == 1:
            W[p + "wo"] = din(p + "wo", [D, D])
            W[p + "bo"] = din(p + "bo", [D])
        else:
            W[p + "wqkv"] = din(p + "wqkv", [D, 3072])
            W[p + "lam"] = din(p + "lam", [256])
            W[p + "subln"] = din(p + "subln", [128])
            W[p + "wo"] = din(p + "wo", [D, D])
        for nm, shp in (("ln1_g", [D]), ("ln1_b", [D]), ("wup", [D, 2 * DFF]), ("cp", [128, NJ, 8]),
                        ("wdown", [DFF, D]), ("ln2_g", [D]), ("ln2_b", [D])):
            W[p + nm] = din(p + nm, shp)
    kinds = set(l % 3 for l in layers)
    if 0 in kinds:
        rope_a = din("rope_a", [S, 2, 64])
    if 2 in kinds:
        rope_c = din("rope_c", [S, 2, 128])
    if 1 in kinds:
        cc_in = din("cc", [256, 512])
        dft_in = din("dft", [S // 512, 128, nt, 2, 512], BF16)
    y_out = nc.dram_tensor("y", [S, D], F32, kind="ExternalOutput").ap()

    def scr(name, shape, dt):
        return nc.dram_tensor(name, list(shape), dt, kind="Internal").ap()

    k.X32 = scr("x32s", [S, D], F32)
    k.AT = scr("at", [NJ, 128, S], BF16)
    k.QT = scr("qt", [8, 128, S], BF16)
    k.KT = scr("kt", [8, 128, S], BF16)
    k.VA = scr("va", [nt, 128, 1024], BF16)
    k.OT = scr("ot", [8, 128, S], BF16)

    with ExitStack() as gst:
        P = Prog(nc)
        k.P = P
        P.setup_sems(gst, 80)
        k.XT = gst.enter_context(nc.sbuf_tensor("XT_sb", [128, 8, S], BF16))
        k.XT_flat = k.XT[:].rearrange("p c s -> p (c s)")
        k.XT_lt = lts(nt, "XT")
        k.ident = gst.enter_context(nc.sbuf_tensor("ident_sb", [128, 128], BF16))
        k.ident_lt = LT()
        load_cast(k, k.ident[:], ident_in, k.ident_lt)
        k.ones = gst.enter_context(nc.sbuf_tensor("ones_sb", [128, 128], BF16))
        k.ones_lt = LT()
        P.add("pool", lambda e: e.memset(k.ones[:], 1.0), writes=[k.ones_lt])
        k.eps_ln = gst.enter_context(nc.sbuf_tensor("eps_ln", [128, 1], F32))
        k.eps_rms = gst.enter_context(nc.sbuf_tensor("eps_rms", [128, 1], F32))
        k.eps_lt = LT()
        P.add("pool", lambda e: e.memset(k.eps_ln[:], LN_EPS), writes=[k.eps_lt])
        P.add("pool", lambda e: e.memset(k.eps_rms[:], RMS_EPS), writes=[k.eps_lt])
        k.mhalf = gst.enter_context(nc.sbuf_tensor("mhalf", [128, 1], F32))
        P.add("pool", lambda e: e.memset(k.mhalf[:], -0.5), writes=[k.eps_lt])
        k.need_xt = True
        phase_prep(k, x_in)
        x_cur = x_in
        for li, l in enumerate(layers):
            p = "l%d_" % l
            kind = l % 3
            last = (li == len(layers) - 1)
            with ExitStack() as wst:
                pre = None
                if kind == 0:
                    phase_qkv_gqa(k, W[p + "wqkv"], W[p + "gain"], rope_a)
                    pre = load_wo(k, wst, W[p + "wo"], None)
                    phase_attn_gqa(k)
                    bias = None
                elif kind == 1:
                    phase_fnet(k, cc_in, dft_in)
                    bias = W[p + "bo"]
                else:
                    phase_qkv_diff(k, W[p + "wqkv"], rope_c)
                    pre = load_wo(k, wst, W[p + "wo"], None)
                    phase_attn_diff(k, W[p + "lam"], W[p + "subln"], lambda_init_of(l))
                    bias = None
                phase_proj_postnorm(k, W[p + "wo"], bias, W[p + "ln1_g"], W[p + "ln1_b"], x_cur, k.X32, pre=pre)
            x_cur = k.X32
            phase_ffn_up(k, W[p + "wup"], W[p + "cp"])
            k.need_xt = not last
            phase_ffn_down(k, W[p + "wdown"], W[p + "ln2_g"], W[p + "ln2_b"], x_cur, y_out if last else k.X32)
        with nc.Block() as block:
            P.emit(block)
    return nc


def _rope_tables(S):
    def cs(pos, dim, theta):
        inv = theta ** (-np.arange(0, dim, 2, dtype=np.float32) / np.float32(dim))
        ang = pos.astype(np.float32)[:, None] * inv[None, :].astype(np.float32)
        return np.cos(ang).astype(np.float32), np.sin(ang).astype(np.float32)
    rows = S // 64
    t_row = np.repeat(np.arange(rows), 64)
    t_col = np.tile(np.arange(64), rows)
    cr, sr = cs(t_row, 32, 10000.0)
    cc, sc = cs(t_col, 32, 10000.0)
    C = np.concatenate([cr, cr, cc, cc], axis=1)
    Sg = np.concatenate([-sr, sr, -sc, sc], axis=1)
    rope_a = np.stack([C, Sg], axis=1).astype(np.float32)
    c8, s8 = cs(np.arange(S), 16, 500000.0)
    rope_c = np.stack([np.tile(c8, (1, 16)), np.tile(s8, (1, 16))], axis=1).astype(np.float32)
    return rope_a, rope_c


def _dft_tables(S):
    n = np.arange(256)
    ang = 2.0 * np.pi * ((n[:, None] * n[None, :]) % 256) / 256.0
    cc = np.concatenate([np.cos(ang), -np.sin(ang)], axis=1).astype(np.float32)
    j = np.arange(S, dtype=np.int64)
    m = (j[:, None] * j[None, :]) % S
    ang = (2.0 * np.pi / S) * m
    Cs = np.cos(ang).astype(ml_dtypes.bfloat16)
    Ss = np.sin(ang).astype(ml_dtypes.bfloat16)
    nt = S // 128
    d = np.stack([Cs, Ss], axis=0).reshape(2, nt, 128, S // 512, 512)
    d = np.ascontiguousarray(d.transpose(3, 2, 1, 0, 4))
    return cc, d


def make_in_maps(inputs, S, layers=(0, 1, 2, 3), n_cores=8):
    f = lambda a: np.ascontiguousarray(np.asarray(a, dtype=np.float32))
    shared = {"ident": np.eye(128, dtype=np.float32)}
    kinds = set(l % 3 for l in layers)
    rope_a, rope_c = _rope_tables(S)
    if 0 in kinds:
        shared["rope_a"] = rope_a
    if 2 in kinds:
        shared["rope_c"] = rope_c
    if 1 in kinds:
        cc, d = _dft_tables(S)
        shared["cc"] = cc
        shared["dft"] = d
    for l in layers:
        p = "l%d_" % l
        kind = l % 3
        if kind == 0:
            shared[p + "wqkv"] = f(inputs[p + "a_wqkv"])
            qn, kn = f(inputs[p + "a_qnorm"]), f(inputs[p + "a_knorm"])
            shared[p + "gain"] = np.concatenate([np.tile(qn, 16), np.tile(kn, 4)]).astype(np.float32)
            shared[p + "wo"] = f(inputs[p + "a_wo"])
        elif kind == 1:
            shared[p + "wo"] = f(inputs[p + "f_wo"])
            shared[p + "bo"] = f(inputs[p + "f_bo"])
        else:
            shared[p + "wqkv"] = f(inputs[p + "c_wqkv"])
            shared[p + "lam"] = np.concatenate([f(inputs[p + "c_lq1"]), f(inputs[p + "c_lk1"]),
                                                f(inputs[p + "c_lq2"]), f(inputs[p + "c_lk2"])]).astype(np.float32)
            shared[p + "subln"] = f(inputs[p + "c_subln"])
            shared[p + "wo"] = f(inputs[p + "c_wo"])
        cw, cb = f(inputs[p + "ffn_conv_w"]), f(inputs[p + "ffn_conv_b"])
        cp = np.zeros((128, NJ, 8), np.float32)
        for hf in range(2):
            sl = slice(hf * DFF, (hf + 1) * DFF)
            cp[:, :, hf * 4 + 0] = cw[0, sl].reshape(NJ, 128).T
            cp[:, :, hf * 4 + 1] = cw[1, sl].reshape(NJ, 128).T
            cp[:, :, hf * 4 + 2] = cw[2, sl].reshape(NJ, 128).T
            cp[:, :, hf * 4 + 3] = cb[sl].reshape(NJ, 128).T
        shared[p + "cp"] = cp
        shared[p + "wup"] = f(inputs[p + "ffn_wup"])
        shared[p + "wdown"] = f(inputs[p + "ffn_wdown"])
        for nm in ("ln1_g", "ln1_b", "ln2_g", "ln2_b"):
            shared[p + nm] = f(inputs[p + nm])
    x = f(inputs["x"])
    maps = []
    for c in range(n_cores):
        m = dict(shared)
        m["x"] = np.ascontiguousarray(x[c])
        maps.append(m)
    return maps


_CACHE = {}


def kernel(**inputs):
    x = np.asarray(inputs["x"])
    B, S, _ = x.shape
    key = (S,)
    if key not in _CACHE:
        _CACHE[key] = build(S)
    nc = _CACHE[key]
    maps = make_in_maps(inputs, S, n_cores=B)
    res = run_bass_kernel_spmd(nc, maps, core_ids=list(range(B)))
    out = np.stack([np.asarray(r["y"], dtype=np.float32) for r in res.results], axis=0)
    return out
```

```python
import math
from contextlib import ExitStack
import numpy as np
import ml_dtypes
import concourse.bass as bass
import concourse.mybir as mybir
from concourse.bass_utils import run_bass_kernel_spmd

F32 = mybir.dt.float32
BF16 = mybir.dt.bfloat16
AF = mybir.ActivationFunctionType
ALU = mybir.AluOpType
AX = mybir.AxisListType

D = 1024
DFF = 2816
NJ = DFF // 128
DEPTH = 4
ALPHA = (2.0 * DEPTH) ** 0.25
LN_EPS = 1e-5
RMS_EPS = 1e-6
ENGINES = ("pe", "act", "dve", "pool", "sp")


class LT:
    __slots__ = ("name", "last_w", "readers", "dsem")

    def __init__(self, name=""):
        self.name = name
        self.last_w = None
        self.readers = []
        self.dsem = None


def lts(n, name=""):
    return [LT(name + str(i)) for i in range(n)]


class DmaSem:
    __slots__ = ("sem", "count")

    def __init__(self, sem):
        self.sem = sem
        self.count = 0


class Op:
    __slots__ = ("eng", "fn", "pos", "needs_inc", "tick", "is_dma", "dsem", "dtarget",
                 "waits_eng", "waits_dma", "clock")

    def __init__(self, eng, fn):
        self.eng = eng
        self.fn = fn
        self.pos = -1
        self.needs_inc = False
        self.tick = 0
        self.is_dma = False
        self.dsem = None
        self.dtarget = 0
        self.waits_eng = {}
        self.waits_dma = {}
        self.clock = None


class Prog:
    def __init__(self, nc):
        self.nc = nc
        self.ops = {e: [] for e in ENGINES}
        self.known = {e: {x: -1 for x in ENGINES} for e in ENGINES}
        self.known_dma = {e: {} for e in ENGINES}
        self.free_dsems = []
        self.all_dsems = []
        self.esem = {}
        self.phase_tiles = []
        self.swq = []

    def setup_sems(self, stack, n_dma):
        for e in ("pe", "act", "dve", "pool"):
            self.esem[e] = stack.enter_context(self.nc.semaphore("es_" + e))
        self.free_dsems = {"pool": [], "sp": []}
        for i in range(n_dma):
            d = DmaSem(stack.enter_context(self.nc.semaphore("ds_%d" % i)))
            self.free_dsems["pool" if i < 30 else "sp"].append(d)
            self.all_dsems.append(d)
        self.dsem_owner = {}

    def _get_dsem(self, tile, eng):
        if tile.dsem is None:
            tile.dsem = self.free_dsems[eng].pop()
            self.dsem_owner[tile.dsem] = eng
            self.phase_tiles.append(tile)
        assert self.dsem_owner[tile.dsem] == eng
        return tile.dsem

    def _add_dep(self, op, dep):
        if dep is None or dep is op:
            return
        e = op.eng
        if dep.is_dma:
            if self.known_dma[e].get(dep.dsem, 0) >= dep.dtarget:
                return
            if op.waits_dma.get(dep.dsem, 0) < dep.dtarget:
                op.waits_dma[dep.dsem] = dep.dtarget
            return
        x = dep.eng
        if x == "pe" and e == "pe":
            return
        if self.known[e][x] >= dep.pos:
            return
        cur = op.waits_eng.get(x)
        if cur is None or cur.pos < dep.pos:
            op.waits_eng[x] = dep

    def add(self, eng, fn, reads=(), writes=(), dma_tile=None):
        op = Op(eng, fn)
        for t in reads:
            self._add_dep(op, t.last_w)
        for t in writes:
            self._add_dep(op, t.last_w)
            for r in t.readers:
                self._add_dep(op, r)
        kn = self.known[eng]
        for x, dep in op.waits_eng.items():
            dep.needs_inc = True
            if kn[x] < dep.pos:
                kn[x] = dep.pos
            if dep.clock is not None:
                for y, p in dep.clock.items():
                    if kn[y] < p:
                        kn[y] = p
        for ds, tgt in op.waits_dma.items():
            self.known_dma[eng][ds] = tgt
        op.pos = len(self.ops[eng])
        self.ops[eng].append(op)
        if dma_tile is not None:
            op.is_dma = True
            op.dsem = self._get_dsem(dma_tile, eng)
            op.dsem.count += 16
            op.dtarget = op.dsem.count
        else:
            op.clock = dict(kn)
            op.clock[eng] = op.pos
        for t in reads:
            t.readers.append(op)
        for t in writes:
            t.last_w = op
            t.readers = []
        return op

    def barrier(self):
        lasts = {}
        for e in ("pe", "act", "dve", "pool"):
            for op in reversed(self.ops[e]):
                if op.fn is not None and not op.is_dma:
                    lasts[e] = op
                    break
        dtargets = {d: d.count for d in self.all_dsems if d.count > 0}
        for e in ENGINES:
            op = Op(e, None)
            for x, dep in lasts.items():
                if x == e:
                    continue
                if self.known[e][x] < dep.pos:
                    op.waits_eng[x] = dep
                    dep.needs_inc = True
                    self.known[e][x] = dep.pos
            for d, tgt in dtargets.items():
                if self.known_dma[e].get(d, 0) < tgt:
                    op.waits_dma[d] = tgt
                    self.known_dma[e][d] = tgt
            op.pos = len(self.ops[e])
            self.ops[e].append(op)
            op.clock = dict(self.known[e])
        for t in self.phase_tiles:
            if t.dsem is not None:
                self.free_dsems[self.dsem_owner[t.dsem]].append(t.dsem)
                t.dsem = None
        self.phase_tiles = []

    def emit(self, block):
        for e in ("pe", "act", "dve", "pool"):
            t = 0
            for op in self.ops[e]:
                if op.needs_inc:
                    t += 1
                    op.tick = t
        esem = self.esem

        def run(e, eng):
            for op in self.ops[e]:
                for x, dep in op.waits_eng.items():
                    eng.wait_ge(esem[x], dep.tick)
                for ds, tgt in op.waits_dma.items():
                    eng.wait_ge(ds.sem, tgt)
                if op.fn is None:
                    continue
                ins = op.fn(eng)
                if op.is_dma:
                    ins.then_inc(op.dsem.sem, 16)
                elif op.needs_inc:
                    ins.then_inc(esem[e], 1)

        @block.tensor
        def _(eng):
            run("pe", eng)

        @block.scalar
        def _(eng):
            run("act", eng)

        @block.vector
        def _(eng):
            run("dve", eng)

        @block.gpsimd
        def _(eng):
            run("pool", eng)

        @block.sync
        def _(eng):
            run("sp", eng)


class K:
    pass


_uid = [0]


def _nm(s):
    _uid[0] += 1
    return "%s_%d" % (s, _uid[0])


def sb(k, st, name, shape, dt):
    return st.enter_context(k.nc.sbuf_tensor(_nm(name), list(shape), dt))


def pst(k, st, name, shape, dt):
    return st.enter_context(k.nc.psum_tensor(_nm(name), list(shape), dt))


def load_cast(k, dst_ap, src_ap, lt):
    P = k.P
    if len(P.swq) >= 8:
        old = P.swq.pop(0)
        w = Op("pool", None)
        if P.known_dma["pool"].get(old.dsem, 0) < old.dtarget:
            w.waits_dma[old.dsem] = old.dtarget
            P.known_dma["pool"][old.dsem] = old.dtarget
        w.pos = len(P.ops["pool"])
        P.ops["pool"].append(w)
    op = P.add("pool", lambda e: e.dma_start(out=dst_ap, in_=src_ap), writes=[lt], dma_tile=lt)
    P.swq.append(op)


def dma(k, dst_ap, src_ap, reads=(), writes=(), tile=None):
    k.P.add("sp", lambda e: e.dma_start(out=dst_ap, in_=src_ap), reads=reads, writes=writes, dma_tile=tile)


class PostNorm:
    NB = 3

    def __init__(self, k, st, g_ap, b_ap, x_src, x_dst, nt):
        self.k = k
        self.nt = nt
        self.x_src = x_src
        self.x_dst = x_dst
        NB = self.NB
        self.G = sb(k, st, "lnG", [128, D], F32)
        self.B = sb(k, st, "lnB", [128, D], F32)
        self.G_lt, self.B_lt = LT(), LT()
        dma(k, self.G[:], g_ap.partition_broadcast(128), writes=[self.G_lt], tile=self.G_lt)
        dma(k, self.B[:], b_ap.partition_broadcast(128), writes=[self.B_lt], tile=self.B_lt)
        self.xin = [sb(k, st, "pn_x", [128, D], F32) for _ in range(NB)]
        self.u = [sb(k, st, "pn_u", [128, D], F32) for _ in range(NB)]
        self.y = [sb(k, st, "pn_y", [128, D], F32) for _ in range(NB)]
        self.ybf = [sb(k, st, "pn_yb", [128, D], BF16) for _ in range(2)]
        self.stt = [sb(k, st, "pn_st", [128, 2, 6], F32) for _ in range(NB)]
        self.mv = [sb(k, st, "pn_mv", [128, 2], F32) for _ in range(NB)]
        self.rstd = [sb(k, st, "pn_rs", [128, 1], F32) for _ in range(NB)]
        self.nmr = [sb(k, st, "pn_nm", [128, 1], F32) for _ in range(NB)]
        self.tp = [pst(k, st, "pn_tp", [128, 8, 128], BF16) for _ in range(2)]
        self.l_xin, self.l_u, self.l_y = lts(NB), lts(NB), lts(NB)
        self.l_st, self.l_mv, self.l_rstd, self.l_nmr = lts(NB), lts(NB), lts(NB), lts(NB)
        self.l_ybf, self.l_tp = lts(2), lts(2)
        self.subs = {}
        self.prefetch(0)
        if nt > 1:
            self.prefetch(1)

    def prefetch(self, t):
        k = self.k
        b = t % self.NB
        dma(k, self.xin[b][:], self.x_src[t * 128:(t + 1) * 128, :], writes=[self.l_xin[b]], tile=self.l_xin[b])

    def pre(self, t):
        pass

    def step(self, t, sub_ps, sub_lt):
        self._s1(t, sub_ps, sub_lt)
        if t - 1 >= 0:
            self._s2(t - 1)
        if t - 2 >= 0:
            self._s3a(t - 2)
            self._s3b(t - 2)
        if t + 2 < self.nt:
            self.prefetch(t + 2)

    def drain(self):
        nt = self.nt
        if nt - 2 >= 0:
            self._s3a(nt - 2)
            self._s3b(nt - 2)
        self._s2(nt - 1)
        self._s3a(nt - 1)
        self._s3b(nt - 1)

    def _s1(self, t, sub_ps, sub_lt):
        k = self.k
        P = k.P
        b = t % self.NB
        xin, u, stt, mv, rstd, nmr = self.xin[b], self.u[b], self.stt[b], self.mv[b], self.rstd[b], self.nmr[b]
        l_xin, l_u, l_st, l_mv, l_rstd, l_nmr = (self.l_xin[b], self.l_u[b], self.l_st[b], self.l_mv[b],
                                                  self.l_rstd[b], self.l_nmr[b])
        P.add("dve", lambda e: e.scalar_tensor_tensor(out=u[:], in0=xin[:], scalar=ALPHA, in1=sub_ps,
                                                      op0=ALU.mult, op1=ALU.add),
              reads=[l_xin, sub_lt], writes=[l_u])
        for c in range(2):
            P.add("dve", lambda e, c=c: e.bn_stats(out=stt[:, c, :], in_=u[:, c * 512:(c + 1) * 512]),
                  reads=[l_u], writes=[l_st])
        P.add("dve", lambda e: e.bn_aggr(out=mv[:], in_=stt[:]), reads=[l_st], writes=[l_mv])
        P.add("pool", lambda e: e.tensor_scalar(out=rstd[:], in0=mv[:, 1:2], scalar1=LN_EPS, scalar2=1.0,
                                                op0=ALU.add, op1=ALU.mult), reads=[l_mv], writes=[l_rstd])
        P.add("pool", lambda e: e.tensor_tensor(out=rstd[:], in0=rstd[:], in1=k.mhalf[:], op=ALU.pow),
              reads=[l_rstd, k.eps_lt], writes=[l_rstd])
        P.add("pool", lambda e: e.tensor_scalar(out=nmr[:], in0=mv[:, 0:1], scalar1=-1.0, scalar2=1.0,
                                                op0=ALU.mult, op1=ALU.mult), reads=[l_mv], writes=[l_nmr])
        P.add("pool", lambda e: e.tensor_tensor(out=nmr[:], in0=nmr[:], in1=rstd[:], op=ALU.mult),
              reads=[l_nmr, l_rstd], writes=[l_nmr])

    def _s2(self, t):
        k = self.k
        P = k.P
        b = t % self.NB
        u, y, rstd, nmr = self.u[b], self.y[b], self.rstd[b], self.nmr[b]
        l_u, l_y, l_rstd, l_nmr = self.l_u[b], self.l_y[b], self.l_rstd[b], self.l_nmr[b]
        P.add("act", lambda e: e.activation(out=u[:], in_=u[:], func=AF.Identity, bias=nmr[:], scale=rstd[:]),
              reads=[l_u, l_nmr, l_rstd], writes=[l_u])
        P.add("dve", lambda e: e.tensor_tensor(out=u[:], in0=u[:], in1=self.G[:], op=ALU.mult),
              reads=[l_u, self.G_lt], writes=[l_u])
        P.add("pool", lambda e: e.tensor_tensor(out=y[:], in0=u[:], in1=self.B[:], op=ALU.add),
              reads=[l_u, self.B_lt], writes=[l_y])
        dma(k, self.x_dst[t * 128:(t + 1) * 128, :], y[:], reads=[l_y], tile=l_y)

    def _s3a(self, t):
        k = self.k
        P = k.P
        if not k.need_xt:
            return
        b = t % self.NB
        y, l_y = self.y[b], self.l_y[b]
        ybf, l_ybf = self.ybf[t % 2], self.l_ybf[t % 2]
        P.add("act", lambda e: e.copy(out=ybf[:], in_=y[:]), reads=[l_y], writes=[l_ybf])

    def _s3b(self, t):
        k = self.k
        P = k.P
        if not k.need_xt:
            return
        ybf, l_ybf = self.ybf[t % 2], self.l_ybf[t % 2]
        tp, l_tp = self.tp[t % 2], self.l_tp[t % 2]
        for c in range(8):
            P.add("pe", lambda e, c=c: e.transpose(out=tp[:, c, :], in_=ybf[:, c * 128:(c + 1) * 128],
                                                   identity=k.ident[:]),
                  reads=[l_ybf, k.ident_lt], writes=[l_tp])
        XT = k.XT
        P.add("dve", lambda e: e.tensor_copy(out=XT[:, :, t * 128:(t + 1) * 128], in_=tp[:]),
              reads=[l_tp], writes=[k.XT_lt[t]])


def phase_prep(k, x_in):
    P = k.P
    S = k.S
    with ExitStack() as st:
        xin = [sb(k, st, "pp_x", [128, D], F32) for _ in range(2)]
        xbf = [sb(k, st, "pp_xb", [128, D], BF16) for _ in range(2)]
        tp = [pst(k, st, "pp_tp", [128, 8, 128], BF16) for _ in range(2)]
        l_x, l_xb, l_tp = lts(2), lts(2), lts(2)
        for t in range(S // 128):
            b = t % 2
            dma(k, xin[b][:], x_in[t * 128:(t + 1) * 128, :], writes=[l_x[b]], tile=l_x[b])
            P.add("act", lambda e, b=b: e.copy(out=xbf[b][:], in_=xin[b][:]), reads=[l_x[b]], writes=[l_xb[b]])
            for c in range(8):
                P.add("pe", lambda e, b=b, c=c: e.transpose(out=tp[b][:, c, :], in_=xbf[b][:, c * 128:(c + 1) * 128],
                                                            identity=k.ident[:]),
                      reads=[l_xb[b], k.ident_lt], writes=[l_tp[b]])
            XT = k.XT
            P.add("dve", lambda e, b=b, t=t: e.tensor_copy(out=XT[:, :, t * 128:(t + 1) * 128], in_=tp[b][:]),
                  reads=[l_tp[b]], writes=[k.XT_lt[t]])
        P.barrier()


def load_wo(k, st, w_ap, bias_ap):
    wo = sb(k, st, "wo", [128, 8, D], BF16)
    wo_lt = lts(8)
    wv = w_ap.rearrange("(c p) n -> p c n", p=128)
    for c in range(8):
        load_cast(k, wo[:, c, :], wv[:, c, :], wo_lt[c])
    bo, bo_lt = None, None
    if bias_ap is not None:
        bo = sb(k, st, "bo", [1, D], BF16)
        bo_lt = LT()
        load_cast(k, bo[:], bias_ap.unsqueeze(0), bo_lt)
    return wo, wo_lt, bo, bo_lt


def phase_proj_postnorm(k, w_ap, bias_ap, g_ap, b_ap, x_src, x_dst, pre=None):
    P = k.P
    S = k.S
    with ExitStack() as st:
        if pre is None:
            pre = load_wo(k, st, w_ap, bias_ap)
        wo, wo_lt, bo, bo_lt = pre
        ob = [sb(k, st, "ob", [128, 8, 512], BF16) for _ in range(2)]
        ob_lt = lts(2)
        sub = [pst(k, st, "sub", [128, D], F32) for _ in range(2)]
        sub_lt = lts(2)
        nt = S // 128
        pn = PostNorm(k, st, g_ap, b_ap, x_src, x_dst, nt)
        otv = k.OT.rearrange("c p s -> p c s")
        def load_ob(blk):
            bb = blk % 2
            dma(k, ob[bb][:], otv[:, :, blk * 512:(blk + 1) * 512], writes=[ob_lt[bb]], tile=ob_lt[bb])

        load_ob(0)
        for t in range(nt):
            blk = t // 4
            bb = blk % 2
            if t % 4 == 0 and (blk + 1) * 512 < S:
                load_ob(blk + 1)
            s_ = t % 2
            pn.pre(t)
            for c in range(8):
                for hf in range(2):
                    P.add("pe", lambda e, c=c, hf=hf, bb=bb, s_=s_, t=t: e.matmul(
                        sub[s_][:, hf * 512:(hf + 1) * 512], ob[bb][:, c, (t % 4) * 128:(t % 4 + 1) * 128],
                        wo[:, c, hf * 512:(hf + 1) * 512], start=(c == 0), stop=(c == 7 and bias_ap is None)),
                        reads=[ob_lt[bb], wo_lt[c]], writes=[sub_lt[s_]])
            if bias_ap is not None:
                for hf in range(2):
                    P.add("pe", lambda e, hf=hf, s_=s_: e.matmul(
                        sub[s_][:, hf * 512:(hf + 1) * 512], k.ones[0:1, :], bo[0:1, hf * 512:(hf + 1) * 512],
                        start=False, stop=True), reads=[bo_lt, k.ones_lt], writes=[sub_lt[s_]])
            pn.step(t, sub[s_][:], sub_lt[s_])
        pn.drain()
        P.barrier()


def phase_ffn_up(k, w_up, cp_ap, wd_pre=None):
    P = k.P
    S = k.S
    nblk = S // 512
    with ExitStack() as st:
        cp = sb(k, st, "cp", [128, NJ, 8], F32)
        cp_lt = LT()
        dma(k, cp[:], cp_ap, writes=[cp_lt], tile=cp_lt)
        wu = [sb(k, st, "wu", [128, 8, 256], BF16) for _ in range(3)]
        wu_lt = lts(3)
        H = [[sb(k, st, "H", [128, S + 2], F32) for _ in range(2)] for _ in range(2)]
        H_lt = [[lts(nblk) for _ in range(2)] for _ in range(2)]
        pad_lt = LT()
        for hf in range(2):
            for jb in range(2):
                P.add("pool", lambda e, hf=hf, jb=jb: e.memset(H[hf][jb][:, 0:1], 0.0), writes=[pad_lt])
                P.add("pool", lambda e, hf=hf, jb=jb: e.memset(H[hf][jb][:, S + 1:S + 2], 0.0), writes=[pad_lt])
        NTB = 3
        Tg = [sb(k, st, "Tg", [128, 512], F32) for _ in range(NTB)]
        Tv = [sb(k, st, "Tv", [128, 512], F32) for _ in range(NTB)]
        Tg_lt, Tv_lt = lts(NTB), lts(NTB)
        TT = [Tg, Tv]
        TT_lt = [Tg_lt, Tv_lt]
        NAB = 4
        A = [sb(k, st, "A", [128, 512], BF16) for _ in range(NAB)]
        A_lt = lts(NAB)
        acnt = [0]
        hp = [[pst(k, st, "hp", [128, 512], F32) for _ in range(2)] for _ in range(2)]
        hp_lt = [lts(2), lts(2)]
        wv = w_up.rearrange("(kc p) f -> p kc f", p=128)
        XT = k.XT

        def load_w(j):
            wb = j % 3
            load_cast(k, wu[wb][:, :, 0:128], wv[:, :, j * 128:(j + 1) * 128], wu_lt[wb])
            load_cast(k, wu[wb][:, :, 128:256], wv[:, :, DFF + j * 128:DFF + (j + 1) * 128], wu_lt[wb])

        load_w(0)
        if NJ > 1:
            load_w(1)
        cnt = 0
        for j in range(NJ):
            if j + 2 < NJ:
                load_w(j + 2)
            if wd_pre is not None:
                wd_, wd_lt_, wdv_ = wd_pre
                load_cast(k, wd_[:, j, :], wdv_[:, j, :], wd_lt_[j])
            wb = j % 3
            jb = j % 2

            def conv(blk, j=j, jb=jb):
                o = blk * 512
                tb = blk % NTB
                rl = [H_lt[0][jb][b2] for b2 in (blk - 1, blk, blk + 1) if 0 <= b2 < nblk] + [pad_lt, cp_lt]
                rv = [H_lt[1][jb][b2] for b2 in (blk - 1, blk, blk + 1) if 0 <= b2 < nblk] + [pad_lt, cp_lt]
                Hg, Hv = H[0][jb], H[1][jb]
                tg, tv = Tg[tb], Tv[tb]
                P.add("dve", lambda e: e.scalar_tensor_tensor(out=tg[:], in0=Hg[:, o:o + 512], scalar=cp[:, j, 0:1],
                                                              in1=tg[:], op0=ALU.mult, op1=ALU.add),
                      reads=rl + [Tg_lt[tb]], writes=[Tg_lt[tb]])
                P.add("dve", lambda e: e.scalar_tensor_tensor(out=tg[:], in0=Hg[:, 2 + o:2 + o + 512],
                                                              scalar=cp[:, j, 2:3], in1=tg[:], op0=ALU.mult,
                                                              op1=ALU.add),
                      reads=rl + [Tg_lt[tb]], writes=[Tg_lt[tb]])
                P.add("dve", lambda e: e.scalar_tensor_tensor(out=tv[:], in0=Hv[:, o:o + 512], scalar=cp[:, j, 4:5],
                                                               in1=tv[:], op0=ALU.mult, op1=ALU.add),
                      reads=rv + [Tv_lt[tb]], writes=[Tv_lt[tb]])
                P.add("dve", lambda e: e.scalar_tensor_tensor(out=tv[:], in0=Hv[:, 2 + o:2 + o + 512],
                                                               scalar=cp[:, j, 6:7], in1=tv[:], op0=ALU.mult,
                                                               op1=ALU.add),
                      reads=rv + [Tv_lt[tb]], writes=[Tv_lt[tb]])
                P.add("act", lambda e: e.activation(out=tg[:], in_=tg[:], func=AF.Silu),
                      reads=[Tg_lt[tb]], writes=[Tg_lt[tb]])
                ai = acnt[0] % NAB
                acnt[0] += 1
                P.add("pool", lambda e: e.tensor_tensor(out=A[ai][:], in0=tg[:], in1=tv[:], op=ALU.mult),
                      reads=[Tg_lt[tb], Tv_lt[tb]], writes=[A_lt[ai]])
                dma(k, k.AT[j][:, o:o + 512], A[ai][:], reads=[A_lt[ai]], tile=A_lt[ai])

            for blk in range(nblk):
                for hf in range(2):
                    pb = cnt % 2
                    for kc in range(8):
                        P.add("pe", lambda e, hf=hf, pb=pb, kc=kc, blk=blk, wb=wb: e.matmul(
                            hp[hf][pb][:], wu[wb][:, kc, hf * 128:(hf + 1) * 128],
                            XT[:, kc, blk * 512:(blk + 1) * 512], start=(kc == 0), stop=(kc == 7)),
                            reads=[wu_lt[wb]] + k.XT_lt[blk * 4:blk * 4 + 4], writes=[hp_lt[hf][pb]])
                    P.add("act", lambda e, hf=hf, pb=pb, blk=blk, jb=jb: e.copy(
                        out=H[hf][jb][:, 1 + blk * 512:1 + (blk + 1) * 512], in_=hp[hf][pb][:]),
                        reads=[hp_lt[hf][pb]], writes=[H_lt[hf][jb][blk]])
                    P.add("act", lambda e, hf=hf, pb=pb, blk=blk, j=j: e.activation(
                        out=TT[hf][blk % NTB][:], in_=hp[hf][pb][:], func=AF.Identity,
                        bias=cp[:, j, hf * 4 + 3:hf * 4 + 4], scale=cp[:, j, hf * 4 + 1:hf * 4 + 2]),
                        reads=[hp_lt[hf][pb], cp_lt], writes=[TT_lt[hf][blk % NTB]])
                cnt += 1
                if blk >= 1:
                    conv(blk - 1)
            conv(nblk - 1)
        P.barrier()


def phase_ffn_down(k, w_down, g_ap, b_ap, x_src, x_dst, wd_pre=None):
    P = k.P
    S = k.S
    with ExitStack() as st:
        if wd_pre is None:
            wd = sb(k, st, "wd", [128, NJ, D], BF16)
            wd_lt = lts(NJ)
            wv = w_down.rearrange("(c p) n -> p c n", p=128)
            for c in range(NJ):
                load_cast(k, wd[:, c, :], wv[:, c, :], wd_lt[c])
        else:
            wd, wd_lt, _ = wd_pre
        ab = [sb(k, st, "ab", [128, NJ, 256], BF16) for _ in range(2)]
        ab_lt = lts(2)
        sub = [pst(k, st, "sub", [128, D], F32) for _ in range(2)]
        sub_lt = lts(2)
        nt = S // 128
        pn = PostNorm(k, st, g_ap, b_ap, x_src, x_dst, nt)
        atv = k.AT.rearrange("c p s -> p c s")

        def load_ab(blk):
            bb = blk % 2
            dma(k, ab[bb][:], atv[:, :, blk * 256:(blk + 1) * 256], writes=[ab_lt[bb]], tile=ab_lt[bb])

        load_ab(0)
        for t in range(nt):
            blk = t // 2
            bb = blk % 2
            if t % 2 == 0 and (blk + 1) * 256 < S:
                load_ab(blk + 1)
            s_ = t % 2
            pn.pre(t)
            for c in range(NJ):
                for hf in range(2):
                    P.add("pe", lambda e, c=c, hf=hf, bb=bb, s_=s_, t=t: e.matmul(
                        sub[s_][:, hf * 512:(hf + 1) * 512], ab[bb][:, c, (t % 2) * 128:(t % 2 + 1) * 128],
                        wd[:, c, hf * 512:(hf + 1) * 512], start=(c == 0), stop=(c == NJ - 1)),
                        reads=[ab_lt[bb], wd_lt[c]], writes=[sub_lt[s_]])
            pn.step(t, sub[s_][:], sub_lt[s_])
        pn.drain()
        P.barrier()


def phase_qkv_gqa(k, w_qkv, gain_ap, rope_ap):
    P = k.P
    S = k.S
    with ExitStack() as st:
        wq = sb(k, st, "wq", [128, 8, 1536], BF16)
        wq_lt = lts(8)
        wv = w_qkv.rearrange("(c p) n -> p c n", p=128)
        for c in range(8):
            load_cast(k, wq[:, c, :], wv[:, c, :], wq_lt[c])
        gain = sb(k, st, "gain", [128, 1280], F32)
        gain_lt = LT()
        dma(k, gain[:], gain_ap.partition_broadcast(128), writes=[gain_lt], tile=gain_lt)
        rp = [sb(k, st, "rp", [128, 2, 64], F32) for _ in range(2)]
        rp_lt = lts(2)
        qkv = [pst(k, st, "qkv", [128, 1536], F32) for _ in range(2)]
        qkv_lt = lts(2)
        tq = pst(k, st, "tq", [128, 8, 128], BF16)
        tk = pst(k, st, "tk", [128, 4, 128], BF16)
        tq_lt, tk_lt = LT(), LT()
        sq = [sb(k, st, "sq", [128, 1280], F32) for _ in range(2)]
        qn = [sb(k, st, "qn", [128, 1280], F32) for _ in range(2)]
        t1 = [sb(k, st, "t1", [128, 1280], F32) for _ in range(2)]
        t2 = [sb(k, st, "t2", [128, 1280], F32) for _ in range(2)]
        ss = [sb(k, st, "ss", [128, 20], F32) for _ in range(2)]
        qb = [sb(k, st, "qb", [128, 1024], BF16) for _ in range(2)]
        kd = [sb(k, st, "kd", [128, 4, 128], BF16) for _ in range(2)]
        va = [sb(k, st, "va", [128, 4, 2, 128], BF16) for _ in range(2)]
        l_sq, l_qn, l_t1, l_t2, l_ss, l_qb, l_kd, l_va = (lts(2) for _ in range(8))
        for b in range(2):
            P.add("pool", lambda e, b=b: e.memset(va[b][:], 1.0), writes=[l_va[b]])
        qs = [sb(k, st, "qs", [128, 8, 512], BF16) for _ in range(2)]
        ks = [sb(k, st, "ks", [128, 4, 512], BF16) for _ in range(2)]
        l_qs, l_ks = lts(2), lts(2)
        XT = k.XT
        qtv = k.QT.rearrange("c p s -> p c s")
        ktv = k.KT.rearrange("c p s -> p c s")
        nt = S // 128

        def stage_a(t):
            b = t % 2
            qk_ps = qkv[b][:, 0:1280]
            P.add("act", lambda e: e.activation(out=sq[b][:], in_=qk_ps, func=AF.Square),
                  reads=[qkv_lt[b]], writes=[l_sq[b]])
            P.add("dve", lambda e: e.tensor_reduce(out=ss[b][:], in_=sq[b][:].rearrange("p (h d) -> p h d", d=64),
                                                   axis=AX.X, op=ALU.add), reads=[l_sq[b]], writes=[l_ss[b]])
            P.add("act", lambda e: e.activation(out=ss[b][:], in_=ss[b][:], func=AF.Sqrt, bias=k.eps_rms[:],
                                                scale=1.0 / 64), reads=[l_ss[b], k.eps_lt], writes=[l_ss[b]])
            P.add("dve", lambda e: e.reciprocal(out=ss[b][:], in_=ss[b][:]), reads=[l_ss[b]], writes=[l_ss[b]])
            P.add("dve", lambda e: e.tensor_tensor(
                out=qn[b][:].rearrange("p (h d) -> p h d", d=64), in0=qk_ps.rearrange("p (h d) -> p h d", d=64),
                in1=ss[b][:].unsqueeze(2).broadcast_to([128, 20, 64]), op=ALU.mult),
                reads=[qkv_lt[b], l_ss[b]], writes=[l_qn[b]])
            v_ps = qkv[b][:, 1280:1536].rearrange("p (g d) -> p g d", d=64)
            P.add("act", lambda e: e.copy(out=va[b][:, :, 0, 0:64], in_=v_ps), reads=[qkv_lt[b]], writes=[l_va[b]])
            P.add("act", lambda e: e.copy(out=va[b][:, :, 1, 64:128], in_=v_ps), reads=[qkv_lt[b]], writes=[l_va[b]])
            dma(k, k.VA[t], va[b][:].rearrange("p g v d -> p (g v d)"), reads=[l_va[b]], tile=l_va[b])

        def stage_b(t):
            b = t % 2
            P.add("pool", lambda e: e.tensor_tensor(out=qn[b][:], in0=qn[b][:], in1=gain[:], op=ALU.mult),
                  reads=[l_qn[b], gain_lt], writes=[l_qn[b]])
            P.add("pool", lambda e: e.tensor_tensor(
                out=t1[b][:].rearrange("p (h d) -> p h d", d=64), in0=qn[b][:].rearrange("p (h d) -> p h d", d=64),
                in1=rp[b][:, 0, :].unsqueeze(1).broadcast_to([128, 20, 64]), op=ALU.mult),
                reads=[l_qn[b], rp_lt[b]], writes=[l_t1[b]])
            for a in range(2):
                P.add("dve", lambda e, a=a: e.tensor_tensor(
                    out=t2[b][:].rearrange("p (h g a d) -> p h g a d", g=2, a=2, d=16)[:, :, :, a, :],
                    in0=qn[b][:].rearrange("p (h g a d) -> p h g a d", g=2, a=2, d=16)[:, :, :, 1 - a, :],
                    in1=rp[b][:, 1, :].rearrange("p (g a d) -> p g a d", g=2, a=2, d=16)[:, :, a, :].unsqueeze(1)
                    .broadcast_to([128, 20, 2, 16]),
                    op=ALU.mult), reads=[l_qn[b], rp_lt[b]], writes=[l_t2[b]])
            P.add("dve", lambda e: e.tensor_tensor(out=qb[b][:], in0=t1[b][:, 0:1024], in1=t2[b][:, 0:1024],
                                                   op=ALU.add), reads=[l_t1[b], l_t2[b]], writes=[l_qb[b]])
            for hh in range(2):
                P.add("pool", lambda e, hh=hh: e.tensor_tensor(
                    out=kd[b][:, :, hh * 64:(hh + 1) * 64], in0=t1[b][:, 1024:1280].rearrange("p (g d) -> p g d", d=64),
                    in1=t2[b][:, 1024:1280].rearrange("p (g d) -> p g d", d=64), op=ALU.add),
                    reads=[l_t1[b], l_t2[b]], writes=[l_kd[b]])

        def tail(t):
            b = t % 2
            blk = t // 4
            sbb = blk % 2
            for c in range(8):
                P.add("pe", lambda e, c=c: e.transpose(out=tq[:, c, :], in_=qb[b][:, c * 128:(c + 1) * 128],
                                                       identity=k.ident[:]),
                      reads=[l_qb[b], k.ident_lt], writes=[tq_lt])
            for g in range(4):
                P.add("pe", lambda e, g=g: e.transpose(out=tk[:, g, :], in_=kd[b][:, g, :], identity=k.ident[:]),
                      reads=[l_kd[b], k.ident_lt], writes=[tk_lt])
            o = (t % 4) * 128
            P.add("act", lambda e: e.copy(out=qs[sbb][:, :, o:o + 128], in_=tq[:]),
                  reads=[tq_lt], writes=[l_qs[sbb]])
            P.add("dve", lambda e: e.tensor_copy(out=ks[sbb][:, :, o:o + 128], in_=tk[:]),
                  reads=[tk_lt], writes=[l_ks[sbb]])
            if t % 4 == 3 or t == nt - 1:
                dma(k, qtv[:, :, blk * 512:(blk + 1) * 512], qs[sbb][:], reads=[l_qs[sbb]], tile=l_qs[sbb])
                dma(k, ktv[:, 0:4, blk * 512:(blk + 1) * 512], ks[sbb][:], reads=[l_ks[sbb]], tile=l_ks[sbb])

        for t in range(nt):
            b = t % 2
            dma(k, rp[b][:], rope_ap[t * 128:(t + 1) * 128], writes=[rp_lt[b]], tile=rp_lt[b])
            for kc in range(8):
                for n in range(3):
                    P.add("pe", lambda e, kc=kc, n=n, b=b, t=t: e.matmul(
                        qkv[b][:, n * 512:(n + 1) * 512], XT[:, kc, t * 128:(t + 1) * 128],
                        wq[:, kc, n * 512:(n + 1) * 512], start=(kc == 0), stop=(kc == 7)),
                        reads=[k.XT_lt[t], wq_lt[kc]], writes=[qkv_lt[b]])
            if t >= 2:
                tail(t - 2)
            stage_a(t)
            if t >= 1:
                stage_b(t - 1)
        stage_b(nt - 1)
        if nt >= 2:
            tail(nt - 2)
        tail(nt - 1)
        P.barrier()


def phase_attn_gqa(k):
    P = k.P
    S = k.S
    nkt = S // 128
    nqb = S // 512
    with ExitStack() as st:
        ktd = [sb(k, st, "ktd", [128, S], BF16) for _ in range(2)]
        vag = [sb(k, st, "vag", [128, nkt, 2, 128], BF16) for _ in range(2)]
        qtc = [sb(k, st, "qtc", [128, 2, S], BF16) for _ in range(2)]
        l_ktd, l_vag, l_qtc = lts(2), lts(2), lts(2)
        sps = [[pst(k, st, "sps", [128, 512], F32) for _ in range(2)] for _ in range(2)]
        l_sps = [lts(2), lts(2)]
        acc = [[pst(k, st, "acc", [128, 512], F32) for _ in range(2)] for _ in range(2)]
        l_acc = [lts(2), lts(2)]
        NPB = 3
        pT = [[sb(k, st, "pT", [128, 512], BF16) for _ in range(NPB)] for _ in range(2)]
        l_pT = [lts(NPB), lts(NPB)]
        rc = [sb(k, st, "rc", [128, 512], F32) for _ in range(2)]
        l_rc = lts(2)
        ost = [sb(k, st, "ost", [128, 512], BF16) for _ in range(2)]
        l_ost = lts(2)
        vav = k.VA.rearrange("t p (g x) -> p t g x", g=4)

        def load_g(g):
            gb = g % 2
            dma(k, ktd[gb][:], k.KT[g], writes=[l_ktd[gb]], tile=l_ktd[gb])
            dma(k, vag[gb][:].rearrange("p t v d -> p t (v d)"), vav[:, :, g, :], writes=[l_vag[gb]], tile=l_vag[gb])
            dma(k, qtc[gb][:], k.QT[2 * g:2 * g + 2].rearrange("c p s -> p c s"), writes=[l_qtc[gb]], tile=l_qtc[gb])

        load_g(0)
        it = 0
        for g in range(4):
            gb = g % 2
            if g + 1 < 4:
                load_g(g + 1)
            for pr in range(2):
                c = 2 * g + pr
                for qb_ in range(nqb):
                    ab = it % 2
                    it += 1

                    def qk(kt, gb=gb, pr=pr, qb_=qb_):
                        s_ = kt % 2
                        for h in range(2):
                            P.add("pe", lambda e, h=h, s_=s_, kt=kt: e.matmul(
                                sps[h][s_][:], ktd[gb][h * 64:(h + 1) * 64, kt * 128:(kt + 1) * 128],
                                qtc[gb][h * 64:(h + 1) * 64, pr, qb_ * 512:(qb_ + 1) * 512], start=True, stop=True),
                                reads=[l_ktd[gb], l_qtc[gb]], writes=[l_sps[h][s_]])
                        for h in range(2):
                            pb = kt % NPB
                            P.add("act", lambda e, h=h, s_=s_, pb=pb: e.activation(
                                out=pT[h][pb][:], in_=sps[h][s_][:], func=AF.Exp, scale=0.125),
                                reads=[l_sps[h][s_]], writes=[l_pT[h][pb]])

                    def pv(kt, gb=gb, ab=ab):
                        pb = kt % NPB
                        for h in range(2):
                            P.add("pe", lambda e, h=h, pb=pb, kt=kt: e.matmul(
                                acc[h][ab][:], vag[gb][:, kt, h, :], pT[h][pb][:], start=(kt == 0),
                                stop=(kt == nkt - 1)),
                                reads=[l_vag[gb], l_pT[h][pb]], writes=[l_acc[h][ab]])

                    for kt in range(nkt + 1):
                        if kt < nkt:
                            qk(kt)
                        if kt >= 1:
                            pv(kt - 1)
                    ob_ = ab
                    P.add("dve", lambda e, ab=ab: e.reciprocal(out=rc[ab][0:64, :], in_=acc[0][ab][64:128, :]),
                          reads=[l_acc[0][ab]], writes=[l_rc[ab]])
                    P.add("dve", lambda e, ab=ab: e.reciprocal(out=rc[ab][64:128, :], in_=acc[1][ab][0:64, :]),
                          reads=[l_acc[1][ab]], writes=[l_rc[ab]])
                    P.add("dve", lambda e, ab=ab: e.tensor_tensor(out=ost[ab][0:64, :], in0=acc[0][ab][0:64, :],
                                                                  in1=rc[ab][0:64, :], op=ALU.mult),
                          reads=[l_acc[0][ab], l_rc[ab]], writes=[l_ost[ab]])
                    P.add("dve", lambda e, ab=ab: e.tensor_tensor(out=ost[ab][64:128, :], in0=acc[1][ab][64:128, :],
                                                                  in1=rc[ab][64:128, :], op=ALU.mult),
                          reads=[l_acc[1][ab], l_rc[ab]], writes=[l_ost[ab]])
                    dma(k, k.OT[c][:, qb_ * 512:(qb_ + 1) * 512], ost[ab][:], reads=[l_ost[ab]], tile=l_ost[ab])
        P.barrier()


def phase_qkv_diff(k, w_qkv, rope_ap):
    P = k.P
    S = k.S
    with ExitStack() as st:
        wq = sb(k, st, "wq", [128, 8, 3072], BF16)
        wq_lt = lts(8)
        wv = w_qkv.rearrange("(c p) n -> p c n", p=128)
        for c in range(8):
            for q3 in range(2):
                load_cast(k, wq[:, c, q3 * 1536:(q3 + 1) * 1536], wv[:, c, q3 * 1536:(q3 + 1) * 1536], wq_lt[c])
        rp = [sb(k, st, "rp", [128, 2, 128], F32) for _ in range(2)]
        rp_lt = lts(2)
        ps_ = [pst(k, st, "qps", [128, 1024], F32) for _ in range(2)]
        ps_lt = lts(2)
        tq = [pst(k, st, "tq", [128, 8, 128], BF16) for _ in range(2)]
        tq_lt = lts(2)
        xb = [sb(k, st, "xb", [128, 1024], BF16) for _ in range(2)]
        l_xb = lts(2)
        xf = [sb(k, st, "xf", [128, 1024], F32) for _ in range(2)]
        l_xf = lts(2)
        tmp = [[sb(k, st, "tmp", [128, 16, 8], F32) for _ in range(4)] for _ in range(2)]
        l_tmp = [lts(4), lts(4)]
        stg = [[sb(k, st, "stg", [128, 8, 512], BF16) for _ in range(2)] for _ in range(2)]
        l_stg = [lts(2), lts(2)]
        XT = k.XT
        dstv = [k.QT.rearrange("c p s -> p c s"), k.KT.rearrange("c p s -> p c s")]
        cnt = 0
        pending = []
        for t in range(S // 128):
            rb = t % 2
            blk = t // 4
            sbb = blk % 2
            dma(k, rp[rb][:], rope_ap[t * 128:(t + 1) * 128], writes=[rp_lt[rb]], tile=rp_lt[rb])
            for part in range(3):
                b = cnt % 2
                cnt += 1
                for kc in range(8):
                    for n in range(2):
                        P.add("pe", lambda e, kc=kc, n=n, b=b, t=t, part=part: e.matmul(
                            ps_[b][:, n * 512:(n + 1) * 512], XT[:, kc, t * 128:(t + 1) * 128],
                            wq[:, kc, part * 1024 + n * 512:part * 1024 + (n + 1) * 512], start=(kc == 0),
                            stop=(kc == 7)), reads=[k.XT_lt[t], wq_lt[kc]], writes=[ps_lt[b]])
                if pending:
                    pending.pop(0)()
                if part == 2:
                    P.add("act", lambda e, b=b: e.copy(out=xb[b][:], in_=ps_[b][:]), reads=[ps_lt[b]], writes=[l_xb[b]])
                    dma(k, k.VA[t], xb[b][:], reads=[l_xb[b]], tile=l_xb[b])
                    continue
                P.add("act", lambda e, b=b: e.copy(out=xf[b][:], in_=ps_[b][:]), reads=[ps_lt[b]], writes=[l_xf[b]])
                xfv = xf[b][:].rearrange("p (m d) -> p m d", d=64)
                x1, x2 = xfv[:, :, 0:8], xfv[:, :, 8:16]
                cb = rp[rb][:, 0, :].rearrange("p (m d) -> p m d", d=8)
                sn = rp[rb][:, 1, :].rearrange("p (m d) -> p m d", d=8)
                tm = tmp[b]
                lt_ = l_tmp[b]
                for i_, (xx, tb_) in enumerate(((x1, cb), (x2, sn), (x2, cb), (x1, sn))):
                    P.add("dve", lambda e, xx=xx, tb_=tb_, i_=i_, tm=tm: e.tensor_tensor(
                        out=tm[i_][:], in0=xx, in1=tb_, op=ALU.mult),
                        reads=[l_xf[b], rp_lt[rb]], writes=[lt_[i_]])
                P.add("dve", lambda e, tm=tm, x1=x1: e.tensor_tensor(out=x1, in0=tm[0][:], in1=tm[1][:], op=ALU.subtract),
                      reads=[lt_[0], lt_[1], l_xf[b]], writes=[l_xf[b]])
                P.add("dve", lambda e, tm=tm, x2=x2: e.tensor_tensor(out=x2, in0=tm[2][:], in1=tm[3][:], op=ALU.add),
                      reads=[lt_[2], lt_[3], l_xf[b]], writes=[l_xf[b]])
                P.add("act", lambda e, b=b: e.copy(out=xb[b][:], in_=xf[b][:]), reads=[l_xf[b]], writes=[l_xb[b]])
                def tail(t=t, b=b, part=part, blk=blk, sbb=sbb):
                    for c in range(8):
                        P.add("pe", lambda e, b=b, c=c, part=part: e.transpose(
                            out=tq[part][:, c, :], in_=xb[b][:, c * 128:(c + 1) * 128], identity=k.ident[:]),
                            reads=[l_xb[b], k.ident_lt], writes=[tq_lt[part]])
                    o = (t % 4) * 128
                    eng = "act" if part == 0 else "dve"
                    if eng == "act":
                        P.add("act", lambda e, part=part, sbb=sbb, o=o: e.copy(out=stg[part][sbb][:, :, o:o + 128],
                                                                              in_=tq[part][:]),
                              reads=[tq_lt[part]], writes=[l_stg[part][sbb]])
                    else:
                        P.add("dve", lambda e, part=part, sbb=sbb, o=o: e.tensor_copy(out=stg[part][sbb][:, :, o:o + 128],
                                                                                     in_=tq[part][:]),
                              reads=[tq_lt[part]], writes=[l_stg[part][sbb]])
                    if t % 4 == 3 or t == S // 128 - 1:
                        dma(k, dstv[part][:, :, blk * 512:(blk + 1) * 512], stg[part][sbb][:],
                            reads=[l_stg[part][sbb]], tile=l_stg[part][sbb])
                pending.append(tail)
        while pending:
            pending.pop(0)()
        P.barrier()


def phase_attn_diff(k, lam_ap, subln_ap, lambda_init):
    P = k.P
    S = k.S
    nkt = S // 128
    nqb = S // 512
    with ExitStack() as st:
        lv = sb(k, st, "lv", [128, 4, 64], F32)
        lv_lt = LT()
        dma(k, lv[:].rearrange("p a d -> p (a d)"), lam_ap.partition_broadcast(128), writes=[lv_lt], tile=lv_lt)
        pr_ = sb(k, st, "lpr", [128, 2, 64], F32)
        ls = sb(k, st, "ls", [128, 2], F32)
        neglam = sb(k, st, "neglam", [128, 1], F32)
        l_pr, l_ls, l_nl = LT(), LT(), LT()
        P.add("dve", lambda e: e.tensor_tensor(out=pr_[:], in0=lv[:, 0:4:2, :], in1=lv[:, 1:4:2, :], op=ALU.mult),
              reads=[lv_lt], writes=[l_pr])
        P.add("dve", lambda e: e.tensor_reduce(out=ls[:], in_=pr_[:], axis=AX.X, op=ALU.add), reads=[l_pr],
              writes=[l_ls])
        P.add("act", lambda e: e.activation(out=ls[:], in_=ls[:], func=AF.Exp), reads=[l_ls], writes=[l_ls])
        P.add("dve", lambda e: e.scalar_tensor_tensor(out=neglam[:], in0=ls[:, 1:2], scalar=-lambda_init,
                                                      in1=ls[:, 0:1], op0=ALU.add, op1=ALU.subtract),
              reads=[l_ls], writes=[l_nl])
        sl = sb(k, st, "subln", [128, 1], F32)
        sl_lt = LT()
        dma(k, sl[:], subln_ap.unsqueeze(1), writes=[sl_lt], tile=sl_lt)
        P.add("dve", lambda e: e.tensor_single_scalar(out=sl[:], in_=sl[:], scalar=1.0 - lambda_init, op=ALU.mult),
              reads=[sl_lt], writes=[sl_lt])
        kth = [sb(k, st, "kth", [128, S], BF16) for _ in range(2)]
        vh = [sb(k, st, "vh", [128, nkt, 128], BF16) for _ in range(2)]
        qth = [sb(k, st, "qth", [128, S], BF16) for _ in range(2)]
        l_kth, l_vh, l_qth = lts(2), lts(2), lts(2)
        NSB = 4
        spsb = [pst(k, st, "sps", [128, 512], F32) for _ in range(NSB)]
        l_spsb = lts(NSB)
        scnt = [0]
        acc = [pst(k, st, "acc", [128, 512], F32) for _ in range(2)]
        rs_ = [pst(k, st, "rsum", [128, 512], F32) for _ in range(2)]
        l_acc, l_rs = lts(2), lts(2)
        NPB = 10
        pT = [[sb(k, st, "pT", [128, 512], BF16) for _ in range(NPB)] for _ in range(2)]
        l_pT = [lts(NPB), lts(NPB)]
        pp = [[sb(k, st, "pp", [128, 512], BF16) for _ in range(2)] for _ in range(2)]
        l_pp = [lts(2), lts(2)]
        qq = [[sb(k, st, "qq", [128, 512], BF16) for _ in range(2)] for _ in range(2)]
        l_qq = [lts(2), lts(2)]
        accS = [[sb(k, st, "accS", [128, 512], F32) for _ in range(2)] for _ in range(2)]
        rsS = [[sb(k, st, "rsS", [128, 512], F32) for _ in range(2)] for _ in range(2)]
        l_accS, l_rsS = [lts(2), lts(2)], [lts(2), lts(2)]
        o_ = [sb(k, st, "o", [128, 512], F32) for _ in range(2)]
        sqb = [sb(k, st, "sqb", [128, 512], BF16) for _ in range(2)]
        ssS = [sb(k, st, "ssS", [128, 512], F32) for _ in range(2)]
        l_o, l_sqb, l_ssS = lts(2), lts(2), lts(2)
        ost = [sb(k, st, "ost", [128, 512], BF16) for _ in range(2)]
        l_ost = lts(2)
        vav = k.VA.rearrange("t p (h d) -> p t h d", h=8)

        def load_h(h):
            hb = h % 2
            dma(k, kth[hb][:], k.KT[h], writes=[l_kth[hb]], tile=l_kth[hb])
            dma(k, vh[hb][:], vav[:, :, h, :], writes=[l_vh[hb]], tile=l_vh[hb])
            dma(k, qth[hb][:], k.QT[h], writes=[l_qth[hb]], tile=l_qth[hb])

        load_h(0)
        it = 0
        deferred = []

        def run_deferred(kt):
            while deferred and deferred[0][0] <= kt:
                deferred.pop(0)[1]()

        for h in range(8):
            hb = h % 2
            if h + 1 < 8:
                load_h(h + 1)
            for qb_ in range(nqb):
                par = it % 2
                it += 1

                def qk(kt, hb=hb, qb_=qb_):
                    sb_ = [(scnt[0] + m) % NSB for m in range(2)]
                    scnt[0] += 2
                    for m in range(2):
                        P.add("pe", lambda e, m=m, kt=kt, bk=sb_[m]: e.matmul(
                            spsb[bk][:], kth[hb][m * 64:(m + 1) * 64, kt * 128:(kt + 1) * 128],
                            qth[hb][m * 64:(m + 1) * 64, qb_ * 512:(qb_ + 1) * 512], start=True, stop=True),
                            reads=[l_kth[hb], l_qth[hb]], writes=[l_spsb[sb_[m]]])
                    for m in range(2):
                        pb = kt % NPB
                        P.add("act", lambda e, m=m, pb=pb, bk=sb_[m]: e.activation(
                            out=pT[m][pb][:], in_=spsb[bk][:], func=AF.Exp, scale=0.125),
                            reads=[l_spsb[sb_[m]]], writes=[l_pT[m][pb]])
                    if kt % 2 == 1:
                        pq = (kt // 2) % 2
                        for m in range(2):
                            P.add("dve", lambda e, m=m, pq=pq, kt=kt: e.tensor_tensor(
                                out=pp[m][pq][:], in0=pT[m][(kt - 1) % NPB][:], in1=pT[m][kt % NPB][:], op=ALU.add),
                                reads=[l_pT[m][(kt - 1) % NPB], l_pT[m][kt % NPB]], writes=[l_pp[m][pq]])
                        if kt % 4 == 3:
                            qi = (kt // 4) % 2
                            for m in range(2):
                                P.add("dve", lambda e, m=m, qi=qi: e.tensor_tensor(
                                    out=qq[m][qi][:], in0=pp[m][0][:], in1=pp[m][1][:], op=ALU.add),
                                    reads=[l_pp[m][0], l_pp[m][1]], writes=[l_qq[m][qi]])

                def pv(kt, hb=hb):
                    pb = kt % NPB
                    for m in range(2):
                        P.add("pe", lambda e, m=m, pb=pb, kt=kt: e.matmul(
                            acc[m][:], vh[hb][:, kt, :], pT[m][pb][:], start=(kt == 0), stop=(kt == nkt - 1)),
                            reads=[l_vh[hb], l_pT[m][pb]], writes=[l_acc[m]])
                    if kt % 4 == 3:
                        qi = (kt // 4) % 2

                        def ones_mm(qi=qi, kt=kt):
                            for m in range(2):
                                P.add("pe", lambda e, m=m: e.matmul(
                                    rs_[m][:], k.ones[:], qq[m][qi][:], start=(kt == 3), stop=(kt == nkt - 1)),
                                    reads=[k.ones_lt, l_qq[m][qi]], writes=[l_rs[m]])
                        pend_ones.append((kt + 3, ones_mm))

                pend_ones = []
                for kt in range(nkt + 1):
                    if kt < nkt:
                        qk(kt)
                    if kt >= 1:
                        pv(kt - 1)
                    while pend_ones and pend_ones[0][0] <= kt - 1:
                        pend_ones.pop(0)[1]()
                    run_deferred(kt)
                while pend_ones:
                    pend_ones.pop(0)[1]()
                run_deferred(10 ** 9)
                for m in range(2):
                    P.add("dve", lambda e, m=m, par=par: e.tensor_copy(out=accS[par][m][:], in_=acc[m][:]),
                          reads=[l_acc[m]], writes=[l_accS[par][m]])
                for m in range(2):
                    P.add("dve", lambda e, m=m, par=par: e.tensor_copy(out=rsS[par][m][:], in_=rs_[m][:]),
                          reads=[l_rs[m]], writes=[l_rsS[par][m]])
                def d_rp(m, q, par=par):
                    def f():
                        P.add("dve", lambda e: e.reciprocal(out=rsS[par][m][:, q * 128:(q + 1) * 128],
                                                            in_=rsS[par][m][:, q * 128:(q + 1) * 128]),
                              reads=[l_rsS[par][m]], writes=[l_rsS[par][m]])
                    return f

                def d_m0(par=par):
                    P.add("pool", lambda e: e.tensor_tensor(out=accS[par][0][:], in0=accS[par][0][:],
                                                            in1=rsS[par][0][:], op=ALU.mult),
                          reads=[l_accS[par][0], l_rsS[par][0]], writes=[l_accS[par][0]])

                def d_m1(par=par):
                    P.add("pool", lambda e: e.tensor_tensor(out=accS[par][1][:], in0=accS[par][1][:],
                                                            in1=rsS[par][1][:], op=ALU.mult),
                          reads=[l_accS[par][1], l_rsS[par][1]], writes=[l_accS[par][1]])

                def d_o(par=par):
                    P.add("dve", lambda e: e.scalar_tensor_tensor(out=o_[par][:], in0=accS[par][1][:],
                                                                  scalar=neglam[:], in1=accS[par][0][:],
                                                                  op0=ALU.mult, op1=ALU.add),
                          reads=[l_accS[par][0], l_accS[par][1], l_nl], writes=[l_o[par]])
                    P.add("pool", lambda e: e.tensor_tensor(out=sqb[par][:], in0=o_[par][:], in1=o_[par][:],
                                                            op=ALU.mult), reads=[l_o[par]], writes=[l_sqb[par]])

                def d_ss(par=par):
                    bk = scnt[0] % NSB
                    scnt[0] += 1
                    P.add("pe", lambda e: e.matmul(spsb[bk][:], k.ones[:], sqb[par][:], start=True, stop=True),
                          reads=[k.ones_lt, l_sqb[par]], writes=[l_spsb[bk]])
                    P.add("act", lambda e: e.activation(out=ssS[par][:], in_=spsb[bk][:], func=AF.Ln, bias=k.eps_rms[:],
                                                        scale=1.0 / 128), reads=[l_spsb[bk], k.eps_lt],
                          writes=[l_ssS[par]])
                    P.add("act", lambda e: e.activation(out=ssS[par][:], in_=ssS[par][:], func=AF.Exp, scale=-0.5),
                          reads=[l_ssS[par]], writes=[l_ssS[par]])

                def d_on(par=par):
                    P.add("pool", lambda e: e.tensor_tensor(out=o_[par][:], in0=o_[par][:], in1=ssS[par][:],
                                                            op=ALU.mult), reads=[l_o[par], l_ssS[par]],
                          writes=[l_o[par]])

                def d_out(par=par, h=h, qb_=qb_):
                    P.add("act", lambda e: e.activation(out=ost[par][:], in_=o_[par][:], func=AF.Identity, scale=sl[:]),
                          reads=[l_o[par], sl_lt], writes=[l_ost[par]])
                    dma(k, k.OT[h][:, qb_ * 512:(qb_ + 1) * 512], ost[par][:], reads=[l_ost[par]], tile=l_ost[par])

                sched = [(1 + q, d_rp(0, q)) for q in range(4)] + [(5, d_m0)] + \
                        [(5 + q, d_rp(1, q)) for q in range(4)] + \
                        [(9, d_m1), (11, d_o), (17, d_ss), (22, d_on), (25, d_out)]
                sched.sort(key=lambda x: x[0])
                for trig, fn in sched:
                    deferred.append((min(trig, nkt - 1), fn))
        run_deferred(10 ** 9)
        P.barrier()


def phase_fnet(k, cc_ap, dft_ap):
    P = k.P
    S = k.S
    nt = S // 128
    nkb = S // 512
    with ExitStack() as st:
        ucs = sb(k, st, "ucs", [128, nt, 4, 512], BF16)
        ucs_lt = lts(nt)
        with ExitStack() as st1:
            ccs = sb(k, st1, "ccs", [128, 2, 512], BF16)
            ccs_lt = LT()
            load_cast(k, ccs[:], cc_ap.rearrange("(c p) n -> p c n", p=128), ccs_lt)
            p1 = [pst(k, st1, "p1", [128, 512], F32) for _ in range(4)]
            p1_lt = lts(4)
            XT = k.XT
            cnt = 0
            for t in range(nt):
                for gr in range(4):
                    b = cnt % 4
                    cnt += 1
                    for kc in range(2):
                        P.add("pe", lambda e, b=b, kc=kc, gr=gr, t=t: e.matmul(
                            p1[b][:], XT[:, gr * 2 + kc, t * 128:(t + 1) * 128], ccs[:, kc, :], start=(kc == 0),
                            stop=(kc == 1)), reads=[k.XT_lt[t], ccs_lt], writes=[p1_lt[b]])
                    if gr % 2 == 0:
                        P.add("act", lambda e, b=b, gr=gr, t=t: e.copy(out=ucs[:, t, gr, :], in_=p1[b][:]),
                              reads=[p1_lt[b]], writes=[ucs_lt[t]])
                    else:
                        P.add("dve", lambda e, b=b, gr=gr, t=t: e.tensor_copy(out=ucs[:, t, gr, :], in_=p1[b][:]),
                              reads=[p1_lt[b]], writes=[ucs_lt[t]])
            P.barrier()
        nq = 4 if nt >= 4 else 1
        jq = nt // nq
        dq = [k.XT_flat[:, q * (jq * 1024):(q + 1) * (jq * 1024)].rearrange("p (j c n) -> p j c n", c=2, n=512)
              for q in range(nq)]
        dq_lt = lts(nq)
        p2 = [pst(k, st, "p2", [128, 512], F32) for _ in range(2)]
        p2_lt = lts(2)
        fst = [sb(k, st, "fst", [128, 512], BF16) for _ in range(2)]
        fst_lt = lts(2)

        def load_d(kb, q):
            dma(k, dq[q], dft_ap[kb, :, q * jq:(q + 1) * jq], writes=[dq_lt[q]], tile=dq_lt[q])

        for q in range(nq):
            load_d(0, q)
        cnt = 0
        for kb in range(nkb):
            for fc in range(8):
                gr, hh = fc // 2, fc % 2
                b = cnt % 2
                cnt += 1
                for jt in range(nt):
                    q, jj = jt // jq, jt % jq
                    for cs in range(2):
                        P.add("pe", lambda e, b=b, jt=jt, q=q, jj=jj, cs=cs, gr=gr, hh=hh: e.matmul(
                            p2[b][:], ucs[:, jt, gr, cs * 256 + hh * 128:cs * 256 + (hh + 1) * 128], dq[q][:, jj, cs, :],
                            start=(jt == 0 and cs == 0), stop=(jt == nt - 1 and cs == 1)),
                            reads=[ucs_lt[jt], dq_lt[q]], writes=[p2_lt[b]])
                    if fc == 7 and jj == jq - 1 and kb + 1 < nkb:
                        load_d(kb + 1, q)
                P.add("act", lambda e, b=b: e.activation(out=fst[b][:], in_=p2[b][:], func=AF.Identity, scale=1.0 / math.sqrt(S * 256.0)),
                      reads=[p2_lt[b]], writes=[fst_lt[b]])
                dma(k, k.OT[fc][:, kb * 512:(kb + 1) * 512], fst[b][:], reads=[fst_lt[b]], tile=fst_lt[b])
        P.barrier()


def lambda_init_of(layer_idx):
    return 0.8 - 0.6 * math.exp(-0.3 * layer_idx)


def build(S, layers=(0, 1, 2, 3)):
    nc = bass.Bass("TRN2", target_bir_lowering=False)
    k = K()
    k.nc = nc
    k.S = S
    nt = S // 128

    def din(name, shape, dt=F32):
        return nc.dram_tensor(name, list(shape), dt, kind="ExternalInput").ap()

    x_in = din("x", [S, D])
    ident_in = din("ident", [128, 128])
    W = {}
    for l in layers:
        p = "l%d_" % l
        kind = l % 3
        if kind == 0:
            W[p + "wqkv"] = din(p + "wqkv", [D, 1536])
            W[p + "gain"] = din(p + "gain", [1280])
            W[p + "wo"] = din(p + "wo", [D, D])
        elif kind == 1:
            W[p + "wo"] = din(p + "wo", [D, D])
            W[p + "bo"] = din(p + "bo", [D])
        else:
            W[p + "wqkv"] = din(p + "wqkv", [D, 3072])
            W[p + "lam"] = din(p + "lam", [256])
            W[p + "subln"] = din(p + "subln", [128])
            W[p + "wo"] = din(p + "wo", [D, D])
        for nm, shp in (("ln1_g", [D]), ("ln1_b", [D]), ("wup", [D, 2 * DFF]), ("cp", [128, NJ, 8]),
                        ("wdown", [DFF, D]), ("ln2_g", [D]), ("ln2_b", [D])):
            W[p + nm] = din(p + nm, shp)
    kinds = set(l % 3 for l in layers)
    if 0 in kinds:
        rope_a = din("rope_a", [S, 2, 64])
    if 2 in kinds:
        rope_c = din("rope_c", [S, 2, 128])
    if 1 in kinds:
        cc_in = din("cc", [256, 512])
        dft_in = din("dft", [S // 512, 128, nt, 2, 512], BF16)
    y_out = nc.dram_tensor("y", [S, D], F32, kind="ExternalOutput").ap()

    def scr(name, shape, dt):
        return nc.dram_tensor(name, list(shape), dt, kind="Internal").ap()

    k.X32 = scr("x32s", [S, D], F32)
    k.AT = scr("at", [NJ, 128, S], BF16)
    k.QT = scr("qt", [8, 128, S], BF16)
    k.KT = scr("kt", [8, 128, S], BF16)
    k.VA = scr("va", [nt, 128, 1024], BF16)
    k.OT = scr("ot", [8, 128, S], BF16)

    with ExitStack() as gst:
        P = Prog(nc)
        k.P = P
        P.setup_sems(gst, 80)
        k.XT = gst.enter_context(nc.sbuf_tensor("XT_sb", [128, 8, S], BF16))
        k.XT_flat = k.XT[:].rearrange("p c s -> p (c s)")
        k.XT_lt = lts(nt, "XT")
        k.ident = gst.enter_context(nc.sbuf_tensor("ident_sb", [128, 128], BF16))
        k.ident_lt = LT()
        load_cast(k, k.ident[:], ident_in, k.ident_lt)
        k.ones = gst.enter_context(nc.sbuf_tensor("ones_sb", [128, 128], BF16))
        k.ones_lt = LT()
        P.add("pool", lambda e: e.memset(k.ones[:], 1.0), writes=[k.ones_lt])
        k.eps_ln = gst.enter_context(nc.sbuf_tensor("eps_ln", [128, 1], F32))
        k.eps_rms = gst.enter_context(nc.sbuf_tensor("eps_rms", [128, 1], F32))
        k.eps_lt = LT()
        P.add("pool", lambda e: e.memset(k.eps_ln[:], LN_EPS), writes=[k.eps_lt])
        P.add("pool", lambda e: e.memset(k.eps_rms[:], RMS_EPS), writes=[k.eps_lt])
        k.mhalf = gst.enter_context(nc.sbuf_tensor("mhalf", [128, 1], F32))
        P.add("pool", lambda e: e.memset(k.mhalf[:], -0.5), writes=[k.eps_lt])
        k.need_xt = True
        phase_prep(k, x_in)
        x_cur = x_in
        for li, l in enumerate(layers):
            p = "l%d_" % l
            kind = l % 3
            last = (li == len(layers) - 1)
            with ExitStack() as wst:
                pre = None
                if kind == 0:
                    phase_qkv_gqa(k, W[p + "wqkv"], W[p + "gain"], rope_a)
                    pre = load_wo(k, wst, W[p + "wo"], None)
                    phase_attn_gqa(k)
                    bias = None
                elif kind == 1:
                    phase_fnet(k, cc_in, dft_in)
                    bias = W[p + "bo"]
                else:
                    phase_qkv_diff(k, W[p + "wqkv"], rope_c)
                    pre = load_wo(k, wst, W[p + "wo"], None)
                    phase_attn_diff(k, W[p + "lam"], W[p + "subln"], lambda_init_of(l))
                    bias = None
                phase_proj_postnorm(k, W[p + "wo"], bias, W[p + "ln1_g"], W[p + "ln1_b"], x_cur, k.X32, pre=pre)
            x_cur = k.X32
            with ExitStack() as fst:
                wd = sb(k, fst, "wd", [128, NJ, D], BF16)
                wd_pre = (wd, lts(NJ), W[p + "wdown"].rearrange("(c p) n -> p c n", p=128))
                phase_ffn_up(k, W[p + "wup"], W[p + "cp"], wd_pre)
                k.need_xt = not last
                phase_ffn_down(k, W[p + "wdown"], W[p + "ln2_g"], W[p + "ln2_b"], x_cur, y_out if last else k.X32,
                               wd_pre)
        with nc.Block() as block:
            P.emit(block)
    return nc


def _rope_tables(S):
    def cs(pos, dim, theta):
        inv = theta ** (-np.arange(0, dim, 2, dtype=np.float32) / np.float32(dim))
        ang = pos.astype(np.float32)[:, None] * inv[None, :].astype(np.float32)
        return np.cos(ang).astype(np.float32), np.sin(ang).astype(np.float32)
    rows = S // 64
    t_row = np.repeat(np.arange(rows), 64)
    t_col = np.tile(np.arange(64), rows)
    cr, sr = cs(t_row, 32, 10000.0)
    cc, sc = cs(t_col, 32, 10000.0)
    C = np.concatenate([cr, cr, cc, cc], axis=1)
    Sg = np.concatenate([-sr, sr, -sc, sc], axis=1)
    rope_a = np.stack([C, Sg], axis=1).astype(np.float32)
    c8, s8 = cs(np.arange(S), 16, 500000.0)
    rope_c = np.stack([np.tile(c8, (1, 16)), np.tile(s8, (1, 16))], axis=1).astype(np.float32)
    return rope_a, rope_c


def _dft_tables(S):
    n = np.arange(256)
    ang = 2.0 * np.pi * ((n[:, None] * n[None, :]) % 256) / 256.0
    cc = np.concatenate([np.cos(ang), -np.sin(ang)], axis=1).astype(np.float32)
    j = np.arange(S, dtype=np.int64)
    m = (j[:, None] * j[None, :]) % S
    ang = (2.0 * np.pi / S) * m
    Cs = np.cos(ang).astype(ml_dtypes.bfloat16)
    Ss = np.sin(ang).astype(ml_dtypes.bfloat16)
    nt = S // 128
    d = np.stack([Cs, Ss], axis=0).reshape(2, nt, 128, S // 512, 512)
    d = np.ascontiguousarray(d.transpose(3, 2, 1, 0, 4))
    return cc, d


def make_in_maps(inputs, S, layers=(0, 1, 2, 3), n_cores=8):
    f = lambda a: np.ascontiguousarray(np.asarray(a, dtype=np.float32))
    shared = {"ident": np.eye(128, dtype=np.float32)}
    kinds = set(l % 3 for l in layers)
    rope_a, rope_c = _rope_tables(S)
    if 0 in kinds:
        shared["rope_a"] = rope_a
    if 2 in kinds:
        shared["rope_c"] = rope_c
    if 1 in kinds:
        cc, d = _dft_tables(S)
        shared["cc"] = cc
        shared["dft"] = d
    for l in layers:
        p = "l%d_" % l
        kind = l % 3
        if kind == 0:
            shared[p + "wqkv"] = f(inputs[p + "a_wqkv"])
            qn, kn = f(inputs[p + "a_qnorm"]), f(inputs[p + "a_knorm"])
            shared[p + "gain"] = np.concatenate([np.tile(qn, 16), np.tile(kn, 4)]).astype(np.float32)
            shared[p + "wo"] = f(inputs[p + "a_wo"])
        elif kind == 1:
            shared[p + "wo"] = f(inputs[p + "f_wo"])
            shared[p + "bo"] = f(inputs[p + "f_bo"])
        else:
            shared[p + "wqkv"] = f(inputs[p + "c_wqkv"])
            shared[p + "lam"] = np.concatenate([f(inputs[p + "c_lq1"]), f(inputs[p + "c_lk1"]),
                                                f(inputs[p + "c_lq2"]), f(inputs[p + "c_lk2"])]).astype(np.float32)
            shared[p + "subln"] = f(inputs[p + "c_subln"])
            shared[p + "wo"] = f(inputs[p + "c_wo"])
        cw, cb = f(inputs[p + "ffn_conv_w"]), f(inputs[p + "ffn_conv_b"])
        cp = np.zeros((128, NJ, 8), np.float32)
        for hf in range(2):
            sl = slice(hf * DFF, (hf + 1) * DFF)
            cp[:, :, hf * 4 + 0] = cw[0, sl].reshape(NJ, 128).T
            cp[:, :, hf * 4 + 1] = cw[1, sl].reshape(NJ, 128).T
            cp[:, :, hf * 4 + 2] = cw[2, sl].reshape(NJ, 128).T
            cp[:, :, hf * 4 + 3] = cb[sl].reshape(NJ, 128).T
        shared[p + "cp"] = cp
        shared[p + "wup"] = f(inputs[p + "ffn_wup"])
        shared[p + "wdown"] = f(inputs[p + "ffn_wdown"])
        for nm in ("ln1_g", "ln1_b", "ln2_g", "ln2_b"):
            shared[p + nm] = f(inputs[p + nm])
    x = f(inputs["x"])
    maps = []
    for c in range(n_cores):
        m = dict(shared)
        m["x"] = np.ascontiguousarray(x[c])
        maps.append(m)
    return maps


_CACHE = {}


def kernel(**inputs):
    x = np.asarray(inputs["x"])
    B, S, _ = x.shape
    key = (S,)
    if key not in _CACHE:
        _CACHE[key] = build(S)
    nc = _CACHE[key]
    maps = make_in_maps(inputs, S, n_cores=B)
    res = run_bass_kernel_spmd(nc, maps, core_ids=list(range(B)))
    out = np.stack([np.asarray(r["y"], dtype=np.float32) for r in res.results], axis=0)
    return out
```
